# Optimizing a Trainium2 kernel written in Bass

```python
import math
import jax, jax.numpy as jnp
from jax import lax
import numpy as np

D_MODEL = 1024
BATCH = 16
SEQ = 4096
DEPTH = 4

GRID_W = 64
HEAD_DIM = 64
RWKV_WIDTH = D_MODEL // 2
RWKV_HEADS = RWKV_WIDTH // HEAD_DIM
NA_WIDTH = D_MODEL // 2
NA_HEADS = NA_WIDTH // HEAD_DIM
NA_KH = 8
NA_KW = 16
DECAY_LORA = max(32, int(round(1.8 * RWKV_WIDTH ** 0.5 / 32)) * 32)
ICLR_LORA = max(32, int(round(1.8 * RWKV_WIDTH ** 0.5 / 32)) * 32)
GATE_LORA = max(32, int(round(0.6 * RWKV_WIDTH ** 0.8 / 32)) * 32)
D_FF = -(-8 * D_MODEL // (3 * 256)) * 256
DEEPNORM_ALPHA = (2.0 * DEPTH) ** 0.25
DEEPNORM_BETA = (8.0 * DEPTH) ** -0.25
LN_EPS = 1e-5
GN_EPS = 64e-5
RWKV_COLS = 3 * RWKV_WIDTH + 2 * DECAY_LORA + 2 * ICLR_LORA + GATE_LORA
NA_COLS = 3 * NA_WIDTH
GATE_COLS = 2 * D_MODEL
IN_COLS = RWKV_COLS + NA_COLS + GATE_COLS

kernel_name = "hybrid_rwkv7_natten_deepnorm_encoder"


def _split(t, sizes):
    offs = np.cumsum(np.array(sizes))[:-1].tolist()
    return jnp.split(t, offs, axis=-1)


def _layer_norm(x, g, b):
    x32 = x.astype(jnp.float32)
    mean = jnp.mean(x32, axis=-1, keepdims=True)
    var = jnp.mean(jnp.square(x32 - mean), axis=-1, keepdims=True)
    y = (x32 - mean) * lax.rsqrt(var + LN_EPS) * g + b
    return y.astype(x.dtype)


def _token_shift_centred(u, mu):
    zeros = jnp.zeros_like(u[:, :1])
    u_prev = jnp.concatenate([zeros, u[:, :-1]], axis=1)
    u_next = jnp.concatenate([u[:, 1:], zeros], axis=1)
    return u + mu[0] * (u_prev - u) + mu[1] * (u_next - u)


def _rwkv7_scan(r, w, k, v, a, b, reverse):
    bsz, _, h, n = r.shape
    xs = (jnp.moveaxis(r.astype(jnp.float32), 1, 0),
          jnp.moveaxis(w.astype(jnp.float32), 1, 0),
          jnp.moveaxis(k.astype(jnp.float32), 1, 0),
          jnp.moveaxis(v.astype(jnp.float32), 1, 0),
          jnp.moveaxis(a.astype(jnp.float32), 1, 0),
          jnp.moveaxis(b.astype(jnp.float32), 1, 0))

    def step(state, inp):
        r_t, w_t, k_t, v_t, a_t, b_t = inp
        sa = jnp.einsum('bhvk,bhk->bhv', state, a_t)
        state = (state * w_t[:, :, None, :]
                 + sa[..., None] * b_t[:, :, None, :]
                 + v_t[..., None] * k_t[:, :, None, :])
        out = jnp.einsum('bhvk,bhk->bhv', state, r_t)
        return state, out

    s0 = jnp.zeros((bsz, h, n, n), jnp.float32)
    _, out = lax.scan(step, s0, xs, reverse=reverse)
    return jnp.moveaxis(out, 0, 1)


def _rwkv7_mix(u, mu, decay_w0, decay_up, iclr_a0, iclr_up, gate_up,
               k_k, k_a, r_k, gn_g, gn_b):
    bsz, s, _ = u.shape
    h, n = RWKV_HEADS, HEAD_DIM
    u = _token_shift_centred(u, mu)
    r, k, v, wdn, adn, gdn = _split(
        u, [RWKV_WIDTH, RWKV_WIDTH, RWKV_WIDTH, 2 * DECAY_LORA, 2 * ICLR_LORA, GATE_LORA])
    wdn = wdn.reshape(bsz, s, 2, DECAY_LORA)
    adn = adn.reshape(bsz, s, 2, ICLR_LORA)
    z = decay_w0 + jnp.einsum('bsdr,drc->bsdc', jnp.tanh(wdn), decay_up)
    w_log = -jax.nn.softplus(-z.astype(jnp.float32)) - 0.5
    decay = jnp.exp(-jnp.exp(w_log))
    iclr = jax.nn.sigmoid(iclr_a0 + jnp.einsum('bsdr,drc->bsdc', adn, iclr_up))
    g = jax.nn.sigmoid(gdn) @ gate_up
    kk = (k * k_k).reshape(bsz, s, h, n).astype(jnp.float32)
    kk = kk / jnp.maximum(jnp.linalg.norm(kk, axis=-1, keepdims=True), 1e-12)
    kd = k[:, :, None, :] * (1.0 + (iclr - 1.0) * k_a)
    rh = r.reshape(bsz, s, h, n)
    vh = v.reshape(bsz, s, h, n)
    kd_h = kd.reshape(bsz, s, 2, h, n)
    iclr_h = iclr.reshape(bsz, s, 2, h, n)
    dec_h = decay.reshape(bsz, s, 2, h, n)
    o_fwd = _rwkv7_scan(rh, dec_h[:, :, 0], kd_h[:, :, 0], vh, -kk,
                        kk * iclr_h[:, :, 0], reverse=False)
    o_bwd = _rwkv7_scan(rh, dec_h[:, :, 1], kd_h[:, :, 1], vh, -kk,
                        kk * iclr_h[:, :, 1], reverse=True)
    o = o_fwd + o_bwd
    mean = jnp.mean(o, axis=-1, keepdims=True)
    var = jnp.mean(jnp.square(o - mean), axis=-1, keepdims=True)
    o = ((o - mean) * lax.rsqrt(var + GN_EPS)).reshape(bsz, s, RWKV_WIDTH) * gn_g + gn_b
    kd_sum = (kd_h[:, :, 0] + kd_h[:, :, 1]).astype(jnp.float32)
    bonus = jnp.sum(rh.astype(jnp.float32) * kd_sum * r_k, axis=-1, keepdims=True) * vh
    o = (o + bonus.reshape(bsz, s, RWKV_WIDTH)) * g
    return o.astype(u.dtype)


def _neighbourhood_attn(q, k, v, rpb):
    bsz, s, _ = q.shape
    rows = s // GRID_W
    kh = min(NA_KH, rows)
    kw = NA_KW

    def to_grid(t):
        return t.reshape(bsz, rows, GRID_W, NA_HEADS, HEAD_DIM).transpose(0, 3, 1, 2, 4)

    qg, kg, vg = to_grid(q), to_grid(k), to_grid(v)
    q_rows = jnp.moveaxis(qg, 2, 0) * (HEAD_DIM ** -0.5)
    col = jnp.arange(GRID_W)
    col_start = jnp.clip(col - kw // 2, 0, GRID_W - kw)
    col_idx = col_start[:, None] + jnp.arange(kw)
    dc_idx = col_idx - col[:, None] + (NA_KW - 1)

    def one_row(args):
        q_i, i = args
        rs = jnp.clip(i - kh // 2, 0, rows - kh)
        k_band = lax.dynamic_slice_in_dim(kg, rs, kh, axis=2)[:, :, :, col_idx, :]
        v_band = lax.dynamic_slice_in_dim(vg, rs, kh, axis=2)[:, :, :, col_idx, :]
        dr_idx = rs + jnp.arange(kh) - i + (NA_KH - 1)
        bias = rpb[:, dr_idx[None, :, None], dc_idx[:, None, :]]
        sc = jnp.einsum('bhqn,bhrqcn->bhqrc', q_i, k_band).astype(jnp.float32) + bias
        p = jax.nn.softmax(sc.reshape(bsz, NA_HEADS, GRID_W, kh * kw), axis=-1)
        p = p.reshape(sc.shape).astype(v.dtype)
        return jnp.einsum('bhqrc,bhrqcn->bhqn', p, v_band)

    out = lax.map(one_row, (q_rows, jnp.arange(rows)))
    return out.transpose(1, 0, 3, 2, 4).reshape(bsz, s, NA_WIDTH)


def setup_inputs(seed: int = 0) -> dict:
    key = jax.random.key(seed)
    ks = jax.random.split(key, 32)
    L, D, C = DEPTH, D_MODEL, RWKV_WIDTH
    nrm = jax.random.normal
    f32 = jnp.float32
    return {
        "x": nrm(ks[0], (BATCH, SEQ, D), f32),
        "ln_in_g": 1.0 + 0.02 * nrm(ks[1], (D,), f32),
        "ln_in_b": 0.02 * nrm(ks[2], (D,), f32),
        "w_in": nrm(ks[3], (L, D, IN_COLS), f32) * D ** -0.5,
        "shift_mu": jax.random.uniform(ks[4], (L, 2, RWKV_COLS), f32, 0.0, 0.5),
        "decay_w0": 0.5 * nrm(ks[5], (L, 2, C), f32),
        "decay_up": nrm(ks[6], (L, 2, DECAY_LORA, C), f32) * DECAY_LORA ** -0.5,
        "iclr_a0": 0.5 * nrm(ks[7], (L, 2, C), f32),
        "iclr_up": nrm(ks[8], (L, 2, ICLR_LORA, C), f32) * ICLR_LORA ** -0.5,
        "gate_up": nrm(ks[9], (L, GATE_LORA, C), f32) * GATE_LORA ** -0.5,
        "k_k": 0.85 + 0.05 * nrm(ks[10], (L, C), f32),
        "k_a": 1.0 + 0.05 * nrm(ks[11], (L, C), f32),
        "r_k": 0.1 * nrm(ks[12], (L, RWKV_HEADS, HEAD_DIM), f32),
        "gn_g": 1.0 + 0.02 * nrm(ks[13], (L, C), f32),
        "gn_b": 0.02 * nrm(ks[14], (L, C), f32),
        "na_rpb": 0.1 * nrm(ks[15], (L, NA_HEADS, 2 * NA_KH - 1, 2 * NA_KW - 1), f32),
        "w_branch_rwkv": nrm(ks[16], (L, C, D), f32) * C ** -0.5,
        "w_branch_na": nrm(ks[17], (L, NA_WIDTH, D), f32) * NA_WIDTH ** -0.5,
        "w_out": nrm(ks[18], (L, D, D), f32) * (D ** -0.5 * DEEPNORM_BETA),
        "ln1_g": 1.0 + 0.02 * nrm(ks[19], (L, D), f32),
        "ln1_b": 0.02 * nrm(ks[20], (L, D), f32),
        "w_ffn_in": nrm(ks[21], (L, D, 2 * D_FF), f32) * D ** -0.5,
        "w_ffn_out": nrm(ks[22], (L, D_FF, D), f32) * (D_FF ** -0.5 * DEEPNORM_BETA),
        "ln2_g": 1.0 + 0.02 * nrm(ks[23], (L, D), f32),
        "ln2_b": 0.02 * nrm(ks[24], (L, D), f32),
    }


def reference(x, ln_in_g, ln_in_b, w_in, shift_mu, decay_w0, decay_up, iclr_a0,
              iclr_up, gate_up, k_k, k_a, r_k, gn_g, gn_b, na_rpb, w_branch_rwkv,
              w_branch_na, w_out, ln1_g, ln1_b, w_ffn_in, w_ffn_out, ln2_g, ln2_b):
    x = _layer_norm(x, ln_in_g, ln_in_b)
    for l in range(DEPTH):
        proj = x @ w_in[l]
        u_rwkv, u_na, gates = _split(proj, [RWKV_COLS, NA_COLS, GATE_COLS])
        y_a = _rwkv7_mix(u_rwkv, shift_mu[l], decay_w0[l], decay_up[l], iclr_a0[l],
                         iclr_up[l], gate_up[l], k_k[l], k_a[l], r_k[l], gn_g[l], gn_b[l])
        q, k, v = _split(u_na, [NA_WIDTH, NA_WIDTH, NA_WIDTH])
        y_b = _neighbourhood_attn(q, k, v, na_rpb[l])
        g_a, g_b = _split(gates, [D_MODEL, D_MODEL])
        merged = (jax.nn.sigmoid(g_a) * (y_a @ w_branch_rwkv[l])
                  + jax.nn.sigmoid(g_b) * (y_b @ w_branch_na[l]))
        x = _layer_norm(DEEPNORM_ALPHA * x + merged @ w_out[l], ln1_g[l], ln1_b[l])
        h_gate, h_up = _split(x @ w_ffn_in[l], [D_FF, D_FF])
        ffn = (jax.nn.silu(h_gate) * h_up) @ w_ffn_out[l]
        x = _layer_norm(DEEPNORM_ALPHA * x + ffn, ln2_g[l], ln2_b[l])
    return x
```

```python
import math
import os
import numpy as np
from contextlib import ExitStack
import concourse.bass as bass
import concourse.mybir as mybir
from concourse.bass_utils import run_bass_kernel_spmd

F32 = mybir.dt.float32
BF16 = mybir.dt.bfloat16
AF = mybir.ActivationFunctionType
ALU = mybir.AluOpType
AX = mybir.AxisListType

D = 1024
CW = 512
HD = 64
DFF = 2816
RWC = 1760
INC = 5344
NAQ0 = 1760
GT0 = 1760 + 1536
GRID_W = 64
DEPTH_FULL = 4
ALPHA = (2.0 * DEPTH_FULL) ** 0.25
LN_EPS = 1e-5
GN_EPS = 64e-5
KAPPA = math.exp(-0.5)
NEG = -30000.0
CH = 128
NE = 31
BG_MODE = 0


class T:
    __slots__ = ("w", "r")

    def __init__(self):
        self.w = None
        self.r = {}


class Prog:
    NDMA = 20

    def __init__(self, nc, es, same_engine_sync=True):
        self.nc = nc
        self.same = same_engine_sync
        self.eng = {"pe": nc.tensor, "act": nc.scalar, "dve": nc.vector,
                    "pool": nc.gpsimd, "sp": nc.sync}
        self.sem = {k: es.enter_context(nc.semaphore("s_" + k)) for k in self.eng}
        self.cnt = {k: 0 for k in self.eng}
        self.waited = {k: {} for k in self.eng}
        self.dsem, self.dcnt, self.drr = {}, {}, {}
        for q in ("sp", "pool"):
            self.drr[q] = 0
            for i in range(self.NDMA):
                k = ("d", q, i)
                self.dsem[k] = es.enter_context(nc.semaphore("d_%s%d" % (q, i)))
                self.dcnt[k] = 0
        self.n_ins = 0
        self.last_rg = None

    def semof(self, k):
        return self.dsem[k] if isinstance(k, tuple) else self.sem[k]

    def _wait(self, X, k, v):
        if self.waited[X].get(k, 0) < v:
            self.eng[X].wait_ge(self.semof(k), v)
            self.waited[X][k] = v
            self.n_ins += 1

    def _deps(self, X, reads, writes):
        deps = {}
        for t in reads:
            if t.w is not None and t.w[1] > deps.get(t.w[0], 0):
                deps[t.w[0]] = t.w[1]
        for t in writes:
            if t.w is not None and t.w[1] > deps.get(t.w[0], 0):
                deps[t.w[0]] = t.w[1]
            for k, v in t.r.items():
                if v > deps.get(k, 0):
                    deps[k] = v
        need = []
        for k, v in deps.items():
            if k == X and (X == "pe" or not self.same):
                continue
            if self.waited[X].get(k, 0) < v:
                need.append((k, v))
        for k, v in need[1:]:
            self._wait(X, k, v)
        return need[0] if need else None

    def op(self, X, fn, reads=(), writes=(), rg=None):
        if X == "pe":
            if rg != self.last_rg and self.cnt["pe"] > 0:
                self._wait("pe", "pe", self.cnt["pe"])
            self.last_rg = rg
        emb = self._deps(X, reads, writes)
        ins = fn(self.eng[X])
        if emb is not None:
            ins._wait_ge(self.semof(emb[0]), emb[1])
            self.waited[X][emb[0]] = emb[1]
        self.cnt[X] += 1
        c = self.cnt[X]
        ins.then_inc(self.sem[X], 1)
        self.n_ins += 1
        for t in reads:
            if t.r.get(X, 0) < c:
                t.r[X] = c
        for t in writes:
            t.w = (X, c)
            t.r = {}
        return ins

    def dma(self, Q, pairs, reads=(), writes=()):
        emb = self._deps(Q, reads, writes)
        if emb is not None:
            self._wait(Q, emb[0], emb[1])
        i = self.drr[Q]
        self.drr[Q] = (i + 1) % self.NDMA
        k = ("d", Q, i)
        self._wait(Q, k, self.dcnt[k])
        for (o, a) in pairs:
            self.eng[Q].dma_start(out=o, in_=a).then_inc(self.dsem[k], 16)
            self.dcnt[k] += 16
            self.n_ins += 1
        c = self.dcnt[k]
        for t in reads:
            t.r[k] = c
        for t in writes:
            t.w = (k, c)
            t.r = {}

    def barrier(self):
        for X in self.eng:
            for k in self.eng:
                if k != X and self.cnt[k] > 0:
                    self._wait(X, k, self.cnt[k])
            for k, v in self.dcnt.items():
                if v > 0:
                    self._wait(X, k, v)


def mm(out, lhsT, rhs, start=True, stop=True):
    return lambda e: e.matmul(out, lhsT=lhsT, rhs=rhs, start=start, stop=stop)


class Builder:
    def __init__(self, L, NS, S, dbg=False):
        self.L, self.NS, self.S, self.dbg = L, NS, S, dbg
        self.R = S // GRID_W
        self.NT = S // 128
        self.NG = S // 512
        nc = self.nc = bass.Bass("TRN2", target_bir_lowering=False)
        self.es = ExitStack()
        dt = lambda n, s, d, kind="ExternalInput": nc.dram_tensor(n, s, d, kind=kind).ap()
        sk = "ExternalOutput" if dbg else "Internal"
        NTOK = NS * S
        self.x_in = dt("x", [NTOK, D], F32)
        self.w_in = dt("w_in", [L, D, INC], F32)
        self.w_a = dt("w_a", [L, CW, D], F32)
        self.w_b = dt("w_b", [L, CW, D], F32)
        self.w_out = dt("w_out", [L, D, D], F32)
        self.w_f1 = dt("w_f1", [L, D, 2 * DFF], F32)
        self.w_f2 = dt("w_f2", [L, DFF, D], F32)
        self.pvec = dt("pvec", [L, 128, NPV], F32)
        self.lora = dt("lora", [L, 128, 4, CW], F32)
        self.gup = dt("gup", [L, 96, CW], F32)
        self.lnp = dt("lnp", [2 + 4 * L, 128, D], F32)
        self.btab = dt("btab", [L, 8, 2, 64, NE + 1, 64], F32)
        self.cst = dt("cst", [128, NCST], F32)
        self.nvar = len(na_plan(self.R)[1])
        self.rmask = dt("rmask", [max(self.nvar, 1), 128, 8, 512], F32)
        self.y = dt("y", [NTOK, D], F32, kind="ExternalOutput")
        self.xres = dt("xres", [NTOK, D], F32, kind=sk)
        self.us = dt("us", [NS, 14 * 128, S], F32, kind=sk)
        self.qk = dt("qk", [NS, 1024, S], BF16, kind=sk)
        self.vna = dt("vna", [NS, S, CW], BF16, kind=sk)
        self.gt = dt("gt", [NS, 2048, S], BF16, kind=sk)
        self.ya = dt("ya", [NS, CW, S], BF16, kind=sk)
        self.yb = dt("yb", [NS, CW, S], BF16, kind=sk)
        self.Tdr = {}
        self.uid = 0

    def tdr(self, *key):
        if key not in self.Tdr:
            self.Tdr[key] = T()
        return self.Tdr[key]

    def sb(self, es, name, shape, dt):
        self.uid += 1
        return es.enter_context(self.nc.sbuf_tensor("%s_%d" % (name, self.uid), shape, dt))

    def ps(self, es, name, shape, dt=F32):
        self.uid += 1
        return es.enter_context(self.nc.psum_tensor("%s_%d" % (name, self.uid), shape, dt))

    def layer_norm(self, P, xt, Tx, g, b, Tgb, st, mv, Tst, out, Tout):
        P.op("dve", lambda e: e.bn_stats(out=st[:, 0, :], in_=xt[:, 0:512]), reads=[Tx], writes=[Tst])
        P.op("dve", lambda e: e.bn_stats(out=st[:, 1, :], in_=xt[:, 512:1024]), reads=[Tx], writes=[Tst])
        P.op("dve", lambda e: e.bn_aggr(out=mv[:, 0:2], in_=st[:].rearrange("p a b -> p (a b)")), reads=[Tst], writes=[Tst])
        P.op("act", lambda e: e.activation(out=mv[:, 2:3], in_=mv[:, 1:2], func=AF.Sqrt, bias=self.c_eps_ln, scale=1.0), reads=[Tst], writes=[Tst])
        P.op("dve", lambda e: e.reciprocal(out=mv[:, 3:4], in_=mv[:, 2:3]), reads=[Tst], writes=[Tst])
        P.op("dve", lambda e: e.tensor_scalar(out=xt[:], in0=xt[:], scalar1=mv[:, 0:1], scalar2=mv[:, 3:4], op0=ALU.subtract, op1=ALU.mult), reads=[Tx, Tst], writes=[Tx])
        P.op("pool", lambda e: e.tensor_tensor(out=xt[:], in0=xt[:], in1=g, op=ALU.mult), reads=[Tx, Tgb], writes=[Tx])
        P.op("pool", lambda e: e.tensor_tensor(out=out, in0=xt[:], in1=b, op=ALU.add), reads=[Tx, Tgb], writes=[Tout])

    def build(self, stages="APRNMF"):
        nc = self.nc
        with self.es as es:
            P = self.P = Prog(nc, es)
            self.cstt = self.sb(es, "cstt", [128, NCST], F32)
            self.Tc = T()
            P.dma("sp", [(self.cstt[:], self.cst[:, :])], writes=[self.Tc])
            c = self.cstt
            self.identf = c[:, C_ID:C_ID + 128]
            self.c_eps_ln = c[:, C_EPS:C_EPS + 1]
            self.c_eps_gn = c[:, C_EPS + 1:C_EPS + 2]
            self.c_eps_kk = c[:, C_EPS + 2:C_EPS + 3]
            self.cb = self.sb(es, "cstb", [128, NCB], BF16)
            self.Tcb = T()
            P.op("dve", lambda e: e.tensor_copy(out=self.cb[:, CB_ID:CB_ID + 128], in_=c[:, C_ID:C_ID + 128]), reads=[self.Tc], writes=[self.Tcb])
            P.op("dve", lambda e: e.tensor_copy(out=self.cb[:, CB_ID2:CB_ID2 + 128], in_=c[:, C_ID:C_ID + 128]), reads=[self.Tc], writes=[self.Tcb])
            P.op("dve", lambda e: e.tensor_copy(out=self.cb[:, CB_ID2 + 128:CB_ID2 + 256], in_=c[:, C_ID:C_ID + 128]), reads=[self.Tc], writes=[self.Tcb])
            P.op("dve", lambda e: e.tensor_copy(out=self.cb[:, CB_ONES:CB_ONES + 64], in_=c[:, C_ONES:C_ONES + 64]), reads=[self.Tc], writes=[self.Tcb])
            for l in range(self.L):
                for s in range(self.NS):
                    if "A" in stages:
                        self.stage_AP(l, s, do_p=("P" in stages))
                    if "R" in stages:
                        self.stage_R(l, s)
                    if "N" in stages:
                        self.stage_N(l, s)
                if "M" in stages:
                    self.stage_M(l)
                if "F" in stages:
                    self.stage_F(l, last=(l == self.L - 1))
            P.barrier()
        return nc

    def stage_AP(self, l, s, do_p=True):
        P, nc, S = self.P, self.nc, self.S
        P.barrier()
        with ExitStack() as es:
            xT = self.sb(es, "xT", [128, 8, S], BF16)
            TxT = [T() for _ in range(self.NT)]
            NXT = 4
            xt = [self.sb(es, "xt%d" % i, [128, D], F32) for i in range(NXT)]
            Txt = [T() for _ in range(NXT)]
            xo = [self.sb(es, "xo%d" % i, [128, D], F32) for i in range(2)] if l == 0 else None
            Txo = [T(), T()]
            st = self.sb(es, "st", [128, 2, 6], F32)
            mv = self.sb(es, "mv", [128, 4], F32)
            Tst = T()
            pp = [self.ps(es, "pp%d" % i, [128, 512]) for i in range(8)]
            Tpp = [T() for _ in range(8)]
            if l == 0:
                lng = self.sb(es, "lng", [128, D], F32)
                lnb = self.sb(es, "lnb", [128, D], F32)
                Tgb = T()
                P.dma("sp", [(lng[:], self.lnp[0]), (lnb[:], self.lnp[1])], writes=[Tgb])
            src = self.x_in if l == 0 else self.xres
            for t in range(self.NT):
                r0 = s * S + t * 128
                b = t % NXT
                b2 = t % 2
                P.dma("sp", [(xt[b][:], src[r0:r0 + 128, :])], reads=[self.tdr("xres", s, t)], writes=[Txt[b]])
                if l == 0:
                    self.layer_norm(P, xt[b], Txt[b], lng[:], lnb[:], Tgb, st, mv, Tst, xo[b2][:], Txo[b2])
                    P.dma("sp", [(self.xres[r0:r0 + 128, :], xo[b2][:])], reads=[Txo[b2]], writes=[self.tdr("xres", s, t)])
                    xs, Txs = xo[b2], Txo[b2]
                else:
                    xs, Txs = xt[b], Txt[b]
                for half in range(2):
                    pb = pp[(2 * t + half) % 8]
                    Tpb = Tpp[(2 * t + half) % 8]
                    for q in range(4):
                        kc = half * 4 + q
                        P.op("pe", lambda e: e.transpose(pb[:, q * 128:(q + 1) * 128], xs[:, kc * 128:(kc + 1) * 128], self.identf),
                             reads=[Txs, self.Tc], writes=[Tpb])
                    eng = "act" if half == 0 else "dve"
                    o_ap = xT[:, half * 4:half * 4 + 4, t * 128:(t + 1) * 128]
                    i_ap = pb[:].rearrange("p (a b) -> p a b", a=4)
                    if eng == "act":
                        P.op("act", lambda e: e.activation(out=o_ap, in_=i_ap, func=AF.Copy), reads=[Tpb], writes=[TxT[t]])
                    else:
                        P.op("dve", lambda e: e.tensor_copy(out=o_ap, in_=i_ap), reads=[Tpb], writes=[TxT[t]])
            if not do_p:
                if self.dbg:
                    self.dbg_xT = (xT, TxT)
                return
            wb = [self.sb(es, "wb%d" % i, [128, 8, 128], BF16) for i in range(2)]
            Twb = [T(), T()]
            wv = self.sb(es, "wv", [128, 8, 512], BF16)
            Twv = T()
            ub = [self.sb(es, "ub%d" % i, [128, S + 2], F32) for i in range(2)]
            Tub = [T(), T()]
            ut1 = self.sb(es, "ut", [128, S], F32)
            Tut1 = T()
            ut = [ut1, ut1]
            Tut = [Tut1, Tut1]
            ob = [self.sb(es, "ob%d" % i, [128, S], BF16) for i in range(2)]
            Tob = [T(), T()]
            vb = [self.sb(es, "vb%d" % i, [128, 512], BF16) for i in range(2)]
            Tvb = [T(), T()]
            pv = self.sb(es, "pvA", [128, NPV], F32)
            Tpv = T()
            P.dma("sp", [(pv[:], self.pvec[l])], writes=[Tpv])
            c0 = self.sb(es, "c0", [128, 14], F32)
            P.op("dve", lambda e: e.tensor_tensor(out=c0[:], in0=pv[:, PV_MU0:PV_MU0 + 14], in1=pv[:, PV_MU1:PV_MU1 + 14], op=ALU.add), reads=[Tpv], writes=[Tpv])
            P.op("dve", lambda e: e.tensor_scalar(out=c0[:], in0=c0[:], scalar1=-1.0, scalar2=1.0, op0=ALU.mult, op1=ALU.add), reads=[Tpv], writes=[Tpv])
            for i in range(2):
                P.op("pool", lambda e: e.memset(ub[i][:, 0:1], 0.0), writes=[Tub[i]])
                P.op("pool", lambda e: e.memset(ub[i][:, S + 1:S + 2], 0.0), writes=[Tub[i]])
            w_l = self.w_in[l].rearrange("(kc p) c -> p kc c", p=128)
            blocks = [("u", j * 128, min(128, RWC - j * 128), j) for j in range(14)]
            blocks += [("q", NAQ0 + j * 128, 128, j) for j in range(4)]
            blocks += [("k", NAQ0 + CW + j * 128, 128, j) for j in range(4)]
            blocks += [("g", GT0 + j * 128, 128, j) for j in range(16)]
            pi = 0
            for bi, (kind, c0c, ncol, j) in enumerate(blocks):
                w = wb[bi % 2]
                Tw = Twb[bi % 2]
                P.dma("pool", [(w[:, :, 0:ncol], w_l[:, :, c0c:c0c + ncol])], writes=[Tw])
                if kind == "u":
                    dst, Tdst = ub[j % 2], Tub[j % 2]
                else:
                    dst, Tdst = ob[bi % 2], Tob[bi % 2]
                for g in range(self.NG):
                    pb, Tpb = pp[pi % 8], Tpp[pi % 8]
                    pi += 1
                    for kc in range(8):
                        P.op("pe", mm(pb[0:ncol, :], w[:, kc, 0:ncol], xT[:, kc, g * 512:(g + 1) * 512], kc == 0, kc == 7),
                             reads=[Tw] + TxT[g * 4:g * 4 + 4], writes=[Tpb])
                    if kind == "u":
                        P.op("act", lambda e: e.activation(out=dst[0:ncol, 1 + g * 512:1 + (g + 1) * 512], in_=pb[0:ncol, :], func=AF.Copy), reads=[Tpb], writes=[Tdst])
                    elif kind == "q":
                        P.op("act", lambda e: e.activation(out=dst[:, g * 512:(g + 1) * 512], in_=pb[:, :], func=AF.Copy, scale=0.125), reads=[Tpb], writes=[Tdst])
                    elif kind == "k":
                        P.op("dve", lambda e: e.tensor_copy(out=dst[:, g * 512:(g + 1) * 512], in_=pb[:, :]), reads=[Tpb], writes=[Tdst])
                    else:
                        P.op("act", lambda e: e.activation(out=dst[:, g * 512:(g + 1) * 512], in_=pb[:, :], func=AF.Sigmoid), reads=[Tpb], writes=[Tdst])
                if kind == "u":
                    u_, Tu_ = ut[j % 2], Tut[j % 2]
                    P.op("act", lambda e: e.activation(out=u_[0:ncol, :], in_=dst[0:ncol, 1:S + 1], func=AF.Copy, scale=c0[0:ncol, j:j + 1]), reads=[Tdst, Tpv], writes=[Tu_])
                    P.op("dve", lambda e: e.scalar_tensor_tensor(out=u_[0:ncol, :], in0=dst[0:ncol, 0:S], scalar=pv[0:ncol, PV_MU0 + j:PV_MU0 + j + 1], in1=u_[0:ncol, :], op0=ALU.mult, op1=ALU.add), reads=[Tdst, Tpv, Tu_], writes=[Tu_])
                    P.op("dve", lambda e: e.scalar_tensor_tensor(out=u_[0:ncol, :], in0=dst[0:ncol, 2:S + 2], scalar=pv[0:ncol, PV_MU1 + j:PV_MU1 + j + 1], in1=u_[0:ncol, :], op0=ALU.mult, op1=ALU.add), reads=[Tdst, Tpv, Tu_], writes=[Tu_])
                    P.dma("sp", [(self.us[s, j * 128:j * 128 + ncol, :], u_[0:ncol, :])], reads=[Tu_], writes=[self.tdr("us", s)])
                elif kind == "q":
                    P.dma("sp", [(self.qk[s, j * 128:(j + 1) * 128, :], dst[:])], reads=[Tdst], writes=[self.tdr("qk", s)])
                elif kind == "k":
                    P.dma("sp", [(self.qk[s, CW + j * 128:CW + (j + 1) * 128, :], dst[:])], reads=[Tdst], writes=[self.tdr("qk", s)])
                else:
                    P.dma("sp", [(self.gt[s, j * 128:(j + 1) * 128, :], dst[:])], reads=[Tdst], writes=[self.tdr("gt", s)])
            vc0 = NAQ0 + 2 * CW
            P.dma("pool", [(wv[:], w_l[:, :, vc0:vc0 + CW])], writes=[Twv])
            for t in range(self.NT):
                pb, Tpb = pp[pi % 8], Tpp[pi % 8]
                pi += 1
                for kc in range(8):
                    P.op("pe", mm(pb[:, :], xT[:, kc, t * 128:(t + 1) * 128], wv[:, kc, :], kc == 0, kc == 7), reads=[Twv, TxT[t]], writes=[Tpb])
                v_, Tv_ = vb[t % 2], Tvb[t % 2]
                P.op("dve", lambda e: e.tensor_copy(out=v_[:], in_=pb[:, :]), reads=[Tpb], writes=[Tv_])
                P.dma("sp", [(self.vna[s, t * 128:(t + 1) * 128, :], v_[:])], reads=[Tv_], writes=[self.tdr("vna", s)])


    def stage_R(self, l, s):
        P, nc, S = self.P, self.nc, self.S
        SEG = 512
        NSEG = S // SEG
        NCS = SEG // CH
        P.barrier()
        c = self.cstt
        ones_f = c[:, C_ONES:C_ONES + 128]
        with ExitStack() as es:
            pv = self.sb(es, "pvR", [128, NPV], F32)
            Tpv = T()
            P.dma("sp", [(pv[:], self.pvec[l])], writes=[Tpv])
            omka = self.sb(es, "omka", [128, 8], F32)
            P.op("dve", lambda e: e.tensor_scalar(out=omka[:, 0:4], in0=pv[:, PV_KA:PV_KA + 4], scalar1=-1.0, scalar2=1.0, op0=ALU.mult, op1=ALU.add), reads=[Tpv], writes=[Tpv])
            P.op("dve", lambda e: e.tensor_scalar(out=omka[:, 4:8], in0=pv[:, PV_KA:PV_KA + 4], scalar1=-2.0, scalar2=2.0, op0=ALU.mult, op1=ALU.add), reads=[Tpv], writes=[Tpv])
            lorab = self.sb(es, "lorab", [128, 4, CW], BF16)
            gupb = self.sb(es, "gupb", [128, CW], BF16)
            Tlw = T()
            P.op("pool", lambda e: e.memset(gupb[:], 0.0), writes=[Tlw])
            P.dma("pool", [(lorab[:], self.lora[l]), (gupb[0:96, :], self.gup[l])], writes=[Tlw])
            twa = self.sb(es, "twa", [128, S], BF16)
            sg = self.sb(es, "sg", [128, S], BF16)
            Ttw = T()
            P.op("pool", lambda e: e.memset(sg[:], 0.0), writes=[Ttw])
            smask = self.sb(es, "smask", [128, SEG], BF16)
            Tsm = T()
            P.op("pool", lambda e: e.memset(smask[:], 1.0), writes=[Tsm])
            P.op("pool", lambda e: e.memset(smask[:].rearrange("p (c t) -> p c t", t=CH)[:, :, 0:1], 0.0), writes=[Tsm])
            Tmk = T()
            SL, SU, IL, IU = (c[:, C_MASK + i * 128:C_MASK + (i + 1) * 128] for i in range(4))
            Fn_ = ["r", "k", "v", "z", "i", "c", "t", "e", "p", "m", "kk", "sq", "kd", "x"]
            F = {n: self.sb(es, "F" + n, [128, SEG], F32) for n in Fn_}
            TF = {n: T() for n in Fn_}
            bank = [[self.ps(es, "pb%d_%d" % (d, i), [128, 512]) for i in range(4)] for d in range(2)]
            Tbank = [[T() for i in range(4)] for d in range(2)]
            for seg in range(NSEG):
                t0 = seg * SEG
                P.dma("sp", [(F["x"][:], self.us[s, 1536:1664, t0:t0 + SEG])], reads=[self.tdr("us", s)], writes=[TF["x"]])
                P.op("act", lambda e: e.activation(out=twa[0:64, t0:t0 + SEG], in_=F["x"][0:64, :], func=AF.Tanh), reads=[TF["x"]], writes=[Ttw])
                P.op("dve", lambda e: e.tensor_copy(out=twa[64:128, t0:t0 + SEG], in_=F["x"][64:128, :]), reads=[TF["x"]], writes=[Ttw])
                P.dma("sp", [(F["x"][0:96, :], self.us[s, 1664:1760, t0:t0 + SEG])], reads=[self.tdr("us", s)], writes=[TF["x"]])
                P.op("act", lambda e: e.activation(out=sg[0:96, t0:t0 + SEG], in_=F["x"][0:96, :], func=AF.Sigmoid), reads=[TF["x"]], writes=[Ttw])
            H = [[{n: self.sb(es, "H%s%d" % (n, d), [128, SEG], BF16) for n in ("k", "b", "v")} for d in range(2)] for hb in range(2)]
            TH = [[T(), T()] for hb in range(2)]
            for hb in range(2):
                for d in range(2):
                    H[hb][d]["z"] = self.sb(es, "Hz%d" % d, [128, NCS, 4, 128], BF16)
                    P.op("pool", lambda e: e.memset(H[hb][d]["z"][:], 0.0), writes=[TH[hb][d]])
            TMt = [[{n: self.sb(es, "M%s%d" % (n, d), [128, NCS, 128], BF16) for n in ("k", "b", "v")} for d in range(2)] for hb in range(2)]
            TTM = [[T(), T()] for hb in range(2)]
            of = [self.sb(es, "of%d" % d, [128, S], F32) for d in range(2)]
            Tof = [[T() for _ in range(NSEG)] for d in range(2)]
            Dd = [self.sb(es, "Dd%d" % d, [128, S // CH + 1], F32) for d in range(2)]
            TD = [T(), T()]
            Sf = [self.sb(es, "Sf%d" % d, [128, 64], F32) for d in range(2)]
            Sb = [self.sb(es, "Sb%d" % d, [128, 64], BF16) for d in range(2)]
            S0D = [self.sb(es, "S0D%d" % d, [128, 64], F32) for d in range(2)]
            TS = [T(), T()]
            TSb = [T(), T()]
            TS0 = [T(), T()]
            NSLOT = 3
            Pp = [[self.sb(es, "Pp%d_%d" % (k, i), [128, 2, 128], BF16) for i in range(2)] for k in range(NSLOT)]
            TPp = [[T(), T()] for _ in range(NSLOT)]
            W = [[self.sb(es, "W%d_%d" % (k, i), [128, 2, 2, 128], BF16) for i in range(2)] for k in range(NSLOT)]
            TWp = [[T(), T()] for _ in range(NSLOT)]
            TWx = [[T(), T()] for _ in range(NSLOT)]
            S1 = [self.sb(es, "S1_%d" % d, [128, NCS, 4, 128], BF16) for d in range(2)]
            S2 = [self.sb(es, "S2_%d" % d, [128, NCS, 4, 128], BF16) for d in range(2)]
            XTs = [self.sb(es, "XTs%d" % d, [128, NCS, 2, 128], BF16) for d in range(2)]
            TST = [[T() for _ in range(NCS)] for d in range(2)]
            mkT = [self.sb(es, "mkT%d" % d, [128, 512], F32) for d in range(2)]
            mkA = [self.sb(es, "mkA%d" % d, [128, 256], F32) for d in range(2)]
            for d, (ms, mi, ma) in enumerate(((SU, IU, SL), (SL, IL, SU))):
                for i, m in enumerate((ms, ms, mi, mi)):
                    P.op("pool", lambda e: e.tensor_copy(out=mkT[d][:, i * 128:(i + 1) * 128], in_=m), reads=[self.Tc], writes=[Tmk])
                for i in range(2):
                    P.op("pool", lambda e: e.tensor_copy(out=mkA[d][:, i * 128:(i + 1) * 128], in_=ma), reads=[self.Tc], writes=[Tmk])
            RHb = [self.sb(es, "RHb%d" % d, [128, 128], BF16) for d in range(2)]
            TRH = [T(), T()]
            Ub = [self.sb(es, "Ub%d" % d, [128, 128], BF16) for d in range(2)]
            TU = [T(), T()]
            yH = self.sb(es, "yH", [128, SEG], BF16)
            TyH = T()
            id2 = self.cb[:, CB_ID2:CB_ID2 + 256].rearrange("p (h t) -> p h t", h=2)
            idb = self.cb[:, CB_ID:CB_ID + 128]

            flat = [bank[0][0], bank[0][1], bank[0][2], bank[0][3], bank[1][0], bank[1][1], bank[1][2], bank[1][3]]
            Tflat = [Tbank[0][0], Tbank[0][1], Tbank[0][2], Tbank[0][3], Tbank[1][0], Tbank[1][1], Tbank[1][2], Tbank[1][3]]
            bgb, Tbgb = (flat[6], flat[7]), (Tflat[6], Tflat[7])

            def prep(hp, d, seg, hb):
                t0 = seg * SEG
                hc = slice(hp * 128, (hp + 1) * 128)
                pz, Tpz = bgb[0], Tbgb[0]
                P.dma("sp", [(F["r"][:], self.us[s, hp * 128:(hp + 1) * 128, t0:t0 + SEG]),
                             (F["k"][:], self.us[s, 512 + hp * 128:512 + (hp + 1) * 128, t0:t0 + SEG]),
                             (F["v"][:], self.us[s, 1024 + hp * 128:1024 + (hp + 1) * 128, t0:t0 + SEG])],
                      reads=[self.tdr("us", s)], writes=[TF["r"], TF["k"], TF["v"]])
                for g in range(SEG // 512):
                    gs = slice(g * 512, (g + 1) * 512)
                    ts_ = slice(t0 + g * 512, t0 + (g + 1) * 512)
                    P.op("pe", mm(pz[:, :], lorab[:, d, hc], twa[:, ts_]), reads=[Tlw, Ttw], writes=[Tpz])
                    P.op("act", lambda e: e.activation(out=F["z"][:, gs], in_=pz[:, :], func=AF.Sigmoid, bias=pv[:, PV_W0 + 4 * d + hp:PV_W0 + 4 * d + hp + 1], scale=1.0), reads=[Tpz, Tpv], writes=[TF["z"]])
                    P.op("pe", mm(pz[:, :], lorab[:, 2 + d, hc], twa[:, ts_]), reads=[Tlw, Ttw], writes=[Tpz])
                    P.op("act", lambda e: e.activation(out=F["i"][:, gs], in_=pz[:, :], func=AF.Sigmoid, bias=pv[:, PV_A0 + 4 * d + hp:PV_A0 + 4 * d + hp + 1], scale=1.0), reads=[Tpz, Tpv], writes=[TF["i"]])
                yield
                P.op("dve", lambda e: e.tensor_tensor_scan(out=F["c"][:], data0=smask[:], data1=F["z"][:], initial=0.0, op0=ALU.mult, op1=ALU.add), reads=[Tsm, TF["z"]], writes=[TF["c"]])
                if d == 0:
                    cum, Tcum = F["c"], TF["c"]
                else:
                    P.op("dve", lambda e: e.tensor_tensor(out=F["t"][:], in0=F["z"][:], in1=F["c"][:], op=ALU.subtract), reads=[TF["z"], TF["c"]], writes=[TF["t"]])
                    c3 = F["c"][:].rearrange("p (c t) -> p c t", t=CH)
                    P.op("dve", lambda e: e.tensor_tensor(out=F["t"][:].rearrange("p (c t) -> p c t", t=CH), in0=F["t"][:].rearrange("p (c t) -> p c t", t=CH),
                                                          in1=c3[:, :, CH - 1:CH].to_broadcast([128, NCS, CH]), op=ALU.add), reads=[TF["t"], TF["c"]], writes=[TF["t"]])
                    cum, Tcum = F["t"], TF["t"]
                P.op("dve", lambda e: e.tensor_tensor(out=F["e"][:], in0=cum[:], in1=F["z"][:], op=ALU.subtract), reads=[Tcum, TF["z"]], writes=[TF["e"]])
                P.op("act", lambda e: e.activation(out=F["e"][:], in_=F["e"][:], func=AF.Exp, scale=-KAPPA), reads=[TF["e"]], writes=[TF["e"]])
                P.op("act", lambda e: e.activation(out=F["p"][:], in_=cum[:], func=AF.Exp, scale=-KAPPA), reads=[Tcum], writes=[TF["p"]])
                P.op("act", lambda e: e.activation(out=F["m"][:], in_=cum[:], func=AF.Exp, scale=KAPPA), reads=[Tcum], writes=[TF["m"]])
                p3 = F["p"][:].rearrange("p (c t) -> p c t", t=CH)
                edge = CH - 1 if d == 0 else 0
                P.op("pool", lambda e: e.tensor_copy(out=Dd[d][:, seg * NCS:(seg + 1) * NCS], in_=p3[:, :, edge]), reads=[TF["p"]], writes=[TD[d]])
                yield
                P.op("act", lambda e: e.activation(out=F["kk"][:], in_=F["k"][:], func=AF.Copy, scale=pv[:, PV_KK + hp:PV_KK + hp + 1]), reads=[TF["k"], Tpv], writes=[TF["kk"]])
                P.op("act", lambda e: e.activation(out=F["sq"][:], in_=F["kk"][:], func=AF.Square), reads=[TF["kk"]], writes=[TF["sq"]])
                for g in range(SEG // 512):
                    gs = slice(g * 512, (g + 1) * 512)
                    P.op("pe", mm(pz[:, :], ones_f, F["sq"][:, gs]), reads=[self.Tc, TF["sq"]], writes=[Tpz])
                    P.op("act", lambda e: e.activation(out=F["sq"][:, gs], in_=pz[:, :], func=AF.Ln, bias=self.c_eps_kk, scale=1.0), reads=[Tpz, self.Tc], writes=[TF["sq"]])
                P.op("act", lambda e: e.activation(out=F["sq"][:], in_=F["sq"][:], func=AF.Exp, scale=-0.5), reads=[TF["sq"]], writes=[TF["sq"]])
                P.op("dve", lambda e: e.tensor_tensor(out=F["kk"][:], in0=F["kk"][:], in1=F["sq"][:], op=ALU.mult), reads=[TF["kk"], TF["sq"]], writes=[TF["kk"]])
                yield
                P.op("dve", lambda e: e.tensor_scalar(out=F["kd"][:], in0=F["i"][:], scalar1=pv[:, PV_KA + hp:PV_KA + hp + 1], scalar2=omka[:, hp:hp + 1], op0=ALU.mult, op1=ALU.add), reads=[TF["i"], Tpv], writes=[TF["kd"]])
                P.op("dve", lambda e: e.tensor_tensor(out=F["kd"][:], in0=F["kd"][:], in1=F["k"][:], op=ALU.mult), reads=[TF["kd"], TF["k"]], writes=[TF["kd"]])
                yield
                Hd = H[hb][d]
                for h in range(2):
                    ph = slice(h * 64, (h + 1) * 64)
                    P.op("dve", lambda e: e.tensor_tensor(out=Hd["z"][ph, :, 2 + h, :], in0=F["r"][ph, :].rearrange("p (c t) -> p c t", t=CH), in1=F["p"][ph, :].rearrange("p (c t) -> p c t", t=CH), op=ALU.mult), reads=[TF["r"], TF["p"]], writes=[TH[hb][d]])
                P.op("dve", lambda e: e.tensor_tensor(out=Hd["k"][:], in0=F["kd"][:], in1=F["m"][:], op=ALU.mult), reads=[TF["kd"], TF["m"]], writes=[TH[hb][d]])
                for h in range(2):
                    ph = slice(h * 64, (h + 1) * 64)
                    P.op("dve", lambda e: e.scalar_tensor_tensor(out=Hd["z"][ph, :, h, :], in0=F["kk"][ph, :].rearrange("p (c t) -> p c t", t=CH), scalar=-1.0, in1=F["e"][ph, :].rearrange("p (c t) -> p c t", t=CH), op0=ALU.mult, op1=ALU.mult), reads=[TF["kk"], TF["e"]], writes=[TH[hb][d]])
                P.op("pool", lambda e: e.tensor_tensor(out=F["x"][:], in0=F["kk"][:], in1=F["i"][:], op=ALU.mult), reads=[TF["kk"], TF["i"]], writes=[TF["x"]])
                P.op("dve", lambda e: e.tensor_tensor(out=Hd["b"][:], in0=F["x"][:], in1=F["m"][:], op=ALU.mult), reads=[TF["x"], TF["m"]], writes=[TH[hb][d]])
                P.op("act", lambda e: e.activation(out=Hd["v"][:], in_=F["v"][:], func=AF.Copy), reads=[TF["v"]], writes=[TH[hb][d]])
                yield
                bi = 0
                for n in ("k", "b", "v"):
                    for half in range(NCS // 4):
                        pb, Tpb = bgb[bi % 2], Tbgb[bi % 2]
                        bi += 1
                        for q in range(4):
                            cc = half * 4 + q
                            P.op("pe", mm(pb[:, q * 128:(q + 1) * 128], Hd[n][:, cc * 128:(cc + 1) * 128], idb), reads=[TH[hb][d], self.Tcb], writes=[Tpb])
                        o_ap = TMt[hb][d][n][:, half * 4:half * 4 + 4, :]
                        i_ap = pb[:].rearrange("p (a b) -> p a b", a=4)
                        if bi % 2 == 0:
                            P.op("act", lambda e: e.activation(out=o_ap, in_=i_ap, func=AF.Copy), reads=[Tpb], writes=[TTM[hb][d]])
                        else:
                            P.op("dve", lambda e: e.tensor_copy(out=o_ap, in_=i_ap), reads=[Tpb], writes=[TTM[hb][d]])
                        yield

            def s0d_update(d, gc_next):
                P.op("act", lambda e: e.activation(out=S0D[d][:], in_=Sf[d][:], func=AF.Copy, scale=Dd[d][:, gc_next:gc_next + 1]), reads=[TS[d], TD[d]], writes=[TS0[d]])

            def ph1(d, cl, k, hb):
                bk0, bk1 = flat[2 * k], flat[2 * k + 1]
                Tb0, Tb1 = Tflat[2 * k], Tflat[2 * k + 1]
                Hd = H[hb][d]
                cs = slice(cl * 128, (cl + 1) * 128)
                Zc = Hd["z"][:, cl]
                bcs, kcs = Hd["b"][:, cs], Hd["k"][:, cs]
                Tst = TST[d][cl]
                f2 = lambda ap: ap.rearrange("p a t -> p (a t)")
                P.op("pe", mm(bk0[:, :], bcs, f2(Zc)), reads=[TH[hb][d]], writes=[Tb0])
                P.op("pe", mm(bk1[:, :], kcs, f2(Zc)), reads=[TH[hb][d]], writes=[Tb1])
                P.op("dve", lambda e: e.tensor_tensor(out=f2(S1[d][:, cl]), in0=bk0[:, :], in1=mkT[d][:], op=ALU.mult), reads=[Tb0, Tmk], writes=[Tst])
                P.op("dve", lambda e: e.tensor_tensor(out=f2(S2[d][:, cl]), in0=bk1[:, :], in1=mkT[d][:], op=ALU.mult), reads=[Tb1, Tmk], writes=[Tst])
                for h in range(2):
                    P.op("pe", mm(bk0[:, h * 128:(h + 1) * 128], Zc[:, h, :], bcs), reads=[TH[hb][d]], writes=[Tb0])
                P.op("dve", lambda e: e.tensor_tensor(out=f2(Pp[k][0][:]), in0=bk0[:, 0:256], in1=mkA[d][:], op=ALU.mult), reads=[Tb0, Tmk], writes=[TPp[k][0]])
                P.op("pool", lambda e: e.tensor_tensor(out=W[k][0][:, :, 1, :], in0=S1[d][:, cl, 0:2, :], in1=id2, op=ALU.add), reads=[Tst, self.Tcb], writes=[TWx[k][0]])
                yield
                b0v = bk0[:].rearrange("p (h x) -> p h x", h=2)
                for h in range(2):
                    P.op("pe", mm(bk1[:, h * 128:(h + 1) * 128], S1[d][:, cl, h, :], Pp[k][0][:, h, :]), reads=[Tst, TPp[k][0]], writes=[Tb1])
                    P.op("pe", mm(bk0[:, h * 256:h * 256 + 128], Pp[k][0][:, h, :], S1[d][:, cl, h, :]), reads=[Tst, TPp[k][0]], writes=[Tb0])
                P.op("act", lambda e: e.activation(out=W[k][0][:, :, 0, :], in_=b0v[:, :, 0:128], func=AF.Copy), reads=[Tb0], writes=[TWp[k][0]])
                P.op("dve", lambda e: e.tensor_copy(out=f2(Pp[k][1][:]), in_=bk1[:, 0:256]), reads=[Tb1], writes=[TPp[k][1]])
                yield
                wi, pi = 0, 1
                for j in range(1, 6):
                    Wc, Wn, Pc, Pnx = W[k][wi], W[k][1 - wi], Pp[k][pi], Pp[k][1 - pi]
                    for h in range(2):
                        if os.environ.get("K_VAR", "") == "1":
                            P.op("pe", mm(bk0[:, h * 256:h * 256 + 128], Pc[:, h, :], Wc[:, h, 0, :]), reads=[TPp[k][pi], TWp[k][wi], TWx[k][wi]], writes=[Tb0])
                            P.op("pe", mm(bk0[:, h * 256 + 128:h * 256 + 256], Pc[:, h, :], Wc[:, h, 1, :]), reads=[TPp[k][pi], TWp[k][wi], TWx[k][wi]], writes=[Tb0])
                        else:
                            P.op("pe", mm(bk0[:, h * 256:(h + 1) * 256], Pc[:, h, :], Wc[:, h].rearrange("p a t -> p (a t)")), reads=[TPp[k][pi], TWp[k][wi], TWx[k][wi]], writes=[Tb0])
                        P.op("pe", mm(bk1[:, h * 128:(h + 1) * 128], Wc[:, h, 0, :], Pc[:, h, :]), reads=[TPp[k][pi], TWp[k][wi]], writes=[Tb1])
                    P.op("dve", lambda e: e.tensor_copy(out=Wn[:, :, 0, :], in_=b0v[:, :, 0:128]), reads=[Tb0], writes=[TWp[k][1 - wi]])
                    P.op("dve", lambda e: e.tensor_tensor(out=Wn[:, :, 1, :], in0=b0v[:, :, 128:256], in1=Wc[:, :, 1, :], op=ALU.add), reads=[Tb0, TWx[k][wi]] + ([TWp[k][1 - wi]] if os.environ.get("K_VAR", "") == "2" else []), writes=[TWx[k][1 - wi]])
                    P.op("act", lambda e: e.activation(out=f2(Pnx[:]), in_=bk1[:, 0:256], func=AF.Copy), reads=[Tb1], writes=[TPp[k][1 - pi]])
                    wi, pi = 1 - wi, 1 - pi
                    yield
                Wc, Pc = W[k][wi], Pp[k][pi]
                for h in range(2):
                    P.op("pe", mm(bk0[:, h * 256 + 128:h * 256 + 256], Pc[:, h, :], Wc[:, h, 1, :]), reads=[TPp[k][pi], TWx[k][wi]], writes=[Tb0])
                P.op("dve", lambda e: e.tensor_tensor(out=XTs[d][:, cl], in0=b0v[:, :, 128:256], in1=Wc[:, :, 1, :], op=ALU.add), reads=[Tb0, TWx[k][wi]], writes=[Tst])
                yield

            def ph2(d, seg, cl, hb):
                gc = seg * NCS + cl
                A0, A1 = flat[3 * d], flat[3 * d + 1]
                TA0, TA1 = Tflat[3 * d], Tflat[3 * d + 1]
                Hd = H[hb][d]
                cs = slice(cl * 128, (cl + 1) * 128)
                hs = (slice(0, 64), slice(64, 128))
                kT_, bT_, vT_ = TMt[hb][d]["k"], TMt[hb][d]["b"], TMt[hb][d]["v"]
                Tst = TST[d][cl]
                A2, TA2 = flat[3 * d + 2], Tflat[3 * d + 2]
                s0d_update(d, gc)
                for h in range(2):
                    P.op("pe", mm(A0[:, h * 64:(h + 1) * 64], Hd["z"][:, cl, h, :], Sb[d][:, :], True, False), reads=[TH[hb][d], TSb[d]], writes=[TA0])
                    P.op("pe", mm(A0[:, h * 64:(h + 1) * 64], S2[d][:, cl, h, :], vT_[:, cl, hs[h]], False, True), reads=[Tst, TTM[hb][d]], writes=[TA0])
                P.op("act", lambda e: e.activation(out=RHb[d][:], in_=A0[:, 0:128], func=AF.Copy), reads=[TA0], writes=[TRH[d]])
                for h in range(2):
                    P.op("pe", mm(A2[hs[h], 0:128], Sb[d][:, :], Hd["z"][:, cl, 2 + h, :], True, False), reads=[TSb[d], TH[hb][d]], writes=[TA2])
                    P.op("pe", mm(A2[hs[h], 0:128], vT_[:, cl, hs[h]], S2[d][:, cl, 2 + h, :], False, False), reads=[TTM[hb][d], Tst], writes=[TA2])
                yield
                for h in range(2):
                    P.op("pe", mm(A0[:, 128 + h * 64:128 + (h + 1) * 64], XTs[d][:, cl, h, :], RHb[d][:, h * 64:(h + 1) * 64]), reads=[Tst, TRH[d]], writes=[TA0])
                P.op("dve", lambda e: e.tensor_copy(out=Ub[d][:], in_=A0[:, 128:256]), reads=[TA0], writes=[TU[d]])
                yield
                for h in range(2):
                    P.op("pe", mm(A1[hs[h], 128:192], bT_[:, cl, hs[h]], Ub[d][:, h * 64:(h + 1) * 64], True, False), reads=[TTM[hb][d], TU[d]], writes=[TA1])
                    P.op("pe", mm(A1[hs[h], 128:192], kT_[:, cl, hs[h]], vT_[:, cl, hs[h]], False, True), reads=[TTM[hb][d]], writes=[TA1])
                P.op("dve", lambda e: e.scalar_tensor_tensor(out=Sb[d][:], in0=A1[:, 128:192], scalar=Dd[d][:, gc:gc + 1], in1=S0D[d][:], op0=ALU.mult, op1=ALU.add),
                     reads=[TA1, TD[d], TS0[d]], writes=[TSb[d]])
                P.op("dve", lambda e: e.scalar_tensor_tensor(out=Sf[d][:], in0=A1[:, 128:192], scalar=Dd[d][:, gc:gc + 1], in1=S0D[d][:], op0=ALU.mult, op1=ALU.add),
                     reads=[TA1, TD[d], TS0[d]], writes=[TS[d]])
                for h in range(2):
                    P.op("pe", mm(A2[hs[h], 0:128], Ub[d][:, h * 64:(h + 1) * 64], S1[d][:, cl, 2 + h, :], False, True), reads=[TU[d], Tst], writes=[TA2])
                P.op("act", lambda e: e.activation(out=of[d][:, seg * SEG + cl * 128:seg * SEG + (cl + 1) * 128], in_=A2[:, 0:128], func=AF.Copy), reads=[TA2], writes=[Tof[d][seg]])
                yield

            def post(hp):
                seg_order = []
                lo, hi = 0, NSEG - 1
                while lo <= hi:
                    seg_order.append(lo)
                    if hi != lo:
                        seg_order.append(hi)
                    lo += 1
                    hi -= 1
                for seg in seg_order:
                    t0 = seg * SEG
                    hc = slice(hp * 128, (hp + 1) * 128)
                    pz, Tpz = bgb[0], Tbgb[0]
                    pz2, Tpz2 = bgb[1], Tbgb[1]
                    P.dma("sp", [(F["r"][:], self.us[s, hp * 128:(hp + 1) * 128, t0:t0 + SEG]),
                                 (F["k"][:], self.us[s, 512 + hp * 128:512 + (hp + 1) * 128, t0:t0 + SEG]),
                                 (F["v"][:], self.us[s, 1024 + hp * 128:1024 + (hp + 1) * 128, t0:t0 + SEG])],
                          reads=[self.tdr("us", s)], writes=[TF["r"], TF["k"], TF["v"]])
                    P.op("dve", lambda e: e.tensor_tensor(out=F["c"][:], in0=of[0][:, t0:t0 + SEG], in1=of[1][:, t0:t0 + SEG], op=ALU.add), reads=[Tof[0][seg], Tof[1][seg]], writes=[TF["c"]])
                    for g in range(SEG // 512):
                        gs = slice(g * 512, (g + 1) * 512)
                        P.op("pe", mm(pz[:, :], ones_f, F["c"][:, gs]), reads=[self.Tc, TF["c"]], writes=[Tpz])
                        P.op("dve", lambda e: e.scalar_tensor_tensor(out=F["t"][:, gs], in0=pz[:, :], scalar=-1.0 / 64, in1=F["c"][:, gs], op0=ALU.mult, op1=ALU.add), reads=[Tpz, TF["c"]], writes=[TF["t"]])
                    yield
                    P.op("act", lambda e: e.activation(out=F["sq"][:], in_=F["t"][:], func=AF.Square), reads=[TF["t"]], writes=[TF["sq"]])
                    for g in range(SEG // 512):
                        gs = slice(g * 512, (g + 1) * 512)
                        P.op("pe", mm(pz2[:, :], ones_f, F["sq"][:, gs]), reads=[self.Tc, TF["sq"]], writes=[Tpz2])
                        P.op("act", lambda e: e.activation(out=F["e"][:, gs], in_=pz2[:, :], func=AF.Ln, bias=self.c_eps_gn, scale=1.0 / 64), reads=[Tpz2, self.Tc], writes=[TF["e"]])
                    P.op("act", lambda e: e.activation(out=F["e"][:], in_=F["e"][:], func=AF.Exp, scale=-0.5), reads=[TF["e"]], writes=[TF["e"]])
                    P.op("dve", lambda e: e.tensor_tensor(out=F["t"][:], in0=F["t"][:], in1=F["e"][:], op=ALU.mult), reads=[TF["t"], TF["e"]], writes=[TF["t"]])
                    P.op("act", lambda e: e.activation(out=F["t"][:], in_=F["t"][:], func=AF.Identity, scale=pv[:, PV_GG + hp:PV_GG + hp + 1], bias=pv[:, PV_GB + hp:PV_GB + hp + 1]), reads=[TF["t"], Tpv], writes=[TF["t"]])
                    yield
                    for g in range(SEG // 512):
                        gs = slice(g * 512, (g + 1) * 512)
                        ts_ = slice(t0 + g * 512, t0 + (g + 1) * 512)
                        P.op("pe", mm(pz[:, :], lorab[:, 2, hc], twa[:, ts_]), reads=[Tlw, Ttw], writes=[Tpz])
                        P.op("act", lambda e: e.activation(out=F["i"][:, gs], in_=pz[:, :], func=AF.Sigmoid, bias=pv[:, PV_A0 + hp:PV_A0 + hp + 1], scale=1.0), reads=[Tpz, Tpv], writes=[TF["i"]])
                        P.op("pe", mm(pz2[:, :], lorab[:, 3, hc], twa[:, ts_]), reads=[Tlw, Ttw], writes=[Tpz2])
                        P.op("act", lambda e: e.activation(out=F["z"][:, gs], in_=pz2[:, :], func=AF.Sigmoid, bias=pv[:, PV_A0 + 4 + hp:PV_A0 + 4 + hp + 1], scale=1.0), reads=[Tpz2, Tpv], writes=[TF["z"]])
                    yield
                    P.op("pool", lambda e: e.tensor_tensor(out=F["i"][:], in0=F["i"][:], in1=F["z"][:], op=ALU.add), reads=[TF["i"], TF["z"]], writes=[TF["i"]])
                    P.op("dve", lambda e: e.tensor_scalar(out=F["i"][:], in0=F["i"][:], scalar1=pv[:, PV_KA + hp:PV_KA + hp + 1], scalar2=omka[:, 4 + hp:5 + hp], op0=ALU.mult, op1=ALU.add), reads=[TF["i"], Tpv], writes=[TF["i"]])
                    P.op("dve", lambda e: e.tensor_tensor(out=F["i"][:], in0=F["i"][:], in1=F["k"][:], op=ALU.mult), reads=[TF["i"], TF["k"]], writes=[TF["i"]])
                    P.op("dve", lambda e: e.scalar_tensor_tensor(out=F["i"][:], in0=F["i"][:], scalar=pv[:, PV_RK + hp:PV_RK + hp + 1], in1=F["r"][:], op0=ALU.mult, op1=ALU.mult), reads=[TF["i"], Tpv, TF["r"]], writes=[TF["i"]])
                    for g in range(SEG // 512):
                        gs = slice(g * 512, (g + 1) * 512)
                        ts_ = slice(t0 + g * 512, t0 + (g + 1) * 512)
                        P.op("pe", mm(pz[:, :], ones_f, F["i"][:, gs]), reads=[self.Tc, TF["i"]], writes=[Tpz])
                        P.op("dve", lambda e: e.tensor_tensor(out=F["m"][:, gs], in0=pz[:, :], in1=F["v"][:, gs], op=ALU.mult), reads=[Tpz, TF["v"]], writes=[TF["m"]])
                    yield
                    P.op("pool", lambda e: e.tensor_tensor(out=F["t"][:], in0=F["t"][:], in1=F["m"][:], op=ALU.add), reads=[TF["t"], TF["m"]], writes=[TF["t"]])
                    for g in range(SEG // 512):
                        gs = slice(g * 512, (g + 1) * 512)
                        ts_ = slice(t0 + g * 512, t0 + (g + 1) * 512)
                        P.op("pe", mm(pz2[:, :], gupb[:, hc], sg[:, ts_]), reads=[Tlw, Ttw], writes=[Tpz2])
                        P.op("dve", lambda e: e.tensor_tensor(out=yH[:, gs], in0=pz2[:, :], in1=F["t"][:, gs], op=ALU.mult), reads=[Tpz2, TF["t"]], writes=[TyH])
                    P.dma("sp", [(self.ya[s, hp * 128:(hp + 1) * 128, t0:t0 + SEG], yH[:])], reads=[TyH], writes=[self.tdr("ya", s)])
                    yield

            def chain(gens):
                for g in gens:
                    for _ in g:
                        yield

            def step(gen, n=1):
                if gen is None:
                    return False
                for _ in range(n):
                    try:
                        next(gen)
                    except StopIteration:
                        return False
                return True

            iters = [(hp, sgi) for hp in range(4) for sgi in range(NSEG)]

            def bg_for(it):
                gens = []
                hp, sgi = iters[it]
                if sgi == 0 and hp > 0:
                    gens.append(post(hp - 1))
                if it + 1 < len(iters):
                    hp2, sg2 = iters[it + 1]
                    segs2 = (sg2, NSEG - 1 - sg2)
                    for d in range(2):
                        gens.append(prep(hp2, d, segs2[d], (it + 1) % 2))
                return chain(gens)

            for d in range(2):
                step(prep(0, d, (0, NSEG - 1)[d], 0), 10 ** 6)
            for it, (hp, sgi) in enumerate(iters):
                hb = it % 2
                segs = (sgi, NSEG - 1 - sgi)
                bg = bg_for(it)
                bg_alive = True
                if BG_MODE == 0:
                    step(bg, 10 ** 6)
                    bg_alive = False
                units = []
                for ci in range(NCS):
                    units.append((0, ci))
                    units.append((1, NCS - 1 - ci))
                queue = list(units)
                active = [None] * NSLOT
                while queue or any(a is not None for a in active):
                    for k in range(NSLOT):
                        if active[k] is None and queue:
                            u = queue.pop(0)
                            active[k] = ph1(u[0], u[1], k, hb)
                        if active[k] is not None and not step(active[k]):
                            active[k] = None
                    if bg_alive:
                        bg_alive = step(bg)
                if sgi == 0:
                    for d in range(2):
                        P.op("pool", lambda e: e.memset(Sf[d][:], 0.0), writes=[TS[d]])
                        P.op("pool", lambda e: e.memset(Sb[d][:], 0.0), writes=[TSb[d]])
                        P.op("pool", lambda e: e.memset(S0D[d][:], 0.0), writes=[TS0[d]])
                for ci in range(NCS):
                    cls = (ci, NCS - 1 - ci)
                    gens = [ph2(d, segs[d], cls[d], hb) for d in range(2)]
                    alive = [True, True]
                    while any(alive):
                        for d in range(2):
                            if alive[d]:
                                alive[d] = step(gens[d])
                        if bg_alive and BG_MODE >= 2:
                            bg_alive = step(bg)
                if bg_alive:
                    step(bg, 10 ** 6)
            step(post(3), 10 ** 6)


    def stage_N(self, l, s):
        P, nc, S, R = self.P, self.nc, self.S, self.R
        plan, variants = na_plan(R)
        NB = R // 8
        nvar = max(len(variants), 1)
        P.barrier()
        with ExitStack() as es:
            qT = self.sb(es, "qT", [128, S], BF16)
            kz = [self.sb(es, "kz%d" % h, [128, S], BF16) for h in range(2)]
            V = self.sb(es, "Vn", [128, self.NT, 128], BF16)
            Tq = T()
            tab = self.sb(es, "tab", [128, 2, 2, NE * 64], F32)
            Ttab = T()
            rmk = self.sb(es, "rmk", [128, nvar, 8, 512], F32)
            Trm = T()
            onesb = self.sb(es, "onesb", [128, 64], BF16)
            Tones = T()
            NE1, NEX, NSC, LA = 5, 5, 4, 3
            e1 = [self.sb(es, "e1_%d" % i, [128, 512], F32) for i in range(NE1)]
            Te1 = [T() for _ in range(NE1)]
            ex = [self.sb(es, "ex_%d" % i, [128, 512], BF16) for i in range(NEX)]
            Tex = [T() for _ in range(NEX)]
            rec = self.sb(es, "rec", [128, 512], F32)
            Trec = T()
            ybT = self.sb(es, "ybT", [128, S], BF16)
            TybT = T()
            sc = [self.ps(es, "sc%d" % i, [128, 512]) for i in range(NSC)]
            Tsc = [T() for _ in range(NSC)]
            NUM = [self.ps(es, "NUM%d" % i, [128, 512]) for i in range(2)]
            DEN = [self.ps(es, "DEN%d" % i, [128, 512]) for i in range(2)]
            TN, TDn = [T(), T()], [T(), T()]
            P.op("pool", lambda e: e.memset(onesb[:], 1.0), writes=[Tones])
            for h in range(2):
                P.op("pool", lambda e: e.memset(kz[h][:], 0.0), writes=[Tq])
            if len(variants) > 0:
                P.dma("sp", [(rmk[:, v], self.rmask[v]) for v in range(len(variants))], writes=[Trm])
            vsrc = self.vna[s].rearrange("(n p) c -> p n c", p=128)
            si = 0
            ei = 0
            for hp in range(4):
                pairs = [(qT[:], self.qk[s, hp * 128:(hp + 1) * 128, :]),
                         (V[:], vsrc[:, :, hp * 128:(hp + 1) * 128])]
                for h in range(2):
                    pairs.append((kz[h][h * 64:(h + 1) * 64, :], self.qk[s, CW + hp * 128 + h * 64:CW + hp * 128 + (h + 1) * 64, :]))
                P.dma("sp", pairs, reads=[self.tdr("qk", s), self.tdr("vna", s)], writes=[Tq])
                tp = []
                for h in range(2):
                    for tt in range(2):
                        for r2 in range(2):
                            tp.append((tab[r2 * 64:(r2 + 1) * 64, h, tt, :].rearrange("p (m c) -> p m c", c=64),
                                       self.btab[l, 2 * hp + h, tt, :, (1 - r2):(1 - r2) + NE, :]))
                P.dma("sp", tp, writes=[Ttab])
                units = []
                for b in range(NB):
                    p0, npair, off, var = plan[b]
                    for h in range(2):
                        for j in range(npair):
                            units.append((b, h, j))

                def sc_part(u, idx):
                    b, h, j = u
                    p0, npair, off, var = plan[b]
                    qs = slice(b * 512, (b + 1) * 512)
                    tt = 0 if var is None else 1
                    keys = slice((p0 + j) * 128, (p0 + j + 1) * 128)
                    scb, Tscb = sc[idx % NSC], Tsc[idx % NSC]
                    P.op("pe", mm(scb[:, :], kz[h][:, keys], qT[:, qs]), reads=[Tq], writes=[Tscb])
                    m0 = 14 - off - 2 * j
                    e_, Te_ = e1[idx % NE1], Te1[idx % NE1]
                    x_, Tx_ = ex[idx % NEX], Tex[idx % NEX]
                    P.op("dve", lambda e: e.tensor_tensor(out=e_[:], in0=scb[:, :], in1=tab[:, h, tt, m0 * 64:(m0 + 8) * 64], op=ALU.add), reads=[Tscb, Ttab], writes=[Te_])
                    if var is not None:
                        P.op("pool", lambda e: e.tensor_tensor(out=e_[:], in0=e_[:], in1=rmk[:, var, j, :], op=ALU.add), reads=[Te_, Trm], writes=[Te_])
                    P.op("act", lambda e: e.activation(out=x_[:], in_=e_[:], func=AF.Exp), reads=[Te_], writes=[Tx_])

                def pv_part(u, idx):
                    b, h, j = u
                    p0, npair, off, var = plan[b]
                    qs = slice(b * 512, (b + 1) * 512)
                    ph = slice(h * 64, (h + 1) * 64)
                    x_, Tx_ = ex[idx % NEX], Tex[idx % NEX]
                    nb_, db_ = NUM[b % 2], DEN[b % 2]
                    P.op("pe", mm(nb_[ph, :], V[:, p0 + j, h * 64:(h + 1) * 64], x_[:], j == 0, j == npair - 1), reads=[Tq, Tx_], writes=[TN[b % 2]])
                    P.op("pe", mm(db_[ph, :], onesb[:], x_[:], j == 0, j == npair - 1), reads=[Tones, Tx_], writes=[TDn[b % 2]])
                    if h == 1 and j == npair - 1:
                        P.op("dve", lambda e: e.reciprocal(out=rec[:], in_=db_[:, :]), reads=[TDn[b % 2]], writes=[Trec])
                        P.op("dve", lambda e: e.tensor_tensor(out=ybT[:, qs], in0=nb_[:, :], in1=rec[:], op=ALU.mult), reads=[TN[b % 2], Trec], writes=[TybT])

                nu = len(units)
                for i in range(nu + LA):
                    if i < nu:
                        sc_part(units[i], i)
                    if i >= LA:
                        pv_part(units[i - LA], i - LA)
                P.dma("sp", [(self.yb[s, hp * 128:(hp + 1) * 128, :], ybT[:])], reads=[TybT], writes=[self.tdr("yb", s)])

    def stage_M(self, l):
        P, nc, S = self.P, self.nc, self.S
        P.barrier()
        with ExitStack() as es:
            WA = self.sb(es, "WA", [128, 4, D], BF16)
            WB = self.sb(es, "WB", [128, 4, D], BF16)
            WO = self.sb(es, "WO", [128, 8, D], BF16)
            Tw = T()
            P.dma("pool", [(WA[:], self.w_a[l].rearrange("(kc p) c -> p kc c", p=128)),
                           (WB[:], self.w_b[l].rearrange("(kc p) c -> p kc c", p=128))], writes=[Tw])
            wo_src = self.w_out[l].rearrange("(kc p) c -> p kc c", p=128)
            P.dma("pool", [(WO[:, 0:4], wo_src[:, 0:4]), (WO[:, 4:8], wo_src[:, 4:8])], writes=[Tw])
            lng = self.sb(es, "lngM", [128, D], F32)
            lnb = self.sb(es, "lnbM", [128, D], F32)
            Tgb = T()
            P.dma("sp", [(lng[:], self.lnp[2 + 4 * l]), (lnb[:], self.lnp[3 + 4 * l])], writes=[Tgb])
            yaT = [self.sb(es, "yaT%d" % i, [128, 4, 512], BF16) for i in range(2)]
            ybT = [self.sb(es, "ybTm%d" % i, [128, 4, 512], BF16) for i in range(2)]
            gab = [self.sb(es, "gab%d" % i, [128, 16, 512], BF16) for i in range(2)]
            Tin = [T(), T()]
            mT = self.sb(es, "mT", [128, 8, 512], BF16)
            TmT = T()
            m1 = [self.sb(es, "m1_%d" % i, [128, 512], F32) for i in range(2)]
            m2 = [self.sb(es, "m2_%d" % i, [128, 512], F32) for i in range(2)]
            Tm1 = [T(), T()]
            Tm2 = [T(), T()]
            xt = [self.sb(es, "xtM%d" % i, [128, D], F32) for i in range(2)]
            Txt = [T(), T()]
            st = self.sb(es, "stM", [128, 2, 6], F32)
            mv = self.sb(es, "mvM", [128, 4], F32)
            Tst = T()
            pa = [self.ps(es, "pa%d" % i, [128, 512]) for i in range(2)]
            pb = [self.ps(es, "pbm%d" % i, [128, 512]) for i in range(2)]
            po = [self.ps(es, "po%d" % i, [128, 512]) for i in range(4)]
            Tpa, Tpb, Tpo = [T(), T()], [T(), T()], [T() for _ in range(4)]
            gi = 0
            ti = 0
            for s in range(self.NS):
                ya_src = self.ya[s].rearrange("(c p) t -> p c t", p=128)
                yb_src = self.yb[s].rearrange("(c p) t -> p c t", p=128)
                gt_src = self.gt[s].rearrange("(c p) t -> p c t", p=128)
                for g in range(self.NG):
                    ts = slice(g * 512, (g + 1) * 512)
                    bsel = gi % 2
                    gi += 1
                    P.dma("sp", [(yaT[bsel][:], ya_src[:, :, ts]), (ybT[bsel][:], yb_src[:, :, ts]), (gab[bsel][:], gt_src[:, :, ts])],
                          reads=[self.tdr("ya", s), self.tdr("yb", s), self.tdr("gt", s)], writes=[Tin[bsel]])
                    for j in range(8):
                        js = slice(j * 128, (j + 1) * 128)
                        a_, Ta_ = pa[j % 2], Tpa[j % 2]
                        b_, Tb_ = pb[j % 2], Tpb[j % 2]
                        for kc in range(4):
                            P.op("pe", mm(a_[:, :], WA[:, kc, js], yaT[bsel][:, kc, :], kc == 0, kc == 3), reads=[Tw, Tin[bsel]], writes=[Ta_])
                        for kc in range(4):
                            P.op("pe", mm(b_[:, :], WB[:, kc, js], ybT[bsel][:, kc, :], kc == 0, kc == 3), reads=[Tw, Tin[bsel]], writes=[Tb_])
                        P.op("dve", lambda e: e.tensor_tensor(out=m1[j % 2][:], in0=a_[:, :], in1=gab[bsel][:, j, :], op=ALU.mult), reads=[Ta_, Tin[bsel]], writes=[Tm1[j % 2]])
                        P.op("dve", lambda e: e.tensor_tensor(out=m2[j % 2][:], in0=b_[:, :], in1=gab[bsel][:, 8 + j, :], op=ALU.mult), reads=[Tb_, Tin[bsel]], writes=[Tm2[j % 2]])
                        P.op("pool", lambda e: e.tensor_tensor(out=mT[:, j, :], in0=m1[j % 2][:], in1=m2[j % 2][:], op=ALU.add), reads=[Tm1[j % 2], Tm2[j % 2]], writes=[TmT])
                    for t in range(4):
                        tok = slice(t * 128, (t + 1) * 128)
                        r0 = s * S + g * 512 + t * 128
                        x_, Tx_ = xt[ti % 2], Txt[ti % 2]
                        P.dma("sp", [(x_[:], self.xres[r0:r0 + 128, :])], reads=[self.tdr("xres", s, g * 4 + t)], writes=[Tx_])
                        for n in range(2):
                            o_, To_ = po[(2 * ti + n) % 4], Tpo[(2 * ti + n) % 4]
                            for kc in range(8):
                                P.op("pe", mm(o_[:, :], mT[:, kc, tok], WO[:, kc, n * 512:(n + 1) * 512], kc == 0, kc == 7), reads=[TmT, Tw], writes=[To_])
                            P.op("dve", lambda e: e.scalar_tensor_tensor(out=x_[:, n * 512:(n + 1) * 512], in0=x_[:, n * 512:(n + 1) * 512], scalar=ALPHA, in1=o_[:, :], op0=ALU.mult, op1=ALU.add),
                                 reads=[Tx_, To_], writes=[Tx_])
                        ti += 1
                        self.layer_norm(P, x_, Tx_, lng[:], lnb[:], Tgb, st, mv, Tst, x_[:], Tx_)
                        P.dma("sp", [(self.xres[r0:r0 + 128, :], x_[:])], reads=[Tx_], writes=[self.tdr("xres", s, g * 4 + t)])

    def stage_F(self, l, last):
        P, nc, S = self.P, self.nc, self.S
        P.barrier()
        G = 256
        NFC = DFF // 128
        with ExitStack() as es:
            W1 = self.sb(es, "W1", [128, 8, 2 * DFF], BF16)
            W2 = self.sb(es, "W2", [128, NFC, D], BF16)
            Tw1 = [T() for _ in range(8)]
            Tw2 = T()
            w1s = self.w_f1[l].rearrange("(kc p) c -> p kc c", p=128)
            for kc in range(8):
                P.dma("pool", [(W1[:, kc, :], w1s[:, kc, :])], writes=[Tw1[kc]])
            w2s = self.w_f2[l].rearrange("(fc p) c -> p fc c", p=128)
            for a in range(0, NFC, 6):
                bnd = min(a + 6, NFC)
                P.dma("pool", [(W2[:, a:bnd, :], w2s[:, a:bnd, :])], writes=[Tw2])
            lng = self.sb(es, "lngF", [128, D], F32)
            lnb = self.sb(es, "lnbF", [128, D], F32)
            Tgb = T()
            P.dma("sp", [(lng[:], self.lnp[4 + 4 * l]), (lnb[:], self.lnp[5 + 4 * l])], writes=[Tgb])
            xt = [self.sb(es, "xtF%d" % i, [128, D], F32) for i in range(2)]
            Txt = [T(), T()]
            x1T = self.sb(es, "x1T", [128, 8, G], BF16)
            Tx1T = T()
            hT = self.sb(es, "hT", [128, NFC, G], BF16)
            ThT = T()
            sgl = [self.sb(es, "sgl%d" % i, [128, G], F32) for i in range(2)]
            Tsg = [T(), T()]
            st = self.sb(es, "stF", [128, 2, 6], F32)
            mv = self.sb(es, "mvF", [128, 4], F32)
            Tst = T()
            pt = [self.ps(es, "pt%d" % i, [128, 512]) for i in range(2)]
            pg = [self.ps(es, "pg%d" % i, [128, 512]) for i in range(2)]
            pu = [self.ps(es, "pu%d" % i, [128, 512]) for i in range(2)]
            po = [self.ps(es, "pof%d" % i, [128, 512]) for i in range(2)]
            Tpt, Tpg, Tpu, Tpo = [T(), T()], [T(), T()], [T(), T()], [T(), T()]
            dst = self.y if last else self.xres
            for s in range(self.NS):
                for g in range(S // G):
                    for t in range(G // 128):
                        r0 = s * S + g * G + t * 128
                        P.dma("sp", [(xt[t][:], self.xres[r0:r0 + 128, :])], reads=[self.tdr("xres", s, g * 2 + t)], writes=[Txt[t]])
                        for half in range(2):
                            for q in range(4):
                                kc = half * 4 + q
                                P.op("pe", lambda e: e.transpose(pt[half][:, q * 128:(q + 1) * 128], xt[t][:, kc * 128:(kc + 1) * 128], self.identf),
                                     reads=[Txt[t], self.Tc], writes=[Tpt[half]])
                            o_ap = x1T[:, half * 4:half * 4 + 4, t * 128:(t + 1) * 128]
                            i_ap = pt[half][:].rearrange("p (a b) -> p a b", a=4)
                            if half == 0:
                                P.op("act", lambda e: e.activation(out=o_ap, in_=i_ap, func=AF.Copy), reads=[Tpt[half]], writes=[Tx1T])
                            else:
                                P.op("dve", lambda e: e.tensor_copy(out=o_ap, in_=i_ap), reads=[Tpt[half]], writes=[Tx1T])
                    for f in range(NFC):
                        g_, Tg_ = pg[f % 2], Tpg[f % 2]
                        u_, Tu_ = pu[f % 2], Tpu[f % 2]
                        for kc in range(8):
                            P.op("pe", mm(g_[:, 0:G], W1[:, kc, f * 128:(f + 1) * 128], x1T[:, kc, :], kc == 0, kc == 7), reads=[Tw1[kc], Tx1T], writes=[Tg_])
                        for kc in range(8):
                            P.op("pe", mm(u_[:, 0:G], W1[:, kc, DFF + f * 128:DFF + (f + 1) * 128], x1T[:, kc, :], kc == 0, kc == 7), reads=[Tw1[kc], Tx1T], writes=[Tu_])
                        P.op("act", lambda e: e.activation(out=sgl[f % 2][:], in_=g_[:, 0:G], func=AF.Silu), reads=[Tg_], writes=[Tsg[f % 2]])
                        P.op("dve", lambda e: e.tensor_tensor(out=hT[:, f, :], in0=u_[:, 0:G], in1=sgl[f % 2][:], op=ALU.mult), reads=[Tu_, Tsg[f % 2]], writes=[ThT])
                    for t in range(G // 128):
                        r0 = s * S + g * G + t * 128
                        tok = slice(t * 128, (t + 1) * 128)
                        for n in range(2):
                            o_, To_ = po[n], Tpo[n]
                            for fc in range(NFC):
                                P.op("pe", mm(o_[:, :], hT[:, fc, tok], W2[:, fc, n * 512:(n + 1) * 512], fc == 0, fc == NFC - 1), reads=[ThT, Tw2], writes=[To_])
                            P.op("dve", lambda e: e.scalar_tensor_tensor(out=xt[t][:, n * 512:(n + 1) * 512], in0=xt[t][:, n * 512:(n + 1) * 512], scalar=ALPHA, in1=o_[:, :], op0=ALU.mult, op1=ALU.add),
                                 reads=[Txt[t], To_], writes=[Txt[t]])
                        self.layer_norm(P, xt[t], Txt[t], lng[:], lnb[:], Tgb, st, mv, Tst, xt[t][:], Txt[t])
                        P.dma("sp", [(dst[r0:r0 + 128, :], xt[t][:])], reads=[Txt[t]], writes=[self.tdr("xres" if not last else "y", s, g * 2 + t)])


PV_MU0, PV_MU1 = 0, 14
PV_W0 = 28
PV_A0 = 36
PV_KK = 44
PV_KA = 48
PV_RK = 52
PV_GG = 56
PV_GB = 60
NPV = 64
C_ID = 0
C_ONES = 128
C_EPS = 256
C_MASK = 260
NCST = C_MASK + 4 * 128
CB_ID, CB_ID2, CB_ONES, CB_SMASK = 0, 128, 384, 448
NCB = 448


def na_plan(R):
    npair = min(8, R // 2)
    plan, variants = [], []
    kh = min(8, R)
    for b in range(R // 8):
        i0 = 8 * b
        mid = (i0 - 4 >= 0) and (i0 + 11 <= R) and kh == 8
        p0 = min(max(4 * b - 2, 0), R // 2 - npair)
        off = 2 * p0 - i0
        var = None
        if not mid:
            m = np.full((npair, 2, 8), NEG, np.float32)
            for j in range(npair):
                for r2 in range(2):
                    kr = 2 * (p0 + j) + r2
                    for qr in range(8):
                        i = i0 + qr
                        rs = min(max(i - kh // 2, 0), R - kh)
                        if rs <= kr < rs + kh:
                            m[j, r2, qr] = 0.0
            key = m.tobytes()
            for vi, (k2, _) in enumerate(variants):
                if k2 == key:
                    var = vi
                    break
            else:
                variants.append((key, m))
                var = len(variants) - 1
        plan.append((p0, npair, off, var))
    return plan, [m for (_, m) in variants]


def host_consts(R):
    cst = np.zeros((128, NCST), np.float32)
    cst[:, C_ID:C_ID + 128] = np.eye(128, dtype=np.float32)
    od = np.zeros((128, 128), np.float32)
    od[0:64, 0:64] = 1.0
    od[64:128, 64:128] = 1.0
    cst[:, C_ONES:C_ONES + 128] = od
    cst[:, C_EPS] = LN_EPS
    cst[:, C_EPS + 1] = GN_EPS
    cst[:, C_EPS + 2] = 1e-30
    i = np.arange(128)
    cst[:, C_MASK + 0:C_MASK + 128] = (i[:, None] > i[None, :])
    cst[:, C_MASK + 128:C_MASK + 256] = (i[:, None] < i[None, :])
    cst[:, C_MASK + 256:C_MASK + 384] = (i[:, None] >= i[None, :])
    cst[:, C_MASK + 384:C_MASK + 512] = (i[:, None] <= i[None, :])
    plan, variants = na_plan(R)
    rm = np.zeros((max(len(variants), 1), 128, 8, 512), np.float32)
    for vi, m in enumerate(variants):
        for j in range(m.shape[0]):
            for r2 in range(2):
                rm[vi, r2 * 64:(r2 + 1) * 64, j, :] = np.repeat(m[j, r2], 64)[None, :]
    return cst, rm


def host_layer_params(inp, L):
    f = lambda a: np.asarray(a, np.float32)
    pvec = np.zeros((L, 128, NPV), np.float32)
    lora = np.zeros((L, 128, 4, CW), np.float32)
    btab = np.full((L, 8, 2, 64, NE + 1, 64), NEG, np.float32)
    lnp = np.zeros((2 + 4 * L, 128, D), np.float32)
    lnp[0] = f(inp["ln_in_g"])[None, :]
    lnp[1] = f(inp["ln_in_b"])[None, :]
    c = np.arange(64)
    qc = np.arange(64)
    cs = np.clip(qc - 8, 0, GRID_W - 16)
    colvalid = (c[:, None] >= cs[None, :]) & (c[:, None] < cs[None, :] + 16)
    dc = c[:, None] - qc[None, :] + 15
    dcc = np.clip(dc, 0, 30)
    for l in range(L):
        mu = f(inp["shift_mu"][l])
        pad = np.zeros((2, 14 * 128), np.float32)
        pad[:, :RWC] = mu
        pvec[l, :, PV_MU0:PV_MU0 + 14] = pad[0].reshape(14, 128).T
        pvec[l, :, PV_MU1:PV_MU1 + 14] = pad[1].reshape(14, 128).T
        for d in range(2):
            pvec[l, :, PV_W0 + 4 * d:PV_W0 + 4 * d + 4] = f(inp["decay_w0"][l, d]).reshape(4, 128).T
            pvec[l, :, PV_A0 + 4 * d:PV_A0 + 4 * d + 4] = f(inp["iclr_a0"][l, d]).reshape(4, 128).T
            lora[l, 32 * d:32 * d + 32, d, :] = f(inp["decay_up"][l, d])
            lora[l, 64 + 32 * d:64 + 32 * d + 32, 2 + d, :] = f(inp["iclr_up"][l, d])
        pvec[l, :, PV_KK:PV_KK + 4] = f(inp["k_k"][l]).reshape(4, 128).T
        pvec[l, :, PV_KA:PV_KA + 4] = f(inp["k_a"][l]).reshape(4, 128).T
        pvec[l, :, PV_RK:PV_RK + 4] = f(inp["r_k"][l]).reshape(4, 128).T
        pvec[l, :, PV_GG:PV_GG + 4] = f(inp["gn_g"][l]).reshape(4, 128).T
        pvec[l, :, PV_GB:PV_GB + 4] = f(inp["gn_b"][l]).reshape(4, 128).T
        lnp[2 + 4 * l + 0] = f(inp["ln1_g"][l])[None, :]
        lnp[2 + 4 * l + 1] = f(inp["ln1_b"][l])[None, :]
        lnp[2 + 4 * l + 2] = f(inp["ln2_g"][l])[None, :]
        lnp[2 + 4 * l + 3] = f(inp["ln2_b"][l])[None, :]
        rpb = f(inp["na_rpb"][l])
        for mp in range(NE + 1):
            delta = 15 - mp
            dr = delta + 7
            if 0 <= dr <= 14:
                vals = np.where(colvalid[None], rpb[:, dr][:, dcc], NEG)
                btab[l, :, 1, :, mp, :] = vals
                if 3 <= dr <= 10:
                    btab[l, :, 0, :, mp, :] = vals
    return pvec, lora, lnp, btab


def kernel(**inputs):
    L, NS, S, NCORE = 4, 2, 4096, 8
    b = Builder(L, NS, S)
    nc = b.build()
    pvec, lora, lnp, btab = host_layer_params(inputs, L)
    cst, rm = host_consts(S // GRID_W)
    x = np.asarray(inputs["x"], np.float32)
    f = lambda a: np.ascontiguousarray(np.asarray(a, np.float32))
    shared = {"w_in": f(inputs["w_in"]), "w_a": f(inputs["w_branch_rwkv"]), "w_b": f(inputs["w_branch_na"]),
              "w_out": f(inputs["w_out"]), "w_f1": f(inputs["w_ffn_in"]), "w_f2": f(inputs["w_ffn_out"]),
              "pvec": pvec, "lora": lora, "gup": f(inputs["gate_up"]), "lnp": lnp, "btab": btab, "cst": cst, "rmask": rm}
    in_maps = []
    for c in range(NCORE):
        m = dict(shared)
        m["x"] = np.ascontiguousarray(x[c * NS:(c + 1) * NS].reshape(NS * S, D))
        in_maps.append(m)
    res = run_bass_kernel_spmd(nc, in_maps, core_ids=list(range(NCORE)))
    out = np.stack([np.asarray(r["y"]).reshape(NS, S, D) for r in res.results], axis=0)
    return out.reshape(NCORE * NS, S, D).astype(np.float32)
```

```python
import math
import os
import numpy as np
from contextlib import ExitStack
import concourse.bass as bass
import concourse.mybir as mybir
from concourse.bass_utils import run_bass_kernel_spmd

F32 = mybir.dt.float32
BF16 = mybir.dt.bfloat16
AF = mybir.ActivationFunctionType
ALU = mybir.AluOpType
AX = mybir.AxisListType

D = 1024
CW = 512
HD = 64
DFF = 2816
RWC = 1760
INC = 5344
NAQ0 = 1760
GT0 = 1760 + 1536
GRID_W = 64
DEPTH_FULL = 4
ALPHA = (2.0 * DEPTH_FULL) ** 0.25
LN_EPS = 1e-5
GN_EPS = 64e-5
KAPPA = math.exp(-0.5)
NEG = -30000.0
CH = 128
NE = 31
BG_MODE = 1


class T:
    __slots__ = ("w", "r")

    def __init__(self):
        self.w = None
        self.r = {}


class Prog:
    NDMA = 20

    def __init__(self, nc, es, same_engine_sync=True):
        self.nc = nc
        self.same = same_engine_sync
        self.eng = {"pe": nc.tensor, "act": nc.scalar, "dve": nc.vector,
                    "pool": nc.gpsimd, "sp": nc.sync}
        self.sem = {k: es.enter_context(nc.semaphore("s_" + k)) for k in self.eng}
        self.cnt = {k: 0 for k in self.eng}
        self.waited = {k: {} for k in self.eng}
        self.dsem, self.dcnt, self.drr = {}, {}, {}
        for q in ("sp", "pool"):
            self.drr[q] = 0
            for i in range(self.NDMA):
                k = ("d", q, i)
                self.dsem[k] = es.enter_context(nc.semaphore("d_%s%d" % (q, i)))
                self.dcnt[k] = 0
        self.n_ins = 0
        self.last_rg = None

    def semof(self, k):
        return self.dsem[k] if isinstance(k, tuple) else self.sem[k]

    def _wait(self, X, k, v):
        if self.waited[X].get(k, 0) < v:
            self.eng[X].wait_ge(self.semof(k), v)
            self.waited[X][k] = v
            self.n_ins += 1

    def _deps(self, X, reads, writes):
        deps = {}
        for t in reads:
            if t.w is not None and t.w[1] > deps.get(t.w[0], 0):
                deps[t.w[0]] = t.w[1]
        for t in writes:
            if t.w is not None and t.w[1] > deps.get(t.w[0], 0):
                deps[t.w[0]] = t.w[1]
            for k, v in t.r.items():
                if v > deps.get(k, 0):
                    deps[k] = v
        need = []
        for k, v in deps.items():
            if k == X and (X == "pe" or not self.same):
                continue
            if self.waited[X].get(k, 0) < v:
                need.append((k, v))
        for k, v in need[1:]:
            self._wait(X, k, v)
        return need[0] if need else None

    def op(self, X, fn, reads=(), writes=(), rg=None):
        if X == "pe":
            if rg != self.last_rg and self.cnt["pe"] > 0:
                self._wait("pe", "pe", self.cnt["pe"])
            self.last_rg = rg
        emb = self._deps(X, reads, writes)
        ins = fn(self.eng[X])
        if emb is not None:
            ins._wait_ge(self.semof(emb[0]), emb[1])
            self.waited[X][emb[0]] = emb[1]
        self.cnt[X] += 1
        c = self.cnt[X]
        ins.then_inc(self.sem[X], 1)
        self.n_ins += 1
        for t in reads:
            if t.r.get(X, 0) < c:
                t.r[X] = c
        for t in writes:
            t.w = (X, c)
            t.r = {}
        return ins

    def dma(self, Q, pairs, reads=(), writes=()):
        emb = self._deps(Q, reads, writes)
        if emb is not None:
            self._wait(Q, emb[0], emb[1])
        i = self.drr[Q]
        self.drr[Q] = (i + 1) % self.NDMA
        k = ("d", Q, i)
        self._wait(Q, k, self.dcnt[k])
        for (o, a) in pairs:
            self.eng[Q].dma_start(out=o, in_=a).then_inc(self.dsem[k], 16)
            self.dcnt[k] += 16
            self.n_ins += 1
        c = self.dcnt[k]
        for t in reads:
            t.r[k] = c
        for t in writes:
            t.w = (k, c)
            t.r = {}

    def barrier(self):
        for X in self.eng:
            for k in self.eng:
                if k != X and self.cnt[k] > 0:
                    self._wait(X, k, self.cnt[k])
            for k, v in self.dcnt.items():
                if v > 0:
                    self._wait(X, k, v)


def mm(out, lhsT, rhs, start=True, stop=True):
    return lambda e: e.matmul(out, lhsT=lhsT, rhs=rhs, start=start, stop=stop)


class Builder:
    def __init__(self, L, NS, S, dbg=False):
        self.L, self.NS, self.S, self.dbg = L, NS, S, dbg
        self.R = S // GRID_W
        self.NT = S // 128
        self.NG = S // 512
        nc = self.nc = bass.Bass("TRN2", target_bir_lowering=False)
        self.es = ExitStack()
        dt = lambda n, s, d, kind="ExternalInput": nc.dram_tensor(n, s, d, kind=kind).ap()
        sk = "ExternalOutput" if dbg else "Internal"
        NTOK = NS * S
        self.x_in = dt("x", [NTOK, D], F32)
        self.w_in = dt("w_in", [L, D, INC], F32)
        self.w_a = dt("w_a", [L, CW, D], F32)
        self.w_b = dt("w_b", [L, CW, D], F32)
        self.w_out = dt("w_out", [L, D, D], F32)
        self.w_f1 = dt("w_f1", [L, D, 2 * DFF], F32)
        self.w_f2 = dt("w_f2", [L, DFF, D], F32)
        self.pvec = dt("pvec", [L, 128, NPV], F32)
        self.lora = dt("lora", [L, 128, 4, CW], F32)
        self.gup = dt("gup", [L, 96, CW], F32)
        self.lnp = dt("lnp", [2 + 4 * L, 128, D], F32)
        self.btab = dt("btab", [L, 8, 2, 64, NE + 1, 64], F32)
        self.cst = dt("cst", [128, NCST], F32)
        self.nvar = len(na_plan(self.R)[1])
        self.rmask = dt("rmask", [max(self.nvar, 1), 128, 8, 512], F32)
        self.y = dt("y", [NTOK, D], F32, kind="ExternalOutput")
        self.xres = dt("xres", [NTOK, D], F32, kind=sk)
        self.us = dt("us", [NS, 14 * 128, S], F32, kind=sk)
        self.qk = dt("qk", [NS, 1024, S], BF16, kind=sk)
        self.vna = dt("vna", [NS, S, CW], BF16, kind=sk)
        self.gt = dt("gt", [NS, 2048, S], BF16, kind=sk)
        self.ya = dt("ya", [NS, CW, S], BF16, kind=sk)
        self.yb = dt("yb", [NS, CW, S], BF16, kind=sk)
        self.Tdr = {}
        self.uid = 0

    def tdr(self, *key):
        if key not in self.Tdr:
            self.Tdr[key] = T()
        return self.Tdr[key]

    def sb(self, es, name, shape, dt):
        self.uid += 1
        return es.enter_context(self.nc.sbuf_tensor("%s_%d" % (name, self.uid), shape, dt))

    def ps(self, es, name, shape, dt=F32):
        self.uid += 1
        return es.enter_context(self.nc.psum_tensor("%s_%d" % (name, self.uid), shape, dt))

    def layer_norm(self, P, xt, Tx, g, b, Tgb, st, mv, Tst, out, Tout):
        P.op("dve", lambda e: e.bn_stats(out=st[:, 0, :], in_=xt[:, 0:512]), reads=[Tx], writes=[Tst])
        P.op("dve", lambda e: e.bn_stats(out=st[:, 1, :], in_=xt[:, 512:1024]), reads=[Tx], writes=[Tst])
        P.op("dve", lambda e: e.bn_aggr(out=mv[:, 0:2], in_=st[:].rearrange("p a b -> p (a b)")), reads=[Tst], writes=[Tst])
        P.op("act", lambda e: e.activation(out=mv[:, 2:3], in_=mv[:, 1:2], func=AF.Sqrt, bias=self.c_eps_ln, scale=1.0), reads=[Tst], writes=[Tst])
        P.op("dve", lambda e: e.reciprocal(out=mv[:, 3:4], in_=mv[:, 2:3]), reads=[Tst], writes=[Tst])
        P.op("dve", lambda e: e.tensor_scalar(out=xt[:], in0=xt[:], scalar1=mv[:, 0:1], scalar2=mv[:, 3:4], op0=ALU.subtract, op1=ALU.mult), reads=[Tx, Tst], writes=[Tx])
        P.op("pool", lambda e: e.tensor_tensor(out=xt[:], in0=xt[:], in1=g, op=ALU.mult), reads=[Tx, Tgb], writes=[Tx])
        P.op("pool", lambda e: e.tensor_tensor(out=out, in0=xt[:], in1=b, op=ALU.add), reads=[Tx, Tgb], writes=[Tout])

    def build(self, stages="APRNMF"):
        nc = self.nc
        with self.es as es:
            P = self.P = Prog(nc, es)
            self.cstt = self.sb(es, "cstt", [128, NCST], F32)
            self.Tc = T()
            P.dma("sp", [(self.cstt[:], self.cst[:, :])], writes=[self.Tc])
            c = self.cstt
            self.identf = c[:, C_ID:C_ID + 128]
            self.c_eps_ln = c[:, C_EPS:C_EPS + 1]
            self.c_eps_gn = c[:, C_EPS + 1:C_EPS + 2]
            self.c_eps_kk = c[:, C_EPS + 2:C_EPS + 3]
            self.cb = self.sb(es, "cstb", [128, NCB], BF16)
            self.Tcb = T()
            P.op("dve", lambda e: e.tensor_copy(out=self.cb[:, CB_ID:CB_ID + 128], in_=c[:, C_ID:C_ID + 128]), reads=[self.Tc], writes=[self.Tcb])
            P.op("dve", lambda e: e.tensor_copy(out=self.cb[:, CB_ID2:CB_ID2 + 128], in_=c[:, C_ID:C_ID + 128]), reads=[self.Tc], writes=[self.Tcb])
            P.op("dve", lambda e: e.tensor_copy(out=self.cb[:, CB_ID2 + 128:CB_ID2 + 256], in_=c[:, C_ID:C_ID + 128]), reads=[self.Tc], writes=[self.Tcb])
            P.op("dve", lambda e: e.tensor_copy(out=self.cb[:, CB_ONES:CB_ONES + 64], in_=c[:, C_ONES:C_ONES + 64]), reads=[self.Tc], writes=[self.Tcb])
            for l in range(self.L):
                for s in range(self.NS):
                    if "A" in stages:
                        self.stage_AP(l, s, do_p=("P" in stages))
                    if "R" in stages:
                        self.stage_R(l, s)
                    if "N" in stages:
                        self.stage_N(l, s)
                if "M" in stages:
                    self.stage_M(l)
                if "F" in stages:
                    self.stage_F(l, last=(l == self.L - 1))
            P.barrier()
        return nc

    def stage_AP(self, l, s, do_p=True):
        P, nc, S = self.P, self.nc, self.S
        P.barrier()
        with ExitStack() as es:
            xT = self.sb(es, "xT", [128, 8, S], BF16)
            TxT = [T() for _ in range(self.NT)]
            NXT = 4
            xt = [self.sb(es, "xt%d" % i, [128, D], F32) for i in range(NXT)]
            Txt = [T() for _ in range(NXT)]
            xo = [self.sb(es, "xo%d" % i, [128, D], F32) for i in range(2)] if l == 0 else None
            Txo = [T(), T()]
            st = self.sb(es, "st", [128, 2, 6], F32)
            mv = self.sb(es, "mv", [128, 4], F32)
            Tst = T()
            pp = [self.ps(es, "pp%d" % i, [128, 512]) for i in range(8)]
            Tpp = [T() for _ in range(8)]
            if l == 0:
                lng = self.sb(es, "lng", [128, D], F32)
                lnb = self.sb(es, "lnb", [128, D], F32)
                Tgb = T()
                P.dma("sp", [(lng[:], self.lnp[0]), (lnb[:], self.lnp[1])], writes=[Tgb])
            src = self.x_in if l == 0 else self.xres
            for t in range(self.NT):
                r0 = s * S + t * 128
                b = t % NXT
                b2 = t % 2
                P.dma("sp", [(xt[b][:], src[r0:r0 + 128, :])], reads=[self.tdr("xres", s, t)], writes=[Txt[b]])
                if l == 0:
                    self.layer_norm(P, xt[b], Txt[b], lng[:], lnb[:], Tgb, st, mv, Tst, xo[b2][:], Txo[b2])
                    P.dma("sp", [(self.xres[r0:r0 + 128, :], xo[b2][:])], reads=[Txo[b2]], writes=[self.tdr("xres", s, t)])
                    xs, Txs = xo[b2], Txo[b2]
                else:
                    xs, Txs = xt[b], Txt[b]
                for half in range(2):
                    pb = pp[(2 * t + half) % 8]
                    Tpb = Tpp[(2 * t + half) % 8]
                    for q in range(4):
                        kc = half * 4 + q
                        P.op("pe", lambda e: e.transpose(pb[:, q * 128:(q + 1) * 128], xs[:, kc * 128:(kc + 1) * 128], self.identf),
                             reads=[Txs, self.Tc], writes=[Tpb])
                    eng = "act" if half == 0 else "dve"
                    o_ap = xT[:, half * 4:half * 4 + 4, t * 128:(t + 1) * 128]
                    i_ap = pb[:].rearrange("p (a b) -> p a b", a=4)
                    if eng == "act":
                        P.op("act", lambda e: e.activation(out=o_ap, in_=i_ap, func=AF.Copy), reads=[Tpb], writes=[TxT[t]])
                    else:
                        P.op("dve", lambda e: e.tensor_copy(out=o_ap, in_=i_ap), reads=[Tpb], writes=[TxT[t]])
            if not do_p:
                if self.dbg:
                    self.dbg_xT = (xT, TxT)
                return
            wb = [self.sb(es, "wb%d" % i, [128, 8, 128], BF16) for i in range(2)]
            Twb = [T(), T()]
            wv = self.sb(es, "wv", [128, 8, 512], BF16)
            Twv = T()
            ub = [self.sb(es, "ub%d" % i, [128, S + 2], F32) for i in range(2)]
            Tub = [T(), T()]
            ut1 = self.sb(es, "ut", [128, S], F32)
            Tut1 = T()
            ut = [ut1, ut1]
            Tut = [Tut1, Tut1]
            ob = [self.sb(es, "ob%d" % i, [128, S], BF16) for i in range(2)]
            Tob = [T(), T()]
            vb = [self.sb(es, "vb%d" % i, [128, 512], BF16) for i in range(2)]
            Tvb = [T(), T()]
            pv = self.sb(es, "pvA", [128, NPV], F32)
            Tpv = T()
            P.dma("sp", [(pv[:], self.pvec[l])], writes=[Tpv])
            c0 = self.sb(es, "c0", [128, 14], F32)
            P.op("dve", lambda e: e.tensor_tensor(out=c0[:], in0=pv[:, PV_MU0:PV_MU0 + 14], in1=pv[:, PV_MU1:PV_MU1 + 14], op=ALU.add), reads=[Tpv], writes=[Tpv])
            P.op("dve", lambda e: e.tensor_scalar(out=c0[:], in0=c0[:], scalar1=-1.0, scalar2=1.0, op0=ALU.mult, op1=ALU.add), reads=[Tpv], writes=[Tpv])
            for i in range(2):
                P.op("pool", lambda e: e.memset(ub[i][:, 0:1], 0.0), writes=[Tub[i]])
                P.op("pool", lambda e: e.memset(ub[i][:, S + 1:S + 2], 0.0), writes=[Tub[i]])
            w_l = self.w_in[l].rearrange("(kc p) c -> p kc c", p=128)
            blocks = [("u", j * 128, min(128, RWC - j * 128), j) for j in range(14)]
            blocks += [("q", NAQ0 + j * 128, 128, j) for j in range(4)]
            blocks += [("k", NAQ0 + CW + j * 128, 128, j) for j in range(4)]
            blocks += [("g", GT0 + j * 128, 128, j) for j in range(16)]
            pi = 0
            for bi, (kind, c0c, ncol, j) in enumerate(blocks):
                w = wb[bi % 2]
                Tw = Twb[bi % 2]
                P.dma("pool", [(w[:, :, 0:ncol], w_l[:, :, c0c:c0c + ncol])], writes=[Tw])
                if kind == "u":
                    dst, Tdst = ub[j % 2], Tub[j % 2]
                else:
                    dst, Tdst = ob[bi % 2], Tob[bi % 2]
                for g in range(self.NG):
                    pb, Tpb = pp[pi % 8], Tpp[pi % 8]
                    pi += 1
                    for kc in range(8):
                        P.op("pe", mm(pb[0:ncol, :], w[:, kc, 0:ncol], xT[:, kc, g * 512:(g + 1) * 512], kc == 0, kc == 7),
                             reads=[Tw] + TxT[g * 4:g * 4 + 4], writes=[Tpb])
                    if kind == "u":
                        P.op("act", lambda e: e.activation(out=dst[0:ncol, 1 + g * 512:1 + (g + 1) * 512], in_=pb[0:ncol, :], func=AF.Copy), reads=[Tpb], writes=[Tdst])
                    elif kind == "q":
                        P.op("act", lambda e: e.activation(out=dst[:, g * 512:(g + 1) * 512], in_=pb[:, :], func=AF.Copy, scale=0.125), reads=[Tpb], writes=[Tdst])
                    elif kind == "k":
                        P.op("dve", lambda e: e.tensor_copy(out=dst[:, g * 512:(g + 1) * 512], in_=pb[:, :]), reads=[Tpb], writes=[Tdst])
                    else:
                        P.op("act", lambda e: e.activation(out=dst[:, g * 512:(g + 1) * 512], in_=pb[:, :], func=AF.Sigmoid), reads=[Tpb], writes=[Tdst])
                if kind == "u":
                    u_, Tu_ = ut[j % 2], Tut[j % 2]
                    P.op("act", lambda e: e.activation(out=u_[0:ncol, :], in_=dst[0:ncol, 1:S + 1], func=AF.Copy, scale=c0[0:ncol, j:j + 1]), reads=[Tdst, Tpv], writes=[Tu_])
                    P.op("dve", lambda e: e.scalar_tensor_tensor(out=u_[0:ncol, :], in0=dst[0:ncol, 0:S], scalar=pv[0:ncol, PV_MU0 + j:PV_MU0 + j + 1], in1=u_[0:ncol, :], op0=ALU.mult, op1=ALU.add), reads=[Tdst, Tpv, Tu_], writes=[Tu_])
                    P.op("dve", lambda e: e.scalar_tensor_tensor(out=u_[0:ncol, :], in0=dst[0:ncol, 2:S + 2], scalar=pv[0:ncol, PV_MU1 + j:PV_MU1 + j + 1], in1=u_[0:ncol, :], op0=ALU.mult, op1=ALU.add), reads=[Tdst, Tpv, Tu_], writes=[Tu_])
                    P.dma("sp", [(self.us[s, j * 128:j * 128 + ncol, :], u_[0:ncol, :])], reads=[Tu_], writes=[self.tdr("us", s)])
                elif kind == "q":
                    P.dma("sp", [(self.qk[s, j * 128:(j + 1) * 128, :], dst[:])], reads=[Tdst], writes=[self.tdr("qk", s)])
                elif kind == "k":
                    P.dma("sp", [(self.qk[s, CW + j * 128:CW + (j + 1) * 128, :], dst[:])], reads=[Tdst], writes=[self.tdr("qk", s)])
                else:
                    P.dma("sp", [(self.gt[s, j * 128:(j + 1) * 128, :], dst[:])], reads=[Tdst], writes=[self.tdr("gt", s)])
            vc0 = NAQ0 + 2 * CW
            P.dma("pool", [(wv[:], w_l[:, :, vc0:vc0 + CW])], writes=[Twv])
            for t in range(self.NT):
                pb, Tpb = pp[pi % 8], Tpp[pi % 8]
                pi += 1
                for kc in range(8):
                    P.op("pe", mm(pb[:, :], xT[:, kc, t * 128:(t + 1) * 128], wv[:, kc, :], kc == 0, kc == 7), reads=[Twv, TxT[t]], writes=[Tpb])
                v_, Tv_ = vb[t % 2], Tvb[t % 2]
                P.op("dve", lambda e: e.tensor_copy(out=v_[:], in_=pb[:, :]), reads=[Tpb], writes=[Tv_])
                P.dma("sp", [(self.vna[s, t * 128:(t + 1) * 128, :], v_[:])], reads=[Tv_], writes=[self.tdr("vna", s)])


    def stage_R(self, l, s):
        P, nc, S = self.P, self.nc, self.S
        SEG = 512
        NSEG = S // SEG
        NCS = SEG // CH
        P.barrier()
        c = self.cstt
        ones_f = c[:, C_ONES:C_ONES + 128]
        with ExitStack() as es:
            pv = self.sb(es, "pvR", [128, NPV], F32)
            Tpv = T()
            P.dma("sp", [(pv[:], self.pvec[l])], writes=[Tpv])
            omka = self.sb(es, "omka", [128, 8], F32)
            P.op("dve", lambda e: e.tensor_scalar(out=omka[:, 0:4], in0=pv[:, PV_KA:PV_KA + 4], scalar1=-1.0, scalar2=1.0, op0=ALU.mult, op1=ALU.add), reads=[Tpv], writes=[Tpv])
            P.op("dve", lambda e: e.tensor_scalar(out=omka[:, 4:8], in0=pv[:, PV_KA:PV_KA + 4], scalar1=-2.0, scalar2=2.0, op0=ALU.mult, op1=ALU.add), reads=[Tpv], writes=[Tpv])
            lorab = self.sb(es, "lorab", [128, 4, CW], BF16)
            gupb = self.sb(es, "gupb", [128, CW], BF16)
            Tlw = T()
            P.op("pool", lambda e: e.memset(gupb[:], 0.0), writes=[Tlw])
            P.dma("pool", [(lorab[:], self.lora[l]), (gupb[0:96, :], self.gup[l])], writes=[Tlw])
            twa = self.sb(es, "twa", [128, S], BF16)
            sg = self.sb(es, "sg", [128, S], BF16)
            Ttw = T()
            P.op("pool", lambda e: e.memset(sg[:], 0.0), writes=[Ttw])
            smask = self.sb(es, "smask", [128, SEG], BF16)
            Tsm = T()
            P.op("pool", lambda e: e.memset(smask[:], 1.0), writes=[Tsm])
            P.op("pool", lambda e: e.memset(smask[:].rearrange("p (c t) -> p c t", t=CH)[:, :, 0:1], 0.0), writes=[Tsm])
            Tmk = T()
            SL, SU, IL, IU = (c[:, C_MASK + i * 128:C_MASK + (i + 1) * 128] for i in range(4))
            Fn_ = ["r", "k", "v", "z", "i", "c", "t", "e", "p", "m", "kk", "sq", "kd", "x"]
            F = {n: self.sb(es, "F" + n, [128, SEG], F32) for n in Fn_}
            TF = {n: T() for n in Fn_}
            bank = [[self.ps(es, "pb%d_%d" % (d, i), [128, 512]) for i in range(4)] for d in range(2)]
            Tbank = [[T() for i in range(4)] for d in range(2)]
            for seg in range(NSEG):
                t0 = seg * SEG
                P.dma("sp", [(F["x"][:], self.us[s, 1536:1664, t0:t0 + SEG])], reads=[self.tdr("us", s)], writes=[TF["x"]])
                P.op("act", lambda e: e.activation(out=twa[0:64, t0:t0 + SEG], in_=F["x"][0:64, :], func=AF.Tanh), reads=[TF["x"]], writes=[Ttw])
                P.op("dve", lambda e: e.tensor_copy(out=twa[64:128, t0:t0 + SEG], in_=F["x"][64:128, :]), reads=[TF["x"]], writes=[Ttw])
                P.dma("sp", [(F["x"][0:96, :], self.us[s, 1664:1760, t0:t0 + SEG])], reads=[self.tdr("us", s)], writes=[TF["x"]])
                P.op("act", lambda e: e.activation(out=sg[0:96, t0:t0 + SEG], in_=F["x"][0:96, :], func=AF.Sigmoid), reads=[TF["x"]], writes=[Ttw])
            H = [[{n: self.sb(es, "H%s%d" % (n, d), [128, SEG], BF16) for n in ("k", "b", "v")} for d in range(2)] for hb in range(2)]
            TH = [[T(), T()] for hb in range(2)]
            for hb in range(2):
                for d in range(2):
                    H[hb][d]["z"] = self.sb(es, "Hz%d" % d, [128, NCS, 4, 128], BF16)
                    P.op("pool", lambda e: e.memset(H[hb][d]["z"][:], 0.0), writes=[TH[hb][d]])
            TMt = [[{n: self.sb(es, "M%s%d" % (n, d), [128, NCS, 128], BF16) for n in ("k", "b", "v")} for d in range(2)] for hb in range(2)]
            TTM = [[T(), T()] for hb in range(2)]
            of = [self.sb(es, "of%d" % d, [128, S], F32) for d in range(2)]
            Tof = [[T() for _ in range(NSEG)] for d in range(2)]
            Dd = [self.sb(es, "Dd%d" % d, [128, S // CH + 1], F32) for d in range(2)]
            TD = [T(), T()]
            Sf = [self.sb(es, "Sf%d" % d, [128, 64], F32) for d in range(2)]
            Sb = [self.sb(es, "Sb%d" % d, [128, 64], BF16) for d in range(2)]
            S0D = [self.sb(es, "S0D%d" % d, [128, 64], F32) for d in range(2)]
            TS = [T(), T()]
            TSb = [T(), T()]
            TS0 = [T(), T()]
            NSLOT = 3
            Pp = [[self.sb(es, "Pp%d_%d" % (k, i), [128, 2, 128], BF16) for i in range(2)] for k in range(NSLOT)]
            TPp = [[T(), T()] for _ in range(NSLOT)]
            W = [[self.sb(es, "W%d_%d" % (k, i), [128, 2, 2, 128], BF16) for i in range(2)] for k in range(NSLOT)]
            TWp = [[T(), T()] for _ in range(NSLOT)]
            TWx = [[T(), T()] for _ in range(NSLOT)]
            S1 = [self.sb(es, "S1_%d" % d, [128, NCS, 4, 128], BF16) for d in range(2)]
            S2 = [self.sb(es, "S2_%d" % d, [128, NCS, 4, 128], BF16) for d in range(2)]
            XTs = [self.sb(es, "XTs%d" % d, [128, NCS, 2, 128], BF16) for d in range(2)]
            TST = [[T() for _ in range(NCS)] for d in range(2)]
            mkT = [self.sb(es, "mkT%d" % d, [128, 512], F32) for d in range(2)]
            mkA = [self.sb(es, "mkA%d" % d, [128, 256], F32) for d in range(2)]
            for d, (ms, mi, ma) in enumerate(((SU, IU, SL), (SL, IL, SU))):
                for i, m in enumerate((ms, ms, mi, mi)):
                    P.op("pool", lambda e: e.tensor_copy(out=mkT[d][:, i * 128:(i + 1) * 128], in_=m), reads=[self.Tc], writes=[Tmk])
                for i in range(2):
                    P.op("pool", lambda e: e.tensor_copy(out=mkA[d][:, i * 128:(i + 1) * 128], in_=ma), reads=[self.Tc], writes=[Tmk])
            RHb = [self.sb(es, "RHb%d" % d, [128, 128], BF16) for d in range(2)]
            TRH = [T(), T()]
            Ub = [self.sb(es, "Ub%d" % d, [128, 128], BF16) for d in range(2)]
            TU = [T(), T()]
            yH = self.sb(es, "yH", [128, SEG], BF16)
            TyH = T()
            id2 = self.cb[:, CB_ID2:CB_ID2 + 256].rearrange("p (h t) -> p h t", h=2)
            idb = self.cb[:, CB_ID:CB_ID + 128]

            flat = [bank[0][0], bank[0][1], bank[0][2], bank[0][3], bank[1][0], bank[1][1], bank[1][2], bank[1][3]]
            Tflat = [Tbank[0][0], Tbank[0][1], Tbank[0][2], Tbank[0][3], Tbank[1][0], Tbank[1][1], Tbank[1][2], Tbank[1][3]]
            bgb, Tbgb = (flat[6], flat[7]), (Tflat[6], Tflat[7])

            def prep(hp, d, seg, hb):
                t0 = seg * SEG
                hc = slice(hp * 128, (hp + 1) * 128)
                pz, Tpz = bgb[0], Tbgb[0]
                P.dma("sp", [(F["r"][:], self.us[s, hp * 128:(hp + 1) * 128, t0:t0 + SEG]),
                             (F["k"][:], self.us[s, 512 + hp * 128:512 + (hp + 1) * 128, t0:t0 + SEG]),
                             (F["v"][:], self.us[s, 1024 + hp * 128:1024 + (hp + 1) * 128, t0:t0 + SEG])],
                      reads=[self.tdr("us", s)], writes=[TF["r"], TF["k"], TF["v"]])
                for g in range(SEG // 512):
                    gs = slice(g * 512, (g + 1) * 512)
                    ts_ = slice(t0 + g * 512, t0 + (g + 1) * 512)
                    P.op("pe", mm(pz[:, :], lorab[:, d, hc], twa[:, ts_]), reads=[Tlw, Ttw], writes=[Tpz])
                    P.op("act", lambda e: e.activation(out=F["z"][:, gs], in_=pz[:, :], func=AF.Sigmoid, bias=pv[:, PV_W0 + 4 * d + hp:PV_W0 + 4 * d + hp + 1], scale=1.0), reads=[Tpz, Tpv], writes=[TF["z"]])
                    P.op("pe", mm(pz[:, :], lorab[:, 2 + d, hc], twa[:, ts_]), reads=[Tlw, Ttw], writes=[Tpz])
                    P.op("act", lambda e: e.activation(out=F["i"][:, gs], in_=pz[:, :], func=AF.Sigmoid, bias=pv[:, PV_A0 + 4 * d + hp:PV_A0 + 4 * d + hp + 1], scale=1.0), reads=[Tpz, Tpv], writes=[TF["i"]])
                yield
                P.op("dve", lambda e: e.tensor_tensor_scan(out=F["c"][:], data0=smask[:], data1=F["z"][:], initial=0.0, op0=ALU.mult, op1=ALU.add), reads=[Tsm, TF["z"]], writes=[TF["c"]])
                if d == 0:
                    cum, Tcum = F["c"], TF["c"]
                else:
                    P.op("dve", lambda e: e.tensor_tensor(out=F["t"][:], in0=F["z"][:], in1=F["c"][:], op=ALU.subtract), reads=[TF["z"], TF["c"]], writes=[TF["t"]])
                    c3 = F["c"][:].rearrange("p (c t) -> p c t", t=CH)
                    P.op("dve", lambda e: e.tensor_tensor(out=F["t"][:].rearrange("p (c t) -> p c t", t=CH), in0=F["t"][:].rearrange("p (c t) -> p c t", t=CH),
                                                          in1=c3[:, :, CH - 1:CH].to_broadcast([128, NCS, CH]), op=ALU.add), reads=[TF["t"], TF["c"]], writes=[TF["t"]])
                    cum, Tcum = F["t"], TF["t"]
                P.op("dve", lambda e: e.tensor_tensor(out=F["e"][:], in0=cum[:], in1=F["z"][:], op=ALU.subtract), reads=[Tcum, TF["z"]], writes=[TF["e"]])
                P.op("act", lambda e: e.activation(out=F["e"][:], in_=F["e"][:], func=AF.Exp, scale=-KAPPA), reads=[TF["e"]], writes=[TF["e"]])
                P.op("act", lambda e: e.activation(out=F["p"][:], in_=cum[:], func=AF.Exp, scale=-KAPPA), reads=[Tcum], writes=[TF["p"]])
                P.op("act", lambda e: e.activation(out=F["m"][:], in_=cum[:], func=AF.Exp, scale=KAPPA), reads=[Tcum], writes=[TF["m"]])
                p3 = F["p"][:].rearrange("p (c t) -> p c t", t=CH)
                edge = CH - 1 if d == 0 else 0
                P.op("pool", lambda e: e.tensor_copy(out=Dd[d][:, seg * NCS:(seg + 1) * NCS], in_=p3[:, :, edge]), reads=[TF["p"]], writes=[TD[d]])
                yield
                P.op("act", lambda e: e.activation(out=F["kk"][:], in_=F["k"][:], func=AF.Copy, scale=pv[:, PV_KK + hp:PV_KK + hp + 1]), reads=[TF["k"], Tpv], writes=[TF["kk"]])
                P.op("act", lambda e: e.activation(out=F["sq"][:], in_=F["kk"][:], func=AF.Square), reads=[TF["kk"]], writes=[TF["sq"]])
                for g in range(SEG // 512):
                    gs = slice(g * 512, (g + 1) * 512)
                    P.op("pe", mm(pz[:, :], ones_f, F["sq"][:, gs]), reads=[self.Tc, TF["sq"]], writes=[Tpz])
                    P.op("act", lambda e: e.activation(out=F["sq"][:, gs], in_=pz[:, :], func=AF.Ln, bias=self.c_eps_kk, scale=1.0), reads=[Tpz, self.Tc], writes=[TF["sq"]])
                P.op("act", lambda e: e.activation(out=F["sq"][:], in_=F["sq"][:], func=AF.Exp, scale=-0.5), reads=[TF["sq"]], writes=[TF["sq"]])
                P.op("dve", lambda e: e.tensor_tensor(out=F["kk"][:], in0=F["kk"][:], in1=F["sq"][:], op=ALU.mult), reads=[TF["kk"], TF["sq"]], writes=[TF["kk"]])
                yield
                P.op("dve", lambda e: e.tensor_scalar(out=F["kd"][:], in0=F["i"][:], scalar1=pv[:, PV_KA + hp:PV_KA + hp + 1], scalar2=omka[:, hp:hp + 1], op0=ALU.mult, op1=ALU.add), reads=[TF["i"], Tpv], writes=[TF["kd"]])
                P.op("dve", lambda e: e.tensor_tensor(out=F["kd"][:], in0=F["kd"][:], in1=F["k"][:], op=ALU.mult), reads=[TF["kd"], TF["k"]], writes=[TF["kd"]])
                yield
                Hd = H[hb][d]
                for h in range(2):
                    ph = slice(h * 64, (h + 1) * 64)
                    P.op("dve", lambda e: e.tensor_tensor(out=Hd["z"][ph, :, 2 + h, :], in0=F["r"][ph, :].rearrange("p (c t) -> p c t", t=CH), in1=F["p"][ph, :].rearrange("p (c t) -> p c t", t=CH), op=ALU.mult), reads=[TF["r"], TF["p"]], writes=[TH[hb][d]])
                P.op("dve", lambda e: e.tensor_tensor(out=Hd["k"][:], in0=F["kd"][:], in1=F["m"][:], op=ALU.mult), reads=[TF["kd"], TF["m"]], writes=[TH[hb][d]])
                for h in range(2):
                    ph = slice(h * 64, (h + 1) * 64)
                    P.op("dve", lambda e: e.scalar_tensor_tensor(out=Hd["z"][ph, :, h, :], in0=F["kk"][ph, :].rearrange("p (c t) -> p c t", t=CH), scalar=-1.0, in1=F["e"][ph, :].rearrange("p (c t) -> p c t", t=CH), op0=ALU.mult, op1=ALU.mult), reads=[TF["kk"], TF["e"]], writes=[TH[hb][d]])
                P.op("pool", lambda e: e.tensor_tensor(out=F["x"][:], in0=F["kk"][:], in1=F["i"][:], op=ALU.mult), reads=[TF["kk"], TF["i"]], writes=[TF["x"]])
                P.op("dve", lambda e: e.tensor_tensor(out=Hd["b"][:], in0=F["x"][:], in1=F["m"][:], op=ALU.mult), reads=[TF["x"], TF["m"]], writes=[TH[hb][d]])
                P.op("act", lambda e: e.activation(out=Hd["v"][:], in_=F["v"][:], func=AF.Copy), reads=[TF["v"]], writes=[TH[hb][d]])
                yield
                bi = 0
                for n in ("k", "b", "v"):
                    for half in range(NCS // 4):
                        pb, Tpb = bgb[bi % 2], Tbgb[bi % 2]
                        bi += 1
                        for q in range(4):
                            cc = half * 4 + q
                            P.op("pe", mm(pb[:, q * 128:(q + 1) * 128], Hd[n][:, cc * 128:(cc + 1) * 128], idb), reads=[TH[hb][d], self.Tcb], writes=[Tpb])
                        o_ap = TMt[hb][d][n][:, half * 4:half * 4 + 4, :]
                        i_ap = pb[:].rearrange("p (a b) -> p a b", a=4)
                        if bi % 2 == 0:
                            P.op("act", lambda e: e.activation(out=o_ap, in_=i_ap, func=AF.Copy), reads=[Tpb], writes=[TTM[hb][d]])
                        else:
                            P.op("dve", lambda e: e.tensor_copy(out=o_ap, in_=i_ap), reads=[Tpb], writes=[TTM[hb][d]])
                        yield

            def s0d_update(d, gc_next):
                P.op("act", lambda e: e.activation(out=S0D[d][:], in_=Sf[d][:], func=AF.Copy, scale=Dd[d][:, gc_next:gc_next + 1]), reads=[TS[d], TD[d]], writes=[TS0[d]])

            def ph1(d, cl, k, hb):
                bk0, bk1 = flat[2 * k], flat[2 * k + 1]
                Tb0, Tb1 = Tflat[2 * k], Tflat[2 * k + 1]
                Hd = H[hb][d]
                cs = slice(cl * 128, (cl + 1) * 128)
                Zc = Hd["z"][:, cl]
                bcs, kcs = Hd["b"][:, cs], Hd["k"][:, cs]
                Tst = TST[d][cl]
                f2 = lambda ap: ap.rearrange("p a t -> p (a t)")
                P.op("pe", mm(bk0[:, :], bcs, f2(Zc)), reads=[TH[hb][d]], writes=[Tb0])
                P.op("pe", mm(bk1[:, :], kcs, f2(Zc)), reads=[TH[hb][d]], writes=[Tb1])
                P.op("dve", lambda e: e.tensor_tensor(out=f2(S1[d][:, cl]), in0=bk0[:, :], in1=mkT[d][:], op=ALU.mult), reads=[Tb0, Tmk], writes=[Tst])
                P.op("dve", lambda e: e.tensor_tensor(out=f2(S2[d][:, cl]), in0=bk1[:, :], in1=mkT[d][:], op=ALU.mult), reads=[Tb1, Tmk], writes=[Tst])
                for h in range(2):
                    P.op("pe", mm(bk0[:, h * 128:(h + 1) * 128], Zc[:, h, :], bcs), reads=[TH[hb][d]], writes=[Tb0])
                P.op("dve", lambda e: e.tensor_tensor(out=f2(Pp[k][0][:]), in0=bk0[:, 0:256], in1=mkA[d][:], op=ALU.mult), reads=[Tb0, Tmk], writes=[TPp[k][0]])
                P.op("pool", lambda e: e.tensor_tensor(out=W[k][0][:, :, 1, :], in0=S1[d][:, cl, 0:2, :], in1=id2, op=ALU.add), reads=[Tst, self.Tcb], writes=[TWx[k][0]])
                yield
                b0v = bk0[:].rearrange("p (h x) -> p h x", h=2)
                for h in range(2):
                    P.op("pe", mm(bk1[:, h * 128:(h + 1) * 128], S1[d][:, cl, h, :], Pp[k][0][:, h, :]), reads=[Tst, TPp[k][0]], writes=[Tb1])
                    P.op("pe", mm(bk0[:, h * 256:h * 256 + 128], Pp[k][0][:, h, :], S1[d][:, cl, h, :]), reads=[Tst, TPp[k][0]], writes=[Tb0])
                P.op("act", lambda e: e.activation(out=W[k][0][:, :, 0, :], in_=b0v[:, :, 0:128], func=AF.Copy), reads=[Tb0], writes=[TWp[k][0]])
                P.op("dve", lambda e: e.tensor_copy(out=f2(Pp[k][1][:]), in_=bk1[:, 0:256]), reads=[Tb1], writes=[TPp[k][1]])
                yield
                wi, pi = 0, 1
                for j in range(1, 6):
                    Wc, Wn, Pc, Pnx = W[k][wi], W[k][1 - wi], Pp[k][pi], Pp[k][1 - pi]
                    for h in range(2):
                        if os.environ.get("K_VAR", "") == "1":
                            P.op("pe", mm(bk0[:, h * 256:h * 256 + 128], Pc[:, h, :], Wc[:, h, 0, :]), reads=[TPp[k][pi], TWp[k][wi], TWx[k][wi]], writes=[Tb0])
                            P.op("pe", mm(bk0[:, h * 256 + 128:h * 256 + 256], Pc[:, h, :], Wc[:, h, 1, :]), reads=[TPp[k][pi], TWp[k][wi], TWx[k][wi]], writes=[Tb0])
                        else:
                            P.op("pe", mm(bk0[:, h * 256:(h + 1) * 256], Pc[:, h, :], Wc[:, h].rearrange("p a t -> p (a t)")), reads=[TPp[k][pi], TWp[k][wi], TWx[k][wi]], writes=[Tb0])
                        P.op("pe", mm(bk1[:, h * 128:(h + 1) * 128], Wc[:, h, 0, :], Pc[:, h, :]), reads=[TPp[k][pi], TWp[k][wi]], writes=[Tb1])
                    P.op("dve", lambda e: e.tensor_copy(out=Wn[:, :, 0, :], in_=b0v[:, :, 0:128]), reads=[Tb0], writes=[TWp[k][1 - wi]])
                    P.op("dve", lambda e: e.tensor_tensor(out=Wn[:, :, 1, :], in0=b0v[:, :, 128:256], in1=Wc[:, :, 1, :], op=ALU.add), reads=[Tb0, TWx[k][wi]] + ([TWp[k][1 - wi]] if os.environ.get("K_VAR", "") == "2" else []), writes=[TWx[k][1 - wi]])
                    P.op("act", lambda e: e.activation(out=f2(Pnx[:]), in_=bk1[:, 0:256], func=AF.Copy), reads=[Tb1], writes=[TPp[k][1 - pi]])
                    wi, pi = 1 - wi, 1 - pi
                    yield
                Wc, Pc = W[k][wi], Pp[k][pi]
                for h in range(2):
                    P.op("pe", mm(bk0[:, h * 256 + 128:h * 256 + 256], Pc[:, h, :], Wc[:, h, 1, :]), reads=[TPp[k][pi], TWx[k][wi]], writes=[Tb0])
                P.op("dve", lambda e: e.tensor_tensor(out=XTs[d][:, cl], in0=b0v[:, :, 128:256], in1=Wc[:, :, 1, :], op=ALU.add), reads=[Tb0, TWx[k][wi]], writes=[Tst])
                yield

            def ph2(d, seg, cl, hb):
                gc = seg * NCS + cl
                A0, A1 = flat[3 * d], flat[3 * d + 1]
                TA0, TA1 = Tflat[3 * d], Tflat[3 * d + 1]
                Hd = H[hb][d]
                cs = slice(cl * 128, (cl + 1) * 128)
                hs = (slice(0, 64), slice(64, 128))
                kT_, bT_, vT_ = TMt[hb][d]["k"], TMt[hb][d]["b"], TMt[hb][d]["v"]
                Tst = TST[d][cl]
                A2, TA2 = flat[3 * d + 2], Tflat[3 * d + 2]
                s0d_update(d, gc)
                for h in range(2):
                    P.op("pe", mm(A0[:, h * 64:(h + 1) * 64], Hd["z"][:, cl, h, :], Sb[d][:, :], True, False), reads=[TH[hb][d], TSb[d]], writes=[TA0])
                    P.op("pe", mm(A0[:, h * 64:(h + 1) * 64], S2[d][:, cl, h, :], vT_[:, cl, hs[h]], False, True), reads=[Tst, TTM[hb][d]], writes=[TA0])
                P.op("act", lambda e: e.activation(out=RHb[d][:], in_=A0[:, 0:128], func=AF.Copy), reads=[TA0], writes=[TRH[d]])
                for h in range(2):
                    P.op("pe", mm(A2[hs[h], 0:128], Sb[d][:, :], Hd["z"][:, cl, 2 + h, :], True, False), reads=[TSb[d], TH[hb][d]], writes=[TA2])
                    P.op("pe", mm(A2[hs[h], 0:128], vT_[:, cl, hs[h]], S2[d][:, cl, 2 + h, :], False, False), reads=[TTM[hb][d], Tst], writes=[TA2])
                yield
                for h in range(2):
                    P.op("pe", mm(A0[:, 128 + h * 64:128 + (h + 1) * 64], XTs[d][:, cl, h, :], RHb[d][:, h * 64:(h + 1) * 64]), reads=[Tst, TRH[d]], writes=[TA0])
                P.op("dve", lambda e: e.tensor_copy(out=Ub[d][:], in_=A0[:, 128:256]), reads=[TA0], writes=[TU[d]])
                yield
                for h in range(2):
                    P.op("pe", mm(A1[hs[h], 128:192], bT_[:, cl, hs[h]], Ub[d][:, h * 64:(h + 1) * 64], True, False), reads=[TTM[hb][d], TU[d]], writes=[TA1])
                    P.op("pe", mm(A1[hs[h], 128:192], kT_[:, cl, hs[h]], vT_[:, cl, hs[h]], False, True), reads=[TTM[hb][d]], writes=[TA1])
                P.op("dve", lambda e: e.scalar_tensor_tensor(out=Sb[d][:], in0=A1[:, 128:192], scalar=Dd[d][:, gc:gc + 1], in1=S0D[d][:], op0=ALU.mult, op1=ALU.add),
                     reads=[TA1, TD[d], TS0[d]], writes=[TSb[d]])
                P.op("dve", lambda e: e.scalar_tensor_tensor(out=Sf[d][:], in0=A1[:, 128:192], scalar=Dd[d][:, gc:gc + 1], in1=S0D[d][:], op0=ALU.mult, op1=ALU.add),
                     reads=[TA1, TD[d], TS0[d]], writes=[TS[d]])
                for h in range(2):
                    P.op("pe", mm(A2[hs[h], 0:128], Ub[d][:, h * 64:(h + 1) * 64], S1[d][:, cl, 2 + h, :], False, True), reads=[TU[d], Tst], writes=[TA2])
                P.op("act", lambda e: e.activation(out=of[d][:, seg * SEG + cl * 128:seg * SEG + (cl + 1) * 128], in_=A2[:, 0:128], func=AF.Copy), reads=[TA2], writes=[Tof[d][seg]])
                yield

            def post(hp):
                seg_order = []
                lo, hi = 0, NSEG - 1
                while lo <= hi:
                    seg_order.append(lo)
                    if hi != lo:
                        seg_order.append(hi)
                    lo += 1
                    hi -= 1
                for seg in seg_order:
                    t0 = seg * SEG
                    hc = slice(hp * 128, (hp + 1) * 128)
                    pz, Tpz = bgb[0], Tbgb[0]
                    pz2, Tpz2 = bgb[1], Tbgb[1]
                    P.dma("sp", [(F["r"][:], self.us[s, hp * 128:(hp + 1) * 128, t0:t0 + SEG]),
                                 (F["k"][:], self.us[s, 512 + hp * 128:512 + (hp + 1) * 128, t0:t0 + SEG]),
                                 (F["v"][:], self.us[s, 1024 + hp * 128:1024 + (hp + 1) * 128, t0:t0 + SEG])],
                          reads=[self.tdr("us", s)], writes=[TF["r"], TF["k"], TF["v"]])
                    P.op("dve", lambda e: e.tensor_tensor(out=F["c"][:], in0=of[0][:, t0:t0 + SEG], in1=of[1][:, t0:t0 + SEG], op=ALU.add), reads=[Tof[0][seg], Tof[1][seg]], writes=[TF["c"]])
                    for g in range(SEG // 512):
                        gs = slice(g * 512, (g + 1) * 512)
                        P.op("pe", mm(pz[:, :], ones_f, F["c"][:, gs]), reads=[self.Tc, TF["c"]], writes=[Tpz])
                        P.op("dve", lambda e: e.scalar_tensor_tensor(out=F["t"][:, gs], in0=pz[:, :], scalar=-1.0 / 64, in1=F["c"][:, gs], op0=ALU.mult, op1=ALU.add), reads=[Tpz, TF["c"]], writes=[TF["t"]])
                    yield
                    P.op("act", lambda e: e.activation(out=F["sq"][:], in_=F["t"][:], func=AF.Square), reads=[TF["t"]], writes=[TF["sq"]])
                    for g in range(SEG // 512):
                        gs = slice(g * 512, (g + 1) * 512)
                        P.op("pe", mm(pz2[:, :], ones_f, F["sq"][:, gs]), reads=[self.Tc, TF["sq"]], writes=[Tpz2])
                        P.op("act", lambda e: e.activation(out=F["e"][:, gs], in_=pz2[:, :], func=AF.Ln, bias=self.c_eps_gn, scale=1.0 / 64), reads=[Tpz2, self.Tc], writes=[TF["e"]])
                    P.op("act", lambda e: e.activation(out=F["e"][:], in_=F["e"][:], func=AF.Exp, scale=-0.5), reads=[TF["e"]], writes=[TF["e"]])
                    P.op("dve", lambda e: e.tensor_tensor(out=F["t"][:], in0=F["t"][:], in1=F["e"][:], op=ALU.mult), reads=[TF["t"], TF["e"]], writes=[TF["t"]])
                    P.op("act", lambda e: e.activation(out=F["t"][:], in_=F["t"][:], func=AF.Identity, scale=pv[:, PV_GG + hp:PV_GG + hp + 1], bias=pv[:, PV_GB + hp:PV_GB + hp + 1]), reads=[TF["t"], Tpv], writes=[TF["t"]])
                    yield
                    for g in range(SEG // 512):
                        gs = slice(g * 512, (g + 1) * 512)
                        ts_ = slice(t0 + g * 512, t0 + (g + 1) * 512)
                        P.op("pe", mm(pz[:, :], lorab[:, 2, hc], twa[:, ts_]), reads=[Tlw, Ttw], writes=[Tpz])
                        P.op("act", lambda e: e.activation(out=F["i"][:, gs], in_=pz[:, :], func=AF.Sigmoid, bias=pv[:, PV_A0 + hp:PV_A0 + hp + 1], scale=1.0), reads=[Tpz, Tpv], writes=[TF["i"]])
                        P.op("pe", mm(pz2[:, :], lorab[:, 3, hc], twa[:, ts_]), reads=[Tlw, Ttw], writes=[Tpz2])
                        P.op("act", lambda e: e.activation(out=F["z"][:, gs], in_=pz2[:, :], func=AF.Sigmoid, bias=pv[:, PV_A0 + 4 + hp:PV_A0 + 4 + hp + 1], scale=1.0), reads=[Tpz2, Tpv], writes=[TF["z"]])
                    yield
                    P.op("pool", lambda e: e.tensor_tensor(out=F["i"][:], in0=F["i"][:], in1=F["z"][:], op=ALU.add), reads=[TF["i"], TF["z"]], writes=[TF["i"]])
                    P.op("dve", lambda e: e.tensor_scalar(out=F["i"][:], in0=F["i"][:], scalar1=pv[:, PV_KA + hp:PV_KA + hp + 1], scalar2=omka[:, 4 + hp:5 + hp], op0=ALU.mult, op1=ALU.add), reads=[TF["i"], Tpv], writes=[TF["i"]])
                    P.op("dve", lambda e: e.tensor_tensor(out=F["i"][:], in0=F["i"][:], in1=F["k"][:], op=ALU.mult), reads=[TF["i"], TF["k"]], writes=[TF["i"]])
                    P.op("dve", lambda e: e.scalar_tensor_tensor(out=F["i"][:], in0=F["i"][:], scalar=pv[:, PV_RK + hp:PV_RK + hp + 1], in1=F["r"][:], op0=ALU.mult, op1=ALU.mult), reads=[TF["i"], Tpv, TF["r"]], writes=[TF["i"]])
                    for g in range(SEG // 512):
                        gs = slice(g * 512, (g + 1) * 512)
                        ts_ = slice(t0 + g * 512, t0 + (g + 1) * 512)
                        P.op("pe", mm(pz[:, :], ones_f, F["i"][:, gs]), reads=[self.Tc, TF["i"]], writes=[Tpz])
                        P.op("dve", lambda e: e.tensor_tensor(out=F["m"][:, gs], in0=pz[:, :], in1=F["v"][:, gs], op=ALU.mult), reads=[Tpz, TF["v"]], writes=[TF["m"]])
                    yield
                    P.op("pool", lambda e: e.tensor_tensor(out=F["t"][:], in0=F["t"][:], in1=F["m"][:], op=ALU.add), reads=[TF["t"], TF["m"]], writes=[TF["t"]])
                    for g in range(SEG // 512):
                        gs = slice(g * 512, (g + 1) * 512)
                        ts_ = slice(t0 + g * 512, t0 + (g + 1) * 512)
                        P.op("pe", mm(pz2[:, :], gupb[:, hc], sg[:, ts_]), reads=[Tlw, Ttw], writes=[Tpz2])
                        P.op("dve", lambda e: e.tensor_tensor(out=yH[:, gs], in0=pz2[:, :], in1=F["t"][:, gs], op=ALU.mult), reads=[Tpz2, TF["t"]], writes=[TyH])
                    P.dma("sp", [(self.ya[s, hp * 128:(hp + 1) * 128, t0:t0 + SEG], yH[:])], reads=[TyH], writes=[self.tdr("ya", s)])
                    yield

            def chain(gens):
                for g in gens:
                    for _ in g:
                        yield

            def step(gen, n=1):
                if gen is None:
                    return False
                for _ in range(n):
                    try:
                        next(gen)
                    except StopIteration:
                        return False
                return True

            iters = [(hp, sgi) for hp in range(4) for sgi in range(NSEG)]

            def bg_for(it):
                gens = []
                hp, sgi = iters[it]
                if sgi == 0 and hp > 0:
                    gens.append(post(hp - 1))
                if it + 1 < len(iters):
                    hp2, sg2 = iters[it + 1]
                    segs2 = (sg2, NSEG - 1 - sg2)
                    for d in range(2):
                        gens.append(prep(hp2, d, segs2[d], (it + 1) % 2))
                return chain(gens)

            for d in range(2):
                step(prep(0, d, (0, NSEG - 1)[d], 0), 10 ** 6)
            for it, (hp, sgi) in enumerate(iters):
                hb = it % 2
                segs = (sgi, NSEG - 1 - sgi)
                bg = bg_for(it)
                bg_alive = True
                if BG_MODE == 0:
                    step(bg, 10 ** 6)
                    bg_alive = False
                units = []
                for ci in range(NCS):
                    units.append((0, ci))
                    units.append((1, NCS - 1 - ci))
                queue = list(units)
                active = [None] * NSLOT
                while queue or any(a is not None for a in active):
                    for k in range(NSLOT):
                        if active[k] is None and queue:
                            u = queue.pop(0)
                            active[k] = ph1(u[0], u[1], k, hb)
                        if active[k] is not None and not step(active[k]):
                            active[k] = None
                    if bg_alive:
                        bg_alive = step(bg)
                if sgi == 0:
                    for d in range(2):
                        P.op("pool", lambda e: e.memset(Sf[d][:], 0.0), writes=[TS[d]])
                        P.op("pool", lambda e: e.memset(Sb[d][:], 0.0), writes=[TSb[d]])
                        P.op("pool", lambda e: e.memset(S0D[d][:], 0.0), writes=[TS0[d]])
                for ci in range(NCS):
                    cls = (ci, NCS - 1 - ci)
                    gens = [ph2(d, segs[d], cls[d], hb) for d in range(2)]
                    alive = [True, True]
                    while any(alive):
                        for d in range(2):
                            if alive[d]:
                                alive[d] = step(gens[d])
                        if bg_alive and BG_MODE >= 2:
                            bg_alive = step(bg)
                if bg_alive:
                    step(bg, 10 ** 6)
            step(post(3), 10 ** 6)


    def stage_N(self, l, s):
        P, nc, S, R = self.P, self.nc, self.S, self.R
        plan, variants = na_plan(R)
        NB = R // 8
        nvar = max(len(variants), 1)
        P.barrier()
        with ExitStack() as es:
            qT = self.sb(es, "qT", [128, S], BF16)
            kz = [self.sb(es, "kz%d" % h, [128, S], BF16) for h in range(2)]
            V = self.sb(es, "Vn", [128, self.NT, 128], BF16)
            Tq = T()
            tab = self.sb(es, "tab", [128, 2, 2, NE * 64], F32)
            Ttab = T()
            rmk = self.sb(es, "rmk", [128, nvar, 8, 512], F32)
            Trm = T()
            onesb = self.sb(es, "onesb", [128, 64], BF16)
            Tones = T()
            NE1, NEX, NSC, LA = 5, 5, 4, 3
            e1 = [self.sb(es, "e1_%d" % i, [128, 512], F32) for i in range(NE1)]
            Te1 = [T() for _ in range(NE1)]
            ex = [self.sb(es, "ex_%d" % i, [128, 512], BF16) for i in range(NEX)]
            Tex = [T() for _ in range(NEX)]
            rec = self.sb(es, "rec", [128, 512], F32)
            Trec = T()
            ybT = self.sb(es, "ybT", [128, S], BF16)
            TybT = T()
            sc = [self.ps(es, "sc%d" % i, [128, 512]) for i in range(NSC)]
            Tsc = [T() for _ in range(NSC)]
            NUM = [self.ps(es, "NUM%d" % i, [128, 512]) for i in range(2)]
            DEN = [self.ps(es, "DEN%d" % i, [128, 512]) for i in range(2)]
            TN, TDn = [T(), T()], [T(), T()]
            P.op("pool", lambda e: e.memset(onesb[:], 1.0), writes=[Tones])
            for h in range(2):
                P.op("pool", lambda e: e.memset(kz[h][:], 0.0), writes=[Tq])
            if len(variants) > 0:
                P.dma("sp", [(rmk[:, v], self.rmask[v]) for v in range(len(variants))], writes=[Trm])
            vsrc = self.vna[s].rearrange("(n p) c -> p n c", p=128)
            si = 0
            ei = 0
            for hp in range(4):
                pairs = [(qT[:], self.qk[s, hp * 128:(hp + 1) * 128, :]),
                         (V[:], vsrc[:, :, hp * 128:(hp + 1) * 128])]
                for h in range(2):
                    pairs.append((kz[h][h * 64:(h + 1) * 64, :], self.qk[s, CW + hp * 128 + h * 64:CW + hp * 128 + (h + 1) * 64, :]))
                P.dma("sp", pairs, reads=[self.tdr("qk", s), self.tdr("vna", s)], writes=[Tq])
                tp = []
                for h in range(2):
                    for tt in range(2):
                        for r2 in range(2):
                            tp.append((tab[r2 * 64:(r2 + 1) * 64, h, tt, :].rearrange("p (m c) -> p m c", c=64),
                                       self.btab[l, 2 * hp + h, tt, :, (1 - r2):(1 - r2) + NE, :]))
                P.dma("sp", tp, writes=[Ttab])
                units = []
                for b in range(NB):
                    p0, npair, off, var = plan[b]
                    for h in range(2):
                        for j in range(npair):
                            units.append((b, h, j))

                def sc_part(u, idx):
                    b, h, j = u
                    p0, npair, off, var = plan[b]
                    qs = slice(b * 512, (b + 1) * 512)
                    tt = 0 if var is None else 1
                    keys = slice((p0 + j) * 128, (p0 + j + 1) * 128)
                    scb, Tscb = sc[idx % NSC], Tsc[idx % NSC]
                    P.op("pe", mm(scb[:, :], kz[h][:, keys], qT[:, qs]), reads=[Tq], writes=[Tscb])
                    m0 = 14 - off - 2 * j
                    e_, Te_ = e1[idx % NE1], Te1[idx % NE1]
                    x_, Tx_ = ex[idx % NEX], Tex[idx % NEX]
                    P.op("dve", lambda e: e.tensor_tensor(out=e_[:], in0=scb[:, :], in1=tab[:, h, tt, m0 * 64:(m0 + 8) * 64], op=ALU.add), reads=[Tscb, Ttab], writes=[Te_])
                    if var is not None:
                        P.op("pool", lambda e: e.tensor_tensor(out=e_[:], in0=e_[:], in1=rmk[:, var, j, :], op=ALU.add), reads=[Te_, Trm], writes=[Te_])
                    P.op("act", lambda e: e.activation(out=x_[:], in_=e_[:], func=AF.Exp), reads=[Te_], writes=[Tx_])

                def pv_part(u, idx):
                    b, h, j = u
                    p0, npair, off, var = plan[b]
                    qs = slice(b * 512, (b + 1) * 512)
                    ph = slice(h * 64, (h + 1) * 64)
                    x_, Tx_ = ex[idx % NEX], Tex[idx % NEX]
                    nb_, db_ = NUM[b % 2], DEN[b % 2]
                    P.op("pe", mm(nb_[ph, :], V[:, p0 + j, h * 64:(h + 1) * 64], x_[:], j == 0, j == npair - 1), reads=[Tq, Tx_], writes=[TN[b % 2]])
                    P.op("pe", mm(db_[ph, :], onesb[:], x_[:], j == 0, j == npair - 1), reads=[Tones, Tx_], writes=[TDn[b % 2]])
                    if h == 1 and j == npair - 1:
                        P.op("dve", lambda e: e.reciprocal(out=rec[:], in_=db_[:, :]), reads=[TDn[b % 2]], writes=[Trec])
                        P.op("dve", lambda e: e.tensor_tensor(out=ybT[:, qs], in0=nb_[:, :], in1=rec[:], op=ALU.mult), reads=[TN[b % 2], Trec], writes=[TybT])

                nu = len(units)
                for i in range(nu + LA):
                    if i < nu:
                        sc_part(units[i], i)
                    if i >= LA:
                        pv_part(units[i - LA], i - LA)
                P.dma("sp", [(self.yb[s, hp * 128:(hp + 1) * 128, :], ybT[:])], reads=[TybT], writes=[self.tdr("yb", s)])

    def stage_M(self, l):
        P, nc, S = self.P, self.nc, self.S
        P.barrier()
        with ExitStack() as es:
            WA = self.sb(es, "WA", [128, 4, D], BF16)
            WB = self.sb(es, "WB", [128, 4, D], BF16)
            WO = self.sb(es, "WO", [128, 8, D], BF16)
            Tw = T()
            P.dma("pool", [(WA[:], self.w_a[l].rearrange("(kc p) c -> p kc c", p=128)),
                           (WB[:], self.w_b[l].rearrange("(kc p) c -> p kc c", p=128))], writes=[Tw])
            wo_src = self.w_out[l].rearrange("(kc p) c -> p kc c", p=128)
            P.dma("pool", [(WO[:, 0:4], wo_src[:, 0:4]), (WO[:, 4:8], wo_src[:, 4:8])], writes=[Tw])
            lng = self.sb(es, "lngM", [128, D], F32)
            lnb = self.sb(es, "lnbM", [128, D], F32)
            Tgb = T()
            P.dma("sp", [(lng[:], self.lnp[2 + 4 * l]), (lnb[:], self.lnp[3 + 4 * l])], writes=[Tgb])
            yaT = [self.sb(es, "yaT%d" % i, [128, 4, 512], BF16) for i in range(2)]
            ybT = [self.sb(es, "ybTm%d" % i, [128, 4, 512], BF16) for i in range(2)]
            gab = [self.sb(es, "gab%d" % i, [128, 16, 512], BF16) for i in range(2)]
            Tin = [T(), T()]
            mT = self.sb(es, "mT", [128, 8, 512], BF16)
            TmT = T()
            m1 = [self.sb(es, "m1_%d" % i, [128, 512], F32) for i in range(2)]
            m2 = [self.sb(es, "m2_%d" % i, [128, 512], F32) for i in range(2)]
            Tm1 = [T(), T()]
            Tm2 = [T(), T()]
            xt = [self.sb(es, "xtM%d" % i, [128, D], F32) for i in range(2)]
            Txt = [T(), T()]
            st = self.sb(es, "stM", [128, 2, 6], F32)
            mv = self.sb(es, "mvM", [128, 4], F32)
            Tst = T()
            pa = [self.ps(es, "pa%d" % i, [128, 512]) for i in range(2)]
            pb = [self.ps(es, "pbm%d" % i, [128, 512]) for i in range(2)]
            po = [self.ps(es, "po%d" % i, [128, 512]) for i in range(4)]
            Tpa, Tpb, Tpo = [T(), T()], [T(), T()], [T() for _ in range(4)]
            gi = 0
            ti = 0
            for s in range(self.NS):
                ya_src = self.ya[s].rearrange("(c p) t -> p c t", p=128)
                yb_src = self.yb[s].rearrange("(c p) t -> p c t", p=128)
                gt_src = self.gt[s].rearrange("(c p) t -> p c t", p=128)
                for g in range(self.NG):
                    ts = slice(g * 512, (g + 1) * 512)
                    bsel = gi % 2
                    gi += 1
                    P.dma("sp", [(yaT[bsel][:], ya_src[:, :, ts]), (ybT[bsel][:], yb_src[:, :, ts]), (gab[bsel][:], gt_src[:, :, ts])],
                          reads=[self.tdr("ya", s), self.tdr("yb", s), self.tdr("gt", s)], writes=[Tin[bsel]])
                    for j in range(8):
                        js = slice(j * 128, (j + 1) * 128)
                        a_, Ta_ = pa[j % 2], Tpa[j % 2]
                        b_, Tb_ = pb[j % 2], Tpb[j % 2]
                        for kc in range(4):
                            P.op("pe", mm(a_[:, :], WA[:, kc, js], yaT[bsel][:, kc, :], kc == 0, kc == 3), reads=[Tw, Tin[bsel]], writes=[Ta_])
                        for kc in range(4):
                            P.op("pe", mm(b_[:, :], WB[:, kc, js], ybT[bsel][:, kc, :], kc == 0, kc == 3), reads=[Tw, Tin[bsel]], writes=[Tb_])
                        P.op("dve", lambda e: e.tensor_tensor(out=m1[j % 2][:], in0=a_[:, :], in1=gab[bsel][:, j, :], op=ALU.mult), reads=[Ta_, Tin[bsel]], writes=[Tm1[j % 2]])
                        P.op("dve", lambda e: e.tensor_tensor(out=m2[j % 2][:], in0=b_[:, :], in1=gab[bsel][:, 8 + j, :], op=ALU.mult), reads=[Tb_, Tin[bsel]], writes=[Tm2[j % 2]])
                        P.op("pool", lambda e: e.tensor_tensor(out=mT[:, j, :], in0=m1[j % 2][:], in1=m2[j % 2][:], op=ALU.add), reads=[Tm1[j % 2], Tm2[j % 2]], writes=[TmT])
                    for t in range(4):
                        tok = slice(t * 128, (t + 1) * 128)
                        r0 = s * S + g * 512 + t * 128
                        x_, Tx_ = xt[ti % 2], Txt[ti % 2]
                        P.dma("sp", [(x_[:], self.xres[r0:r0 + 128, :])], reads=[self.tdr("xres", s, g * 4 + t)], writes=[Tx_])
                        for n in range(2):
                            o_, To_ = po[(2 * ti + n) % 4], Tpo[(2 * ti + n) % 4]
                            for kc in range(8):
                                P.op("pe", mm(o_[:, :], mT[:, kc, tok], WO[:, kc, n * 512:(n + 1) * 512], kc == 0, kc == 7), reads=[TmT, Tw], writes=[To_])
                            P.op("dve", lambda e: e.scalar_tensor_tensor(out=x_[:, n * 512:(n + 1) * 512], in0=x_[:, n * 512:(n + 1) * 512], scalar=ALPHA, in1=o_[:, :], op0=ALU.mult, op1=ALU.add),
                                 reads=[Tx_, To_], writes=[Tx_])
                        ti += 1
                        self.layer_norm(P, x_, Tx_, lng[:], lnb[:], Tgb, st, mv, Tst, x_[:], Tx_)
                        P.dma("sp", [(self.xres[r0:r0 + 128, :], x_[:])], reads=[Tx_], writes=[self.tdr("xres", s, g * 4 + t)])

    def stage_F(self, l, last):
        P, nc, S = self.P, self.nc, self.S
        P.barrier()
        G = 256
        NFC = DFF // 128
        with ExitStack() as es:
            W1 = self.sb(es, "W1", [128, 8, 2 * DFF], BF16)
            W2 = self.sb(es, "W2", [128, NFC, D], BF16)
            Tw1 = [T() for _ in range(8)]
            Tw2 = T()
            w1s = self.w_f1[l].rearrange("(kc p) c -> p kc c", p=128)
            for kc in range(8):
                P.dma("pool", [(W1[:, kc, :], w1s[:, kc, :])], writes=[Tw1[kc]])
            w2s = self.w_f2[l].rearrange("(fc p) c -> p fc c", p=128)
            for a in range(0, NFC, 6):
                bnd = min(a + 6, NFC)
                P.dma("pool", [(W2[:, a:bnd, :], w2s[:, a:bnd, :])], writes=[Tw2])
            lng = self.sb(es, "lngF", [128, D], F32)
            lnb = self.sb(es, "lnbF", [128, D], F32)
            Tgb = T()
            P.dma("sp", [(lng[:], self.lnp[4 + 4 * l]), (lnb[:], self.lnp[5 + 4 * l])], writes=[Tgb])
            xt = [self.sb(es, "xtF%d" % i, [128, D], F32) for i in range(2)]
            Txt = [T(), T()]
            x1T = self.sb(es, "x1T", [128, 8, G], BF16)
            Tx1T = T()
            hT = self.sb(es, "hT", [128, NFC, G], BF16)
            ThT = T()
            sgl = [self.sb(es, "sgl%d" % i, [128, G], F32) for i in range(2)]
            Tsg = [T(), T()]
            st = self.sb(es, "stF", [128, 2, 6], F32)
            mv = self.sb(es, "mvF", [128, 4], F32)
            Tst = T()
            pt = [self.ps(es, "pt%d" % i, [128, 512]) for i in range(2)]
            pg = [self.ps(es, "pg%d" % i, [128, 512]) for i in range(2)]
            pu = [self.ps(es, "pu%d" % i, [128, 512]) for i in range(2)]
            po = [self.ps(es, "pof%d" % i, [128, 512]) for i in range(2)]
            Tpt, Tpg, Tpu, Tpo = [T(), T()], [T(), T()], [T(), T()], [T(), T()]
            dst = self.y if last else self.xres
            for s in range(self.NS):
                for g in range(S // G):
                    for t in range(G // 128):
                        r0 = s * S + g * G + t * 128
                        P.dma("sp", [(xt[t][:], self.xres[r0:r0 + 128, :])], reads=[self.tdr("xres", s, g * 2 + t)], writes=[Txt[t]])
                        for half in range(2):
                            for q in range(4):
                                kc = half * 4 + q
                                P.op("pe", lambda e: e.transpose(pt[half][:, q * 128:(q + 1) * 128], xt[t][:, kc * 128:(kc + 1) * 128], self.identf),
                                     reads=[Txt[t], self.Tc], writes=[Tpt[half]])
                            o_ap = x1T[:, half * 4:half * 4 + 4, t * 128:(t + 1) * 128]
                            i_ap = pt[half][:].rearrange("p (a b) -> p a b", a=4)
                            if half == 0:
                                P.op("act", lambda e: e.activation(out=o_ap, in_=i_ap, func=AF.Copy), reads=[Tpt[half]], writes=[Tx1T])
                            else:
                                P.op("dve", lambda e: e.tensor_copy(out=o_ap, in_=i_ap), reads=[Tpt[half]], writes=[Tx1T])
                    for f in range(NFC):
                        g_, Tg_ = pg[f % 2], Tpg[f % 2]
                        u_, Tu_ = pu[f % 2], Tpu[f % 2]
                        for kc in range(8):
                            P.op("pe", mm(g_[:, 0:G], W1[:, kc, f * 128:(f + 1) * 128], x1T[:, kc, :], kc == 0, kc == 7), reads=[Tw1[kc], Tx1T], writes=[Tg_])
                        for kc in range(8):
                            P.op("pe", mm(u_[:, 0:G], W1[:, kc, DFF + f * 128:DFF + (f + 1) * 128], x1T[:, kc, :], kc == 0, kc == 7), reads=[Tw1[kc], Tx1T], writes=[Tu_])
                        P.op("act", lambda e: e.activation(out=sgl[f % 2][:], in_=g_[:, 0:G], func=AF.Silu), reads=[Tg_], writes=[Tsg[f % 2]])
                        P.op("dve", lambda e: e.tensor_tensor(out=hT[:, f, :], in0=u_[:, 0:G], in1=sgl[f % 2][:], op=ALU.mult), reads=[Tu_, Tsg[f % 2]], writes=[ThT])
                    for t in range(G // 128):
                        r0 = s * S + g * G + t * 128
                        tok = slice(t * 128, (t + 1) * 128)
                        for n in range(2):
                            o_, To_ = po[n], Tpo[n]
                            for fc in range(NFC):
                                P.op("pe", mm(o_[:, :], hT[:, fc, tok], W2[:, fc, n * 512:(n + 1) * 512], fc == 0, fc == NFC - 1), reads=[ThT, Tw2], writes=[To_])
                            P.op("dve", lambda e: e.scalar_tensor_tensor(out=xt[t][:, n * 512:(n + 1) * 512], in0=xt[t][:, n * 512:(n + 1) * 512], scalar=ALPHA, in1=o_[:, :], op0=ALU.mult, op1=ALU.add),
                                 reads=[Txt[t], To_], writes=[Txt[t]])
                        self.layer_norm(P, xt[t], Txt[t], lng[:], lnb[:], Tgb, st, mv, Tst, xt[t][:], Txt[t])
                        P.dma("sp", [(dst[r0:r0 + 128, :], xt[t][:])], reads=[Txt[t]], writes=[self.tdr("xres" if not last else "y", s, g * 2 + t)])


PV_MU0, PV_MU1 = 0, 14
PV_W0 = 28
PV_A0 = 36
PV_KK = 44
PV_KA = 48
PV_RK = 52
PV_GG = 56
PV_GB = 60
NPV = 64
C_ID = 0
C_ONES = 128
C_EPS = 256
C_MASK = 260
NCST = C_MASK + 4 * 128
CB_ID, CB_ID2, CB_ONES, CB_SMASK = 0, 128, 384, 448
NCB = 448


def na_plan(R):
    npair = min(8, R // 2)
    plan, variants = [], []
    kh = min(8, R)
    for b in range(R // 8):
        i0 = 8 * b
        mid = (i0 - 4 >= 0) and (i0 + 11 <= R) and kh == 8
        p0 = min(max(4 * b - 2, 0), R // 2 - npair)
        off = 2 * p0 - i0
        var = None
        if not mid:
            m = np.full((npair, 2, 8), NEG, np.float32)
            for j in range(npair):
                for r2 in range(2):
                    kr = 2 * (p0 + j) + r2
                    for qr in range(8):
                        i = i0 + qr
                        rs = min(max(i - kh // 2, 0), R - kh)
                        if rs <= kr < rs + kh:
                            m[j, r2, qr] = 0.0
            key = m.tobytes()
            for vi, (k2, _) in enumerate(variants):
                if k2 == key:
                    var = vi
                    break
            else:
                variants.append((key, m))
                var = len(variants) - 1
        plan.append((p0, npair, off, var))
    return plan, [m for (_, m) in variants]


def host_consts(R):
    cst = np.zeros((128, NCST), np.float32)
    cst[:, C_ID:C_ID + 128] = np.eye(128, dtype=np.float32)
    od = np.zeros((128, 128), np.float32)
    od[0:64, 0:64] = 1.0
    od[64:128, 64:128] = 1.0
    cst[:, C_ONES:C_ONES + 128] = od
    cst[:, C_EPS] = LN_EPS
    cst[:, C_EPS + 1] = GN_EPS
    cst[:, C_EPS + 2] = 1e-30
    i = np.arange(128)
    cst[:, C_MASK + 0:C_MASK + 128] = (i[:, None] > i[None, :])
    cst[:, C_MASK + 128:C_MASK + 256] = (i[:, None] < i[None, :])
    cst[:, C_MASK + 256:C_MASK + 384] = (i[:, None] >= i[None, :])
    cst[:, C_MASK + 384:C_MASK + 512] = (i[:, None] <= i[None, :])
    plan, variants = na_plan(R)
    rm = np.zeros((max(len(variants), 1), 128, 8, 512), np.float32)
    for vi, m in enumerate(variants):
        for j in range(m.shape[0]):
            for r2 in range(2):
                rm[vi, r2 * 64:(r2 + 1) * 64, j, :] = np.repeat(m[j, r2], 64)[None, :]
    return cst, rm


def host_layer_params(inp, L):
    f = lambda a: np.asarray(a, np.float32)
    pvec = np.zeros((L, 128, NPV), np.float32)
    lora = np.zeros((L, 128, 4, CW), np.float32)
    btab = np.full((L, 8, 2, 64, NE + 1, 64), NEG, np.float32)
    lnp = np.zeros((2 + 4 * L, 128, D), np.float32)
    lnp[0] = f(inp["ln_in_g"])[None, :]
    lnp[1] = f(inp["ln_in_b"])[None, :]
    c = np.arange(64)
    qc = np.arange(64)
    cs = np.clip(qc - 8, 0, GRID_W - 16)
    colvalid = (c[:, None] >= cs[None, :]) & (c[:, None] < cs[None, :] + 16)
    dc = c[:, None] - qc[None, :] + 15
    dcc = np.clip(dc, 0, 30)
    for l in range(L):
        mu = f(inp["shift_mu"][l])
        pad = np.zeros((2, 14 * 128), np.float32)
        pad[:, :RWC] = mu
        pvec[l, :, PV_MU0:PV_MU0 + 14] = pad[0].reshape(14, 128).T
        pvec[l, :, PV_MU1:PV_MU1 + 14] = pad[1].reshape(14, 128).T
        for d in range(2):
            pvec[l, :, PV_W0 + 4 * d:PV_W0 + 4 * d + 4] = f(inp["decay_w0"][l, d]).reshape(4, 128).T
            pvec[l, :, PV_A0 + 4 * d:PV_A0 + 4 * d + 4] = f(inp["iclr_a0"][l, d]).reshape(4, 128).T
            lora[l, 32 * d:32 * d + 32, d, :] = f(inp["decay_up"][l, d])
            lora[l, 64 + 32 * d:64 + 32 * d + 32, 2 + d, :] = f(inp["iclr_up"][l, d])
        pvec[l, :, PV_KK:PV_KK + 4] = f(inp["k_k"][l]).reshape(4, 128).T
        pvec[l, :, PV_KA:PV_KA + 4] = f(inp["k_a"][l]).reshape(4, 128).T
        pvec[l, :, PV_RK:PV_RK + 4] = f(inp["r_k"][l]).reshape(4, 128).T
        pvec[l, :, PV_GG:PV_GG + 4] = f(inp["gn_g"][l]).reshape(4, 128).T
        pvec[l, :, PV_GB:PV_GB + 4] = f(inp["gn_b"][l]).reshape(4, 128).T
        lnp[2 + 4 * l + 0] = f(inp["ln1_g"][l])[None, :]
        lnp[2 + 4 * l + 1] = f(inp["ln1_b"][l])[None, :]
        lnp[2 + 4 * l + 2] = f(inp["ln2_g"][l])[None, :]
        lnp[2 + 4 * l + 3] = f(inp["ln2_b"][l])[None, :]
        rpb = f(inp["na_rpb"][l])
        for mp in range(NE + 1):
            delta = 15 - mp
            dr = delta + 7
            if 0 <= dr <= 14:
                vals = np.where(colvalid[None], rpb[:, dr][:, dcc], NEG)
                btab[l, :, 1, :, mp, :] = vals
                if 3 <= dr <= 10:
                    btab[l, :, 0, :, mp, :] = vals
    return pvec, lora, lnp, btab


def kernel(**inputs):
    L, NS, S, NCORE = 4, 2, 4096, 8
    b = Builder(L, NS, S)
    nc = b.build()
    pvec, lora, lnp, btab = host_layer_params(inputs, L)
    cst, rm = host_consts(S // GRID_W)
    x = np.asarray(inputs["x"], np.float32)
    f = lambda a: np.ascontiguousarray(np.asarray(a, np.float32))
    shared = {"w_in": f(inputs["w_in"]), "w_a": f(inputs["w_branch_rwkv"]), "w_b": f(inputs["w_branch_na"]),
              "w_out": f(inputs["w_out"]), "w_f1": f(inputs["w_ffn_in"]), "w_f2": f(inputs["w_ffn_out"]),
              "pvec": pvec, "lora": lora, "gup": f(inputs["gate_up"]), "lnp": lnp, "btab": btab, "cst": cst, "rmask": rm}
    in_maps = []
    for c in range(NCORE):
        m = dict(shared)
        m["x"] = np.ascontiguousarray(x[c * NS:(c + 1) * NS].reshape(NS * S, D))
        in_maps.append(m)
    res = run_bass_kernel_spmd(nc, in_maps, core_ids=list(range(NCORE)))
    out = np.stack([np.asarray(r["y"]).reshape(NS, S, D) for r in res.results], axis=0)
    return out.reshape(NCORE * NS, S, D).astype(np.float32)
```

```python
import math
import os
import numpy as np
from contextlib import ExitStack
import concourse.bass as bass
import concourse.mybir as mybir
from concourse.bass_utils import run_bass_kernel_spmd

F32 = mybir.dt.float32
BF16 = mybir.dt.bfloat16
AF = mybir.ActivationFunctionType
ALU = mybir.AluOpType
AX = mybir.AxisListType

D = 1024
CW = 512
HD = 64
DFF = 2816
RWC = 1760
INC = 5344
NAQ0 = 1760
GT0 = 1760 + 1536
GRID_W = 64
DEPTH_FULL = 4
ALPHA = (2.0 * DEPTH_FULL) ** 0.25
LN_EPS = 1e-5
GN_EPS = 64e-5
KAPPA = math.exp(-0.5)
NEG = -30000.0
CH = 128
NE = 31
BG_MODE = 1


class T:
    __slots__ = ("w", "r")

    def __init__(self):
        self.w = None
        self.r = {}


class Prog:
    NDMA = 20

    def __init__(self, nc, es, same_engine_sync=True):
        self.nc = nc
        self.same = same_engine_sync
        self.eng = {"pe": nc.tensor, "act": nc.scalar, "dve": nc.vector,
                    "pool": nc.gpsimd, "sp": nc.sync}
        self.sem = {k: es.enter_context(nc.semaphore("s_" + k)) for k in self.eng}
        self.cnt = {k: 0 for k in self.eng}
        self.waited = {k: {} for k in self.eng}
        self.dsem, self.dcnt, self.drr = {}, {}, {}
        for q in ("sp", "pool"):
            self.drr[q] = 0
            for i in range(self.NDMA):
                k = ("d", q, i)
                self.dsem[k] = es.enter_context(nc.semaphore("d_%s%d" % (q, i)))
                self.dcnt[k] = 0
        self.n_ins = 0
        self.last_rg = None

    def semof(self, k):
        return self.dsem[k] if isinstance(k, tuple) else self.sem[k]

    def _wait(self, X, k, v):
        if self.waited[X].get(k, 0) < v:
            self.eng[X].wait_ge(self.semof(k), v)
            self.waited[X][k] = v
            self.n_ins += 1

    def _deps(self, X, reads, writes):
        deps = {}
        for t in reads:
            if t.w is not None and t.w[1] > deps.get(t.w[0], 0):
                deps[t.w[0]] = t.w[1]
        for t in writes:
            if t.w is not None and t.w[1] > deps.get(t.w[0], 0):
                deps[t.w[0]] = t.w[1]
            for k, v in t.r.items():
                if v > deps.get(k, 0):
                    deps[k] = v
        need = []
        for k, v in deps.items():
            if k == X and (X == "pe" or not self.same):
                continue
            if self.waited[X].get(k, 0) < v:
                need.append((k, v))
        for k, v in need[1:]:
            self._wait(X, k, v)
        return need[0] if need else None

    def op(self, X, fn, reads=(), writes=(), rg=None):
        if X == "pe":
            if rg != self.last_rg and self.cnt["pe"] > 0:
                self._wait("pe", "pe", self.cnt["pe"])
            self.last_rg = rg
        emb = self._deps(X, reads, writes)
        ins = fn(self.eng[X])
        if emb is not None:
            ins._wait_ge(self.semof(emb[0]), emb[1])
            self.waited[X][emb[0]] = emb[1]
        self.cnt[X] += 1
        c = self.cnt[X]
        ins.then_inc(self.sem[X], 1)
        self.n_ins += 1
        for t in reads:
            if t.r.get(X, 0) < c:
                t.r[X] = c
        for t in writes:
            t.w = (X, c)
            t.r = {}
        return ins

    def dma(self, Q, pairs, reads=(), writes=()):
        emb = self._deps(Q, reads, writes)
        if emb is not None:
            self._wait(Q, emb[0], emb[1])
        i = self.drr[Q]
        self.drr[Q] = (i + 1) % self.NDMA
        k = ("d", Q, i)
        self._wait(Q, k, self.dcnt[k])
        for (o, a) in pairs:
            self.eng[Q].dma_start(out=o, in_=a).then_inc(self.dsem[k], 16)
            self.dcnt[k] += 16
            self.n_ins += 1
        c = self.dcnt[k]
        for t in reads:
            t.r[k] = c
        for t in writes:
            t.w = (k, c)
            t.r = {}

    def barrier(self):
        for X in self.eng:
            for k in self.eng:
                if k != X and self.cnt[k] > 0:
                    self._wait(X, k, self.cnt[k])
            for k, v in self.dcnt.items():
                if v > 0:
                    self._wait(X, k, v)


def mm(out, lhsT, rhs, start=True, stop=True):
    return lambda e: e.matmul(out, lhsT=lhsT, rhs=rhs, start=start, stop=stop)


class Builder:
    def __init__(self, L, NS, S, dbg=False):
        self.L, self.NS, self.S, self.dbg = L, NS, S, dbg
        self.R = S // GRID_W
        self.NT = S // 128
        self.NG = S // 512
        nc = self.nc = bass.Bass("TRN2", target_bir_lowering=False)
        self.es = ExitStack()
        dt = lambda n, s, d, kind="ExternalInput": nc.dram_tensor(n, s, d, kind=kind).ap()
        sk = "ExternalOutput" if dbg else "Internal"
        NTOK = NS * S
        self.x_in = dt("x", [NTOK, D], F32)
        self.w_in = dt("w_in", [L, D, INC], F32)
        self.w_a = dt("w_a", [L, CW, D], F32)
        self.w_b = dt("w_b", [L, CW, D], F32)
        self.w_out = dt("w_out", [L, D, D], F32)
        self.w_f1 = dt("w_f1", [L, D, 2 * DFF], F32)
        self.w_f2 = dt("w_f2", [L, DFF, D], F32)
        self.pvec = dt("pvec", [L, 128, NPV], F32)
        self.lora = dt("lora", [L, 128, 4, CW], F32)
        self.gup = dt("gup", [L, 96, CW], F32)
        self.lnp = dt("lnp", [2 + 4 * L, 128, D], F32)
        self.btab = dt("btab", [L, 8, 2, 64, NE + 1, 64], F32)
        self.cst = dt("cst", [128, NCST], F32)
        self.nvar = len(na_plan(self.R)[1])
        self.rmask = dt("rmask", [max(self.nvar, 1), 128, 8, 512], F32)
        self.y = dt("y", [NTOK, D], F32, kind="ExternalOutput")
        self.xres = dt("xres", [NTOK, D], F32, kind=sk)
        self.us = dt("us", [NS, 14 * 128, S], F32, kind=sk)
        self.qk = dt("qk", [NS, 1024, S], BF16, kind=sk)
        self.vna = dt("vna", [NS, S, CW], BF16, kind=sk)
        self.gt = dt("gt", [NS, 2048, S], BF16, kind=sk)
        self.ya = dt("ya", [NS, CW, S], BF16, kind=sk)
        self.yb = dt("yb", [NS, CW, S], BF16, kind=sk)
        self.Tdr = {}
        self.uid = 0

    def tdr(self, *key):
        if key not in self.Tdr:
            self.Tdr[key] = T()
        return self.Tdr[key]

    def sb(self, es, name, shape, dt):
        self.uid += 1
        return es.enter_context(self.nc.sbuf_tensor("%s_%d" % (name, self.uid), shape, dt))

    def ps(self, es, name, shape, dt=F32):
        self.uid += 1
        return es.enter_context(self.nc.psum_tensor("%s_%d" % (name, self.uid), shape, dt))

    def layer_norm(self, P, xt, Tx, g, b, Tgb, st, mv, Tst, out, Tout):
        P.op("dve", lambda e: e.bn_stats(out=st[:, 0, :], in_=xt[:, 0:512]), reads=[Tx], writes=[Tst])
        P.op("dve", lambda e: e.bn_stats(out=st[:, 1, :], in_=xt[:, 512:1024]), reads=[Tx], writes=[Tst])
        P.op("dve", lambda e: e.bn_aggr(out=mv[:, 0:2], in_=st[:].rearrange("p a b -> p (a b)")), reads=[Tst], writes=[Tst])
        P.op("act", lambda e: e.activation(out=mv[:, 2:3], in_=mv[:, 1:2], func=AF.Sqrt, bias=self.c_eps_ln, scale=1.0), reads=[Tst], writes=[Tst])
        P.op("dve", lambda e: e.reciprocal(out=mv[:, 3:4], in_=mv[:, 2:3]), reads=[Tst], writes=[Tst])
        P.op("dve", lambda e: e.tensor_scalar(out=xt[:], in0=xt[:], scalar1=mv[:, 0:1], scalar2=mv[:, 3:4], op0=ALU.subtract, op1=ALU.mult), reads=[Tx, Tst], writes=[Tx])
        P.op("pool", lambda e: e.tensor_tensor(out=xt[:], in0=xt[:], in1=g, op=ALU.mult), reads=[Tx, Tgb], writes=[Tx])
        P.op("pool", lambda e: e.tensor_tensor(out=out, in0=xt[:], in1=b, op=ALU.add), reads=[Tx, Tgb], writes=[Tout])

    def build(self, stages="APRNMF"):
        nc = self.nc
        with self.es as es:
            P = self.P = Prog(nc, es)
            self.cstt = self.sb(es, "cstt", [128, NCST], F32)
            self.Tc = T()
            P.dma("sp", [(self.cstt[:], self.cst[:, :])], writes=[self.Tc])
            c = self.cstt
            self.identf = c[:, C_ID:C_ID + 128]
            self.c_eps_ln = c[:, C_EPS:C_EPS + 1]
            self.c_eps_gn = c[:, C_EPS + 1:C_EPS + 2]
            self.c_eps_kk = c[:, C_EPS + 2:C_EPS + 3]
            self.cb = self.sb(es, "cstb", [128, NCB], BF16)
            self.Tcb = T()
            P.op("dve", lambda e: e.tensor_copy(out=self.cb[:, CB_ID:CB_ID + 128], in_=c[:, C_ID:C_ID + 128]), reads=[self.Tc], writes=[self.Tcb])
            P.op("dve", lambda e: e.tensor_copy(out=self.cb[:, CB_ID2:CB_ID2 + 128], in_=c[:, C_ID:C_ID + 128]), reads=[self.Tc], writes=[self.Tcb])
            P.op("dve", lambda e: e.tensor_copy(out=self.cb[:, CB_ID2 + 128:CB_ID2 + 256], in_=c[:, C_ID:C_ID + 128]), reads=[self.Tc], writes=[self.Tcb])
            P.op("dve", lambda e: e.tensor_copy(out=self.cb[:, CB_ONES:CB_ONES + 64], in_=c[:, C_ONES:C_ONES + 64]), reads=[self.Tc], writes=[self.Tcb])
            for l in range(self.L):
                for s in range(self.NS):
                    if "A" in stages:
                        self.stage_AP(l, s, do_p=("P" in stages))
                    if "R" in stages:
                        self.stage_R(l, s)
                    if "N" in stages:
                        self.stage_N(l, s)
                if "M" in stages:
                    self.stage_M(l)
                if "F" in stages:
                    self.stage_F(l, last=(l == self.L - 1))
            P.barrier()
        return nc

    def stage_AP(self, l, s, do_p=True):
        P, nc, S = self.P, self.nc, self.S
        P.barrier()
        with ExitStack() as es:
            xT = self.sb(es, "xT", [128, 8, S], BF16)
            TxT = [T() for _ in range(self.NT)]
            NXT = 4
            xt = [self.sb(es, "xt%d" % i, [128, D], F32) for i in range(NXT)]
            Txt = [T() for _ in range(NXT)]
            xo = [self.sb(es, "xo%d" % i, [128, D], F32) for i in range(2)] if l == 0 else None
            Txo = [T(), T()]
            st = self.sb(es, "st", [128, 2, 6], F32)
            mv = self.sb(es, "mv", [128, 4], F32)
            Tst = T()
            pp = [self.ps(es, "pp%d" % i, [128, 512]) for i in range(8)]
            Tpp = [T() for _ in range(8)]
            if l == 0:
                lng = self.sb(es, "lng", [128, D], F32)
                lnb = self.sb(es, "lnb", [128, D], F32)
                Tgb = T()
                P.dma("sp", [(lng[:], self.lnp[0]), (lnb[:], self.lnp[1])], writes=[Tgb])
            src = self.x_in if l == 0 else self.xres
            for t in range(self.NT):
                r0 = s * S + t * 128
                b = t % NXT
                b2 = t % 2
                P.dma("sp", [(xt[b][:], src[r0:r0 + 128, :])], reads=[self.tdr("xres", s, t)], writes=[Txt[b]])
                if l == 0:
                    self.layer_norm(P, xt[b], Txt[b], lng[:], lnb[:], Tgb, st, mv, Tst, xo[b2][:], Txo[b2])
                    P.dma("sp", [(self.xres[r0:r0 + 128, :], xo[b2][:])], reads=[Txo[b2]], writes=[self.tdr("xres", s, t)])
                    xs, Txs = xo[b2], Txo[b2]
                else:
                    xs, Txs = xt[b], Txt[b]
                for half in range(2):
                    pb = pp[(2 * t + half) % 8]
                    Tpb = Tpp[(2 * t + half) % 8]
                    for q in range(4):
                        kc = half * 4 + q
                        P.op("pe", lambda e: e.transpose(pb[:, q * 128:(q + 1) * 128], xs[:, kc * 128:(kc + 1) * 128], self.identf),
                             reads=[Txs, self.Tc], writes=[Tpb])
                    eng = "act" if half == 0 else "dve"
                    o_ap = xT[:, half * 4:half * 4 + 4, t * 128:(t + 1) * 128]
                    i_ap = pb[:].rearrange("p (a b) -> p a b", a=4)
                    if eng == "act":
                        P.op("act", lambda e: e.activation(out=o_ap, in_=i_ap, func=AF.Copy), reads=[Tpb], writes=[TxT[t]])
                    else:
                        P.op("dve", lambda e: e.tensor_copy(out=o_ap, in_=i_ap), reads=[Tpb], writes=[TxT[t]])
            if not do_p:
                if self.dbg:
                    self.dbg_xT = (xT, TxT)
                return
            wb = [self.sb(es, "wb%d" % i, [128, 8, 128], BF16) for i in range(2)]
            Twb = [T(), T()]
            wv = self.sb(es, "wv", [128, 8, 512], BF16)
            Twv = T()
            ub = [self.sb(es, "ub%d" % i, [128, S + 2], F32) for i in range(2)]
            Tub = [T(), T()]
            ut1 = self.sb(es, "ut", [128, S], F32)
            Tut1 = T()
            ut = [ut1, ut1]
            Tut = [Tut1, Tut1]
            ob = [self.sb(es, "ob%d" % i, [128, S], BF16) for i in range(2)]
            Tob = [T(), T()]
            vb = [self.sb(es, "vb%d" % i, [128, 512], BF16) for i in range(2)]
            Tvb = [T(), T()]
            pv = self.sb(es, "pvA", [128, NPV], F32)
            Tpv = T()
            P.dma("sp", [(pv[:], self.pvec[l])], writes=[Tpv])
            c0 = self.sb(es, "c0", [128, 14], F32)
            P.op("dve", lambda e: e.tensor_tensor(out=c0[:], in0=pv[:, PV_MU0:PV_MU0 + 14], in1=pv[:, PV_MU1:PV_MU1 + 14], op=ALU.add), reads=[Tpv], writes=[Tpv])
            P.op("dve", lambda e: e.tensor_scalar(out=c0[:], in0=c0[:], scalar1=-1.0, scalar2=1.0, op0=ALU.mult, op1=ALU.add), reads=[Tpv], writes=[Tpv])
            for i in range(2):
                P.op("pool", lambda e: e.memset(ub[i][:, 0:1], 0.0), writes=[Tub[i]])
                P.op("pool", lambda e: e.memset(ub[i][:, S + 1:S + 2], 0.0), writes=[Tub[i]])
            w_l = self.w_in[l].rearrange("(kc p) c -> p kc c", p=128)
            blocks = [("u", j * 128, min(128, RWC - j * 128), j) for j in range(14)]
            blocks += [("q", NAQ0 + j * 128, 128, j) for j in range(4)]
            blocks += [("k", NAQ0 + CW + j * 128, 128, j) for j in range(4)]
            blocks += [("g", GT0 + j * 128, 128, j) for j in range(16)]
            pi = 0
            for bi, (kind, c0c, ncol, j) in enumerate(blocks):
                w = wb[bi % 2]
                Tw = Twb[bi % 2]
                P.dma("pool", [(w[:, :, 0:ncol], w_l[:, :, c0c:c0c + ncol])], writes=[Tw])
                if kind == "u":
                    dst, Tdst = ub[j % 2], Tub[j % 2]
                else:
                    dst, Tdst = ob[bi % 2], Tob[bi % 2]
                for g in range(self.NG):
                    pb, Tpb = pp[pi % 8], Tpp[pi % 8]
                    pi += 1
                    for kc in range(8):
                        P.op("pe", mm(pb[0:ncol, :], w[:, kc, 0:ncol], xT[:, kc, g * 512:(g + 1) * 512], kc == 0, kc == 7),
                             reads=[Tw] + TxT[g * 4:g * 4 + 4], writes=[Tpb])
                    if kind == "u":
                        P.op("act", lambda e: e.activation(out=dst[0:ncol, 1 + g * 512:1 + (g + 1) * 512], in_=pb[0:ncol, :], func=AF.Copy), reads=[Tpb], writes=[Tdst])
                    elif kind == "q":
                        P.op("act", lambda e: e.activation(out=dst[:, g * 512:(g + 1) * 512], in_=pb[:, :], func=AF.Copy, scale=0.125), reads=[Tpb], writes=[Tdst])
                    elif kind == "k":
                        P.op("dve", lambda e: e.tensor_copy(out=dst[:, g * 512:(g + 1) * 512], in_=pb[:, :]), reads=[Tpb], writes=[Tdst])
                    else:
                        P.op("act", lambda e: e.activation(out=dst[:, g * 512:(g + 1) * 512], in_=pb[:, :], func=AF.Sigmoid), reads=[Tpb], writes=[Tdst])
                if kind == "u":
                    u_, Tu_ = ut[j % 2], Tut[j % 2]
                    P.op("act", lambda e: e.activation(out=u_[0:ncol, :], in_=dst[0:ncol, 1:S + 1], func=AF.Copy, scale=c0[0:ncol, j:j + 1]), reads=[Tdst, Tpv], writes=[Tu_])
                    P.op("dve", lambda e: e.scalar_tensor_tensor(out=u_[0:ncol, :], in0=dst[0:ncol, 0:S], scalar=pv[0:ncol, PV_MU0 + j:PV_MU0 + j + 1], in1=u_[0:ncol, :], op0=ALU.mult, op1=ALU.add), reads=[Tdst, Tpv, Tu_], writes=[Tu_])
                    P.op("dve", lambda e: e.scalar_tensor_tensor(out=u_[0:ncol, :], in0=dst[0:ncol, 2:S + 2], scalar=pv[0:ncol, PV_MU1 + j:PV_MU1 + j + 1], in1=u_[0:ncol, :], op0=ALU.mult, op1=ALU.add), reads=[Tdst, Tpv, Tu_], writes=[Tu_])
                    P.dma("sp", [(self.us[s, j * 128:j * 128 + ncol, :], u_[0:ncol, :])], reads=[Tu_], writes=[self.tdr("us", s)])
                elif kind == "q":
                    P.dma("sp", [(self.qk[s, j * 128:(j + 1) * 128, :], dst[:])], reads=[Tdst], writes=[self.tdr("qk", s)])
                elif kind == "k":
                    P.dma("sp", [(self.qk[s, CW + j * 128:CW + (j + 1) * 128, :], dst[:])], reads=[Tdst], writes=[self.tdr("qk", s)])
                else:
                    P.dma("sp", [(self.gt[s, j * 128:(j + 1) * 128, :], dst[:])], reads=[Tdst], writes=[self.tdr("gt", s)])
            vc0 = NAQ0 + 2 * CW
            P.dma("pool", [(wv[:], w_l[:, :, vc0:vc0 + CW])], writes=[Twv])
            for t in range(self.NT):
                pb, Tpb = pp[pi % 8], Tpp[pi % 8]
                pi += 1
                for kc in range(8):
                    P.op("pe", mm(pb[:, :], xT[:, kc, t * 128:(t + 1) * 128], wv[:, kc, :], kc == 0, kc == 7), reads=[Twv, TxT[t]], writes=[Tpb])
                v_, Tv_ = vb[t % 2], Tvb[t % 2]
                P.op("dve", lambda e: e.tensor_copy(out=v_[:], in_=pb[:, :]), reads=[Tpb], writes=[Tv_])
                P.dma("sp", [(self.vna[s, t * 128:(t + 1) * 128, :], v_[:])], reads=[Tv_], writes=[self.tdr("vna", s)])


    def stage_R(self, l, s):
        P, nc, S = self.P, self.nc, self.S
        SEG = 512
        NSEG = S // SEG
        NCS = SEG // CH
        P.barrier()
        c = self.cstt
        ones_f = c[:, C_ONES:C_ONES + 128]
        with ExitStack() as es:
            pv = self.sb(es, "pvR", [128, NPV], F32)
            Tpv = T()
            P.dma("sp", [(pv[:], self.pvec[l])], writes=[Tpv])
            omka = self.sb(es, "omka", [128, 8], F32)
            P.op("dve", lambda e: e.tensor_scalar(out=omka[:, 0:4], in0=pv[:, PV_KA:PV_KA + 4], scalar1=-1.0, scalar2=1.0, op0=ALU.mult, op1=ALU.add), reads=[Tpv], writes=[Tpv])
            P.op("dve", lambda e: e.tensor_scalar(out=omka[:, 4:8], in0=pv[:, PV_KA:PV_KA + 4], scalar1=-2.0, scalar2=2.0, op0=ALU.mult, op1=ALU.add), reads=[Tpv], writes=[Tpv])
            lorab = self.sb(es, "lorab", [128, 4, CW], BF16)
            gupb = self.sb(es, "gupb", [128, CW], BF16)
            Tlw = T()
            P.op("pool", lambda e: e.memset(gupb[:], 0.0), writes=[Tlw])
            P.dma("pool", [(lorab[:], self.lora[l]), (gupb[0:96, :], self.gup[l])], writes=[Tlw])
            twa = self.sb(es, "twa", [128, S], BF16)
            sg = self.sb(es, "sg", [128, S], BF16)
            Ttw = T()
            P.op("pool", lambda e: e.memset(sg[:], 0.0), writes=[Ttw])
            smask = self.sb(es, "smask", [128, SEG], BF16)
            Tsm = T()
            P.op("pool", lambda e: e.memset(smask[:], 1.0), writes=[Tsm])
            P.op("pool", lambda e: e.memset(smask[:].rearrange("p (c t) -> p c t", t=CH)[:, :, 0:1], 0.0), writes=[Tsm])
            Tmk = T()
            SL, SU, IL, IU = (c[:, C_MASK + i * 128:C_MASK + (i + 1) * 128] for i in range(4))
            Fn_ = ["r", "k", "v", "z", "i", "c", "t", "e", "p", "m", "kk", "sq", "kd", "x"]
            F = {n: self.sb(es, "F" + n, [128, SEG], F32) for n in Fn_}
            TF = {n: T() for n in Fn_}
            bank = [[self.ps(es, "pb%d_%d" % (d, i), [128, 512]) for i in range(4)] for d in range(2)]
            Tbank = [[T() for i in range(4)] for d in range(2)]
            for seg in range(NSEG):
                t0 = seg * SEG
                P.dma("sp", [(F["x"][:], self.us[s, 1536:1664, t0:t0 + SEG])], reads=[self.tdr("us", s)], writes=[TF["x"]])
                P.op("act", lambda e: e.activation(out=twa[0:64, t0:t0 + SEG], in_=F["x"][0:64, :], func=AF.Tanh), reads=[TF["x"]], writes=[Ttw])
                P.op("dve", lambda e: e.tensor_copy(out=twa[64:128, t0:t0 + SEG], in_=F["x"][64:128, :]), reads=[TF["x"]], writes=[Ttw])
                P.dma("sp", [(F["x"][0:96, :], self.us[s, 1664:1760, t0:t0 + SEG])], reads=[self.tdr("us", s)], writes=[TF["x"]])
                P.op("act", lambda e: e.activation(out=sg[0:96, t0:t0 + SEG], in_=F["x"][0:96, :], func=AF.Sigmoid), reads=[TF["x"]], writes=[Ttw])
            H = [[{n: self.sb(es, "H%s%d" % (n, d), [128, SEG], BF16) for n in ("k", "b", "v")} for d in range(2)] for hb in range(2)]
            TH = [[T(), T()] for hb in range(2)]
            for hb in range(2):
                for d in range(2):
                    H[hb][d]["z"] = self.sb(es, "Hz%d" % d, [128, NCS, 4, 128], BF16)
                    P.op("pool", lambda e: e.memset(H[hb][d]["z"][:], 0.0), writes=[TH[hb][d]])
            TMt = [[{n: self.sb(es, "M%s%d" % (n, d), [128, NCS, 128], BF16) for n in ("k", "b", "v")} for d in range(2)] for hb in range(2)]
            TTM = [[T(), T()] for hb in range(2)]
            of = [self.sb(es, "of%d" % d, [128, S], F32) for d in range(2)]
            Tof = [[T() for _ in range(NSEG)] for d in range(2)]
            Dd = [self.sb(es, "Dd%d" % d, [128, S // CH + 1], F32) for d in range(2)]
            TD = [T(), T()]
            Sf = [self.sb(es, "Sf%d" % d, [128, 64], F32) for d in range(2)]
            Sb = [self.sb(es, "Sb%d" % d, [128, 64], BF16) for d in range(2)]
            S0D = [self.sb(es, "S0D%d" % d, [128, 64], F32) for d in range(2)]
            TS = [T(), T()]
            TSb = [T(), T()]
            TS0 = [T(), T()]
            NSLOT = 3
            Pp = [[self.sb(es, "Pp%d_%d" % (k, i), [128, 2, 128], BF16) for i in range(2)] for k in range(NSLOT)]
            TPp = [[T(), T()] for _ in range(NSLOT)]
            W = [[self.sb(es, "W%d_%d" % (k, i), [128, 2, 2, 128], BF16) for i in range(2)] for k in range(NSLOT)]
            TWp = [[T(), T()] for _ in range(NSLOT)]
            TWx = [[T(), T()] for _ in range(NSLOT)]
            S1 = [self.sb(es, "S1_%d" % d, [128, NCS, 4, 128], BF16) for d in range(2)]
            S2 = [self.sb(es, "S2_%d" % d, [128, NCS, 4, 128], BF16) for d in range(2)]
            XTs = [self.sb(es, "XTs%d" % d, [128, NCS, 2, 128], BF16) for d in range(2)]
            TST = [[T() for _ in range(NCS)] for d in range(2)]
            mkT = [self.sb(es, "mkT%d" % d, [128, 512], F32) for d in range(2)]
            mkA = [self.sb(es, "mkA%d" % d, [128, 256], F32) for d in range(2)]
            for d, (ms, mi, ma) in enumerate(((SU, IU, SL), (SL, IL, SU))):
                for i, m in enumerate((ms, ms, mi, mi)):
                    P.op("pool", lambda e: e.tensor_copy(out=mkT[d][:, i * 128:(i + 1) * 128], in_=m), reads=[self.Tc], writes=[Tmk])
                for i in range(2):
                    P.op("pool", lambda e: e.tensor_copy(out=mkA[d][:, i * 128:(i + 1) * 128], in_=ma), reads=[self.Tc], writes=[Tmk])
            RHb = [self.sb(es, "RHb%d" % d, [128, 128], BF16) for d in range(2)]
            TRH = [T(), T()]
            Ub = [self.sb(es, "Ub%d" % d, [128, 128], BF16) for d in range(2)]
            TU = [T(), T()]
            yH = self.sb(es, "yH", [128, SEG], BF16)
            TyH = T()
            id2 = self.cb[:, CB_ID2:CB_ID2 + 256].rearrange("p (h t) -> p h t", h=2)
            idb = self.cb[:, CB_ID:CB_ID + 128]

            flat = [bank[0][0], bank[0][1], bank[0][2], bank[0][3], bank[1][0], bank[1][1], bank[1][2], bank[1][3]]
            Tflat = [Tbank[0][0], Tbank[0][1], Tbank[0][2], Tbank[0][3], Tbank[1][0], Tbank[1][1], Tbank[1][2], Tbank[1][3]]
            bgb, Tbgb = (flat[6], flat[7]), (Tflat[6], Tflat[7])

            def prep(hp, d, seg, hb):
                t0 = seg * SEG
                hc = slice(hp * 128, (hp + 1) * 128)
                pz, Tpz = bgb[0], Tbgb[0]
                P.dma("sp", [(F["r"][:], self.us[s, hp * 128:(hp + 1) * 128, t0:t0 + SEG]),
                             (F["k"][:], self.us[s, 512 + hp * 128:512 + (hp + 1) * 128, t0:t0 + SEG]),
                             (F["v"][:], self.us[s, 1024 + hp * 128:1024 + (hp + 1) * 128, t0:t0 + SEG])],
                      reads=[self.tdr("us", s)], writes=[TF["r"], TF["k"], TF["v"]])
                for g in range(SEG // 512):
                    gs = slice(g * 512, (g + 1) * 512)
                    ts_ = slice(t0 + g * 512, t0 + (g + 1) * 512)
                    P.op("pe", mm(pz[:, :], lorab[:, d, hc], twa[:, ts_]), reads=[Tlw, Ttw], writes=[Tpz])
                    P.op("act", lambda e: e.activation(out=F["z"][:, gs], in_=pz[:, :], func=AF.Sigmoid, bias=pv[:, PV_W0 + 4 * d + hp:PV_W0 + 4 * d + hp + 1], scale=1.0), reads=[Tpz, Tpv], writes=[TF["z"]])
                    P.op("pe", mm(pz[:, :], lorab[:, 2 + d, hc], twa[:, ts_]), reads=[Tlw, Ttw], writes=[Tpz])
                    P.op("act", lambda e: e.activation(out=F["i"][:, gs], in_=pz[:, :], func=AF.Sigmoid, bias=pv[:, PV_A0 + 4 * d + hp:PV_A0 + 4 * d + hp + 1], scale=1.0), reads=[Tpz, Tpv], writes=[TF["i"]])
                yield
                P.op("dve", lambda e: e.tensor_tensor_scan(out=F["c"][:], data0=smask[:], data1=F["z"][:], initial=0.0, op0=ALU.mult, op1=ALU.add), reads=[Tsm, TF["z"]], writes=[TF["c"]])
                if d == 0:
                    cum, Tcum = F["c"], TF["c"]
                else:
                    P.op("dve", lambda e: e.tensor_tensor(out=F["t"][:], in0=F["z"][:], in1=F["c"][:], op=ALU.subtract), reads=[TF["z"], TF["c"]], writes=[TF["t"]])
                    c3 = F["c"][:].rearrange("p (c t) -> p c t", t=CH)
                    P.op("dve", lambda e: e.tensor_tensor(out=F["t"][:].rearrange("p (c t) -> p c t", t=CH), in0=F["t"][:].rearrange("p (c t) -> p c t", t=CH),
                                                          in1=c3[:, :, CH - 1:CH].to_broadcast([128, NCS, CH]), op=ALU.add), reads=[TF["t"], TF["c"]], writes=[TF["t"]])
                    cum, Tcum = F["t"], TF["t"]
                P.op("dve", lambda e: e.tensor_tensor(out=F["e"][:], in0=cum[:], in1=F["z"][:], op=ALU.subtract), reads=[Tcum, TF["z"]], writes=[TF["e"]])
                P.op("act", lambda e: e.activation(out=F["e"][:], in_=F["e"][:], func=AF.Exp, scale=-KAPPA), reads=[TF["e"]], writes=[TF["e"]])
                P.op("act", lambda e: e.activation(out=F["p"][:], in_=cum[:], func=AF.Exp, scale=-KAPPA), reads=[Tcum], writes=[TF["p"]])
                P.op("act", lambda e: e.activation(out=F["m"][:], in_=cum[:], func=AF.Exp, scale=KAPPA), reads=[Tcum], writes=[TF["m"]])
                p3 = F["p"][:].rearrange("p (c t) -> p c t", t=CH)
                edge = CH - 1 if d == 0 else 0
                P.op("pool", lambda e: e.tensor_copy(out=Dd[d][:, seg * NCS:(seg + 1) * NCS], in_=p3[:, :, edge]), reads=[TF["p"]], writes=[TD[d]])
                yield
                P.op("act", lambda e: e.activation(out=F["kk"][:], in_=F["k"][:], func=AF.Copy, scale=pv[:, PV_KK + hp:PV_KK + hp + 1]), reads=[TF["k"], Tpv], writes=[TF["kk"]])
                P.op("act", lambda e: e.activation(out=F["sq"][:], in_=F["kk"][:], func=AF.Square), reads=[TF["kk"]], writes=[TF["sq"]])
                for g in range(SEG // 512):
                    gs = slice(g * 512, (g + 1) * 512)
                    P.op("pe", mm(pz[:, :], ones_f, F["sq"][:, gs]), reads=[self.Tc, TF["sq"]], writes=[Tpz])
                    P.op("act", lambda e: e.activation(out=F["sq"][:, gs], in_=pz[:, :], func=AF.Ln, bias=self.c_eps_kk, scale=1.0), reads=[Tpz, self.Tc], writes=[TF["sq"]])
                P.op("act", lambda e: e.activation(out=F["sq"][:], in_=F["sq"][:], func=AF.Exp, scale=-0.5), reads=[TF["sq"]], writes=[TF["sq"]])
                P.op("dve", lambda e: e.tensor_tensor(out=F["kk"][:], in0=F["kk"][:], in1=F["sq"][:], op=ALU.mult), reads=[TF["kk"], TF["sq"]], writes=[TF["kk"]])
                yield
                P.op("dve", lambda e: e.tensor_scalar(out=F["kd"][:], in0=F["i"][:], scalar1=pv[:, PV_KA + hp:PV_KA + hp + 1], scalar2=omka[:, hp:hp + 1], op0=ALU.mult, op1=ALU.add), reads=[TF["i"], Tpv], writes=[TF["kd"]])
                P.op("dve", lambda e: e.tensor_tensor(out=F["kd"][:], in0=F["kd"][:], in1=F["k"][:], op=ALU.mult), reads=[TF["kd"], TF["k"]], writes=[TF["kd"]])
                yield
                Hd = H[hb][d]
                for h in range(2):
                    ph = slice(h * 64, (h + 1) * 64)
                    P.op("dve", lambda e: e.tensor_tensor(out=Hd["z"][ph, :, 2 + h, :], in0=F["r"][ph, :].rearrange("p (c t) -> p c t", t=CH), in1=F["p"][ph, :].rearrange("p (c t) -> p c t", t=CH), op=ALU.mult), reads=[TF["r"], TF["p"]], writes=[TH[hb][d]])
                P.op("dve", lambda e: e.tensor_tensor(out=Hd["k"][:], in0=F["kd"][:], in1=F["m"][:], op=ALU.mult), reads=[TF["kd"], TF["m"]], writes=[TH[hb][d]])
                for h in range(2):
                    ph = slice(h * 64, (h + 1) * 64)
                    P.op("dve", lambda e: e.scalar_tensor_tensor(out=Hd["z"][ph, :, h, :], in0=F["kk"][ph, :].rearrange("p (c t) -> p c t", t=CH), scalar=-1.0, in1=F["e"][ph, :].rearrange("p (c t) -> p c t", t=CH), op0=ALU.mult, op1=ALU.mult), reads=[TF["kk"], TF["e"]], writes=[TH[hb][d]])
                P.op("pool", lambda e: e.tensor_tensor(out=F["x"][:], in0=F["kk"][:], in1=F["i"][:], op=ALU.mult), reads=[TF["kk"], TF["i"]], writes=[TF["x"]])
                P.op("dve", lambda e: e.tensor_tensor(out=Hd["b"][:], in0=F["x"][:], in1=F["m"][:], op=ALU.mult), reads=[TF["x"], TF["m"]], writes=[TH[hb][d]])
                P.op("act", lambda e: e.activation(out=Hd["v"][:], in_=F["v"][:], func=AF.Copy), reads=[TF["v"]], writes=[TH[hb][d]])
                yield
                bi = 0
                for n in ("k", "b", "v"):
                    for half in range(NCS // 4):
                        pb, Tpb = bgb[bi % 2], Tbgb[bi % 2]
                        bi += 1
                        for q in range(4):
                            cc = half * 4 + q
                            P.op("pe", mm(pb[:, q * 128:(q + 1) * 128], Hd[n][:, cc * 128:(cc + 1) * 128], idb), reads=[TH[hb][d], self.Tcb], writes=[Tpb])
                        o_ap = TMt[hb][d][n][:, half * 4:half * 4 + 4, :]
                        i_ap = pb[:].rearrange("p (a b) -> p a b", a=4)
                        if bi % 2 == 0:
                            P.op("act", lambda e: e.activation(out=o_ap, in_=i_ap, func=AF.Copy), reads=[Tpb], writes=[TTM[hb][d]])
                        else:
                            P.op("dve", lambda e: e.tensor_copy(out=o_ap, in_=i_ap), reads=[Tpb], writes=[TTM[hb][d]])
                        yield

            def s0d_update(d, gc_next):
                P.op("act", lambda e: e.activation(out=S0D[d][:], in_=Sf[d][:], func=AF.Copy, scale=Dd[d][:, gc_next:gc_next + 1]), reads=[TS[d], TD[d]], writes=[TS0[d]])

            def ph1(d, cl, k, hb):
                bk0, bk1 = flat[2 * k], flat[2 * k + 1]
                Tb0, Tb1 = Tflat[2 * k], Tflat[2 * k + 1]
                Hd = H[hb][d]
                cs = slice(cl * 128, (cl + 1) * 128)
                Zc = Hd["z"][:, cl]
                bcs, kcs = Hd["b"][:, cs], Hd["k"][:, cs]
                Tst = TST[d][cl]
                f2 = lambda ap: ap.rearrange("p a t -> p (a t)")
                P.op("pe", mm(bk0[:, :], bcs, f2(Zc)), reads=[TH[hb][d]], writes=[Tb0])
                P.op("pe", mm(bk1[:, :], kcs, f2(Zc)), reads=[TH[hb][d]], writes=[Tb1])
                P.op("dve", lambda e: e.tensor_tensor(out=f2(S1[d][:, cl]), in0=bk0[:, :], in1=mkT[d][:], op=ALU.mult), reads=[Tb0, Tmk], writes=[Tst])
                P.op("dve", lambda e: e.tensor_tensor(out=f2(S2[d][:, cl]), in0=bk1[:, :], in1=mkT[d][:], op=ALU.mult), reads=[Tb1, Tmk], writes=[Tst])
                for h in range(2):
                    P.op("pe", mm(bk0[:, h * 128:(h + 1) * 128], Zc[:, h, :], bcs), reads=[TH[hb][d]], writes=[Tb0])
                P.op("dve", lambda e: e.tensor_tensor(out=f2(Pp[k][0][:]), in0=bk0[:, 0:256], in1=mkA[d][:], op=ALU.mult), reads=[Tb0, Tmk], writes=[TPp[k][0]])
                P.op("pool", lambda e: e.tensor_tensor(out=W[k][0][:, :, 1, :], in0=S1[d][:, cl, 0:2, :], in1=id2, op=ALU.add), reads=[Tst, self.Tcb], writes=[TWx[k][0]])
                yield
                b0v = bk0[:].rearrange("p (h x) -> p h x", h=2)
                for h in range(2):
                    P.op("pe", mm(bk1[:, h * 128:(h + 1) * 128], S1[d][:, cl, h, :], Pp[k][0][:, h, :]), reads=[Tst, TPp[k][0]], writes=[Tb1])
                    P.op("pe", mm(bk0[:, h * 256:h * 256 + 128], Pp[k][0][:, h, :], S1[d][:, cl, h, :]), reads=[Tst, TPp[k][0]], writes=[Tb0])
                P.op("act", lambda e: e.activation(out=W[k][0][:, :, 0, :], in_=b0v[:, :, 0:128], func=AF.Copy), reads=[Tb0], writes=[TWp[k][0]])
                P.op("dve", lambda e: e.tensor_copy(out=f2(Pp[k][1][:]), in_=bk1[:, 0:256]), reads=[Tb1], writes=[TPp[k][1]])
                yield
                wi, pi = 0, 1
                for j in range(1, 6):
                    Wc, Wn, Pc, Pnx = W[k][wi], W[k][1 - wi], Pp[k][pi], Pp[k][1 - pi]
                    for h in range(2):
                        if os.environ.get("K_VAR", "") == "1":
                            P.op("pe", mm(bk0[:, h * 256:h * 256 + 128], Pc[:, h, :], Wc[:, h, 0, :]), reads=[TPp[k][pi], TWp[k][wi], TWx[k][wi]], writes=[Tb0])
                            P.op("pe", mm(bk0[:, h * 256 + 128:h * 256 + 256], Pc[:, h, :], Wc[:, h, 1, :]), reads=[TPp[k][pi], TWp[k][wi], TWx[k][wi]], writes=[Tb0])
                        elif j == 5:
                            P.op("pe", mm(bk0[:, h * 256 + 128:h * 256 + 256], Pc[:, h, :], Wc[:, h, 1, :]), reads=[TPp[k][pi], TWx[k][wi]], writes=[Tb0])
                        else:
                            P.op("pe", mm(bk0[:, h * 256:(h + 1) * 256], Pc[:, h, :], Wc[:, h].rearrange("p a t -> p (a t)")), reads=[TPp[k][pi], TWp[k][wi], TWx[k][wi]], writes=[Tb0])
                        P.op("pe", mm(bk1[:, h * 128:(h + 1) * 128], Wc[:, h, 0, :], Pc[:, h, :]), reads=[TPp[k][pi], TWp[k][wi]], writes=[Tb1])
                    if j < 5:
                        P.op("dve", lambda e: e.tensor_copy(out=Wn[:, :, 0, :], in_=b0v[:, :, 0:128]), reads=[Tb0], writes=[TWp[k][1 - wi]])
                    P.op("dve", lambda e: e.tensor_tensor(out=Wn[:, :, 1, :], in0=b0v[:, :, 128:256], in1=Wc[:, :, 1, :], op=ALU.add), reads=[Tb0, TWx[k][wi]] + ([TWp[k][1 - wi]] if os.environ.get("K_VAR", "") == "2" else []), writes=[TWx[k][1 - wi]])
                    P.op("act", lambda e: e.activation(out=f2(Pnx[:]), in_=bk1[:, 0:256], func=AF.Copy), reads=[Tb1], writes=[TPp[k][1 - pi]])
                    wi, pi = 1 - wi, 1 - pi
                    yield
                Wc, Pc = W[k][wi], Pp[k][pi]
                for h in range(2):
                    P.op("pe", mm(bk0[:, h * 256 + 128:h * 256 + 256], Pc[:, h, :], Wc[:, h, 1, :]), reads=[TPp[k][pi], TWx[k][wi]], writes=[Tb0])
                P.op("dve", lambda e: e.tensor_tensor(out=XTs[d][:, cl], in0=b0v[:, :, 128:256], in1=Wc[:, :, 1, :], op=ALU.add), reads=[Tb0, TWx[k][wi]], writes=[Tst])
                yield

            def ph2(d, seg, cl, hb):
                gc = seg * NCS + cl
                A0, A1 = flat[3 * d], flat[3 * d + 1]
                TA0, TA1 = Tflat[3 * d], Tflat[3 * d + 1]
                Hd = H[hb][d]
                cs = slice(cl * 128, (cl + 1) * 128)
                hs = (slice(0, 64), slice(64, 128))
                kT_, bT_, vT_ = TMt[hb][d]["k"], TMt[hb][d]["b"], TMt[hb][d]["v"]
                Tst = TST[d][cl]
                A2, TA2 = flat[3 * d + 2], Tflat[3 * d + 2]
                s0d_update(d, gc)
                for h in range(2):
                    P.op("pe", mm(A0[:, h * 64:(h + 1) * 64], Hd["z"][:, cl, h, :], Sb[d][:, :], True, False), reads=[TH[hb][d], TSb[d]], writes=[TA0])
                    P.op("pe", mm(A0[:, h * 64:(h + 1) * 64], S2[d][:, cl, h, :], vT_[:, cl, hs[h]], False, True), reads=[Tst, TTM[hb][d]], writes=[TA0])
                P.op("act", lambda e: e.activation(out=RHb[d][:], in_=A0[:, 0:128], func=AF.Copy), reads=[TA0], writes=[TRH[d]])
                for h in range(2):
                    P.op("pe", mm(A2[hs[h], 0:128], Sb[d][:, :], Hd["z"][:, cl, 2 + h, :], True, False), reads=[TSb[d], TH[hb][d]], writes=[TA2])
                    P.op("pe", mm(A2[hs[h], 0:128], vT_[:, cl, hs[h]], S2[d][:, cl, 2 + h, :], False, False), reads=[TTM[hb][d], Tst], writes=[TA2])
                yield
                for h in range(2):
                    P.op("pe", mm(A0[:, 128 + h * 64:128 + (h + 1) * 64], XTs[d][:, cl, h, :], RHb[d][:, h * 64:(h + 1) * 64]), reads=[Tst, TRH[d]], writes=[TA0])
                P.op("dve", lambda e: e.tensor_copy(out=Ub[d][:], in_=A0[:, 128:256]), reads=[TA0], writes=[TU[d]])
                yield
                for h in range(2):
                    P.op("pe", mm(A1[hs[h], 128:192], bT_[:, cl, hs[h]], Ub[d][:, h * 64:(h + 1) * 64], True, False), reads=[TTM[hb][d], TU[d]], writes=[TA1])
                    P.op("pe", mm(A1[hs[h], 128:192], kT_[:, cl, hs[h]], vT_[:, cl, hs[h]], False, True), reads=[TTM[hb][d]], writes=[TA1])
                P.op("dve", lambda e: e.scalar_tensor_tensor(out=Sb[d][:], in0=A1[:, 128:192], scalar=Dd[d][:, gc:gc + 1], in1=S0D[d][:], op0=ALU.mult, op1=ALU.add),
                     reads=[TA1, TD[d], TS0[d]], writes=[TSb[d]])
                P.op("dve", lambda e: e.scalar_tensor_tensor(out=Sf[d][:], in0=A1[:, 128:192], scalar=Dd[d][:, gc:gc + 1], in1=S0D[d][:], op0=ALU.mult, op1=ALU.add),
                     reads=[TA1, TD[d], TS0[d]], writes=[TS[d]])
                for h in range(2):
                    P.op("pe", mm(A2[hs[h], 0:128], Ub[d][:, h * 64:(h + 1) * 64], S1[d][:, cl, 2 + h, :], False, True), reads=[TU[d], Tst], writes=[TA2])
                P.op("act", lambda e: e.activation(out=of[d][:, seg * SEG + cl * 128:seg * SEG + (cl + 1) * 128], in_=A2[:, 0:128], func=AF.Copy), reads=[TA2], writes=[Tof[d][seg]])
                yield

            def post(hp, seg_order):
                for seg in seg_order:
                    t0 = seg * SEG
                    hc = slice(hp * 128, (hp + 1) * 128)
                    pz, Tpz = bgb[0], Tbgb[0]
                    pz2, Tpz2 = bgb[1], Tbgb[1]
                    P.dma("sp", [(F["r"][:], self.us[s, hp * 128:(hp + 1) * 128, t0:t0 + SEG]),
                                 (F["k"][:], self.us[s, 512 + hp * 128:512 + (hp + 1) * 128, t0:t0 + SEG]),
                                 (F["v"][:], self.us[s, 1024 + hp * 128:1024 + (hp + 1) * 128, t0:t0 + SEG])],
                          reads=[self.tdr("us", s)], writes=[TF["r"], TF["k"], TF["v"]])
                    P.op("dve", lambda e: e.tensor_tensor(out=F["c"][:], in0=of[0][:, t0:t0 + SEG], in1=of[1][:, t0:t0 + SEG], op=ALU.add), reads=[Tof[0][seg], Tof[1][seg]], writes=[TF["c"]])
                    for g in range(SEG // 512):
                        gs = slice(g * 512, (g + 1) * 512)
                        P.op("pe", mm(pz[:, :], ones_f, F["c"][:, gs]), reads=[self.Tc, TF["c"]], writes=[Tpz])
                        P.op("dve", lambda e: e.scalar_tensor_tensor(out=F["t"][:, gs], in0=pz[:, :], scalar=-1.0 / 64, in1=F["c"][:, gs], op0=ALU.mult, op1=ALU.add), reads=[Tpz, TF["c"]], writes=[TF["t"]])
                    yield
                    P.op("act", lambda e: e.activation(out=F["sq"][:], in_=F["t"][:], func=AF.Square), reads=[TF["t"]], writes=[TF["sq"]])
                    for g in range(SEG // 512):
                        gs = slice(g * 512, (g + 1) * 512)
                        P.op("pe", mm(pz2[:, :], ones_f, F["sq"][:, gs]), reads=[self.Tc, TF["sq"]], writes=[Tpz2])
                        P.op("act", lambda e: e.activation(out=F["e"][:, gs], in_=pz2[:, :], func=AF.Ln, bias=self.c_eps_gn, scale=1.0 / 64), reads=[Tpz2, self.Tc], writes=[TF["e"]])
                    P.op("act", lambda e: e.activation(out=F["e"][:], in_=F["e"][:], func=AF.Exp, scale=-0.5), reads=[TF["e"]], writes=[TF["e"]])
                    P.op("dve", lambda e: e.tensor_tensor(out=F["t"][:], in0=F["t"][:], in1=F["e"][:], op=ALU.mult), reads=[TF["t"], TF["e"]], writes=[TF["t"]])
                    P.op("act", lambda e: e.activation(out=F["t"][:], in_=F["t"][:], func=AF.Identity, scale=pv[:, PV_GG + hp:PV_GG + hp + 1], bias=pv[:, PV_GB + hp:PV_GB + hp + 1]), reads=[TF["t"], Tpv], writes=[TF["t"]])
                    yield
                    for g in range(SEG // 512):
                        gs = slice(g * 512, (g + 1) * 512)
                        ts_ = slice(t0 + g * 512, t0 + (g + 1) * 512)
                        P.op("pe", mm(pz[:, :], lorab[:, 2, hc], twa[:, ts_]), reads=[Tlw, Ttw], writes=[Tpz])
                        P.op("act", lambda e: e.activation(out=F["i"][:, gs], in_=pz[:, :], func=AF.Sigmoid, bias=pv[:, PV_A0 + hp:PV_A0 + hp + 1], scale=1.0), reads=[Tpz, Tpv], writes=[TF["i"]])
                        P.op("pe", mm(pz2[:, :], lorab[:, 3, hc], twa[:, ts_]), reads=[Tlw, Ttw], writes=[Tpz2])
                        P.op("act", lambda e: e.activation(out=F["z"][:, gs], in_=pz2[:, :], func=AF.Sigmoid, bias=pv[:, PV_A0 + 4 + hp:PV_A0 + 4 + hp + 1], scale=1.0), reads=[Tpz2, Tpv], writes=[TF["z"]])
                    yield
                    P.op("pool", lambda e: e.tensor_tensor(out=F["i"][:], in0=F["i"][:], in1=F["z"][:], op=ALU.add), reads=[TF["i"], TF["z"]], writes=[TF["i"]])
                    P.op("dve", lambda e: e.tensor_scalar(out=F["i"][:], in0=F["i"][:], scalar1=pv[:, PV_KA + hp:PV_KA + hp + 1], scalar2=omka[:, 4 + hp:5 + hp], op0=ALU.mult, op1=ALU.add), reads=[TF["i"], Tpv], writes=[TF["i"]])
                    P.op("dve", lambda e: e.tensor_tensor(out=F["i"][:], in0=F["i"][:], in1=F["k"][:], op=ALU.mult), reads=[TF["i"], TF["k"]], writes=[TF["i"]])
                    P.op("dve", lambda e: e.scalar_tensor_tensor(out=F["i"][:], in0=F["i"][:], scalar=pv[:, PV_RK + hp:PV_RK + hp + 1], in1=F["r"][:], op0=ALU.mult, op1=ALU.mult), reads=[TF["i"], Tpv, TF["r"]], writes=[TF["i"]])
                    for g in range(SEG // 512):
                        gs = slice(g * 512, (g + 1) * 512)
                        ts_ = slice(t0 + g * 512, t0 + (g + 1) * 512)
                        P.op("pe", mm(pz[:, :], ones_f, F["i"][:, gs]), reads=[self.Tc, TF["i"]], writes=[Tpz])
                        P.op("dve", lambda e: e.tensor_tensor(out=F["m"][:, gs], in0=pz[:, :], in1=F["v"][:, gs], op=ALU.mult), reads=[Tpz, TF["v"]], writes=[TF["m"]])
                    yield
                    P.op("pool", lambda e: e.tensor_tensor(out=F["t"][:], in0=F["t"][:], in1=F["m"][:], op=ALU.add), reads=[TF["t"], TF["m"]], writes=[TF["t"]])
                    for g in range(SEG // 512):
                        gs = slice(g * 512, (g + 1) * 512)
                        ts_ = slice(t0 + g * 512, t0 + (g + 1) * 512)
                        P.op("pe", mm(pz2[:, :], gupb[:, hc], sg[:, ts_]), reads=[Tlw, Ttw], writes=[Tpz2])
                        P.op("dve", lambda e: e.tensor_tensor(out=yH[:, gs], in0=pz2[:, :], in1=F["t"][:, gs], op=ALU.mult), reads=[Tpz2, TF["t"]], writes=[TyH])
                    P.dma("sp", [(self.ya[s, hp * 128:(hp + 1) * 128, t0:t0 + SEG], yH[:])], reads=[TyH], writes=[self.tdr("ya", s)])
                    yield

            def chain(gens):
                for g in gens:
                    for _ in g:
                        yield

            def step(gen, n=1):
                if gen is None:
                    return False
                for _ in range(n):
                    try:
                        next(gen)
                    except StopIteration:
                        return False
                return True

            iters = [(hp, sgi) for hp in range(4) for sgi in range(NSEG)]

            def bg_for(it):
                hp, sgi = iters[it]
                gp, gq = [], []
                if hp > 0 and sgi < (NSEG + 1) // 2:
                    segl = [sgi] + ([NSEG - 1 - sgi] if NSEG - 1 - sgi != sgi else [])
                    gp.append(post(hp - 1, segl))
                if it + 1 < len(iters):
                    hp2, sg2 = iters[it + 1]
                    segs2 = (sg2, NSEG - 1 - sg2)
                    for d in range(2):
                        gq.append(prep(hp2, d, segs2[d], (it + 1) % 2))
                return chain(gp), chain(gq)

            for d in range(2):
                step(prep(0, d, (0, NSEG - 1)[d], 0), 10 ** 6)
            for it, (hp, sgi) in enumerate(iters):
                hb = it % 2
                segs = (sgi, NSEG - 1 - sgi)
                bgp, bgq = bg_for(it)
                p_alive, q_alive = True, True
                if BG_MODE == 0:
                    step(bgp, 10 ** 6)
                    step(bgq, 10 ** 6)
                    p_alive = q_alive = False
                units = []
                for ci in range(NCS):
                    units.append((0, ci))
                    units.append((1, NCS - 1 - ci))
                queue = list(units)
                active = [None] * NSLOT
                while queue or any(a is not None for a in active):
                    for k in range(NSLOT):
                        if active[k] is None and queue:
                            u = queue.pop(0)
                            active[k] = ph1(u[0], u[1], k, hb)
                        if active[k] is not None and not step(active[k]):
                            active[k] = None
                    if p_alive:
                        p_alive = step(bgp)
                    elif q_alive:
                        q_alive = step(bgq)
                if p_alive:
                    step(bgp, 10 ** 6)
                if sgi == 0:
                    for d in range(2):
                        P.op("pool", lambda e: e.memset(Sf[d][:], 0.0), writes=[TS[d]])
                        P.op("pool", lambda e: e.memset(Sb[d][:], 0.0), writes=[TSb[d]])
                        P.op("pool", lambda e: e.memset(S0D[d][:], 0.0), writes=[TS0[d]])
                for ci in range(NCS):
                    cls = (ci, NCS - 1 - ci)
                    gens = [ph2(d, segs[d], cls[d], hb) for d in range(2)]
                    alive = [True, True]
                    while any(alive):
                        for d in range(2):
                            if alive[d]:
                                alive[d] = step(gens[d])
                if q_alive:
                    step(bgq, 10 ** 6)
            step(post(3, list(range(NSEG))), 10 ** 6)


    def stage_N(self, l, s):
        P, nc, S, R = self.P, self.nc, self.S, self.R
        plan, variants = na_plan(R)
        NB = R // 8
        nvar = max(len(variants), 1)
        P.barrier()
        with ExitStack() as es:
            qT = self.sb(es, "qT", [128, S], BF16)
            kz = [self.sb(es, "kz%d" % h, [128, S], BF16) for h in range(2)]
            V = self.sb(es, "Vn", [128, self.NT, 128], BF16)
            Tq = T()
            tab = self.sb(es, "tab", [128, 2, 2, NE * 64], F32)
            Ttab = T()
            rmk = self.sb(es, "rmk", [128, nvar, 8, 512], F32)
            Trm = T()
            onesb = self.sb(es, "onesb", [128, 64], BF16)
            Tones = T()
            NE1, NEX, NSC, LA = 5, 5, 4, 3
            e1 = [self.sb(es, "e1_%d" % i, [128, 512], F32) for i in range(NE1)]
            Te1 = [T() for _ in range(NE1)]
            ex = [self.sb(es, "ex_%d" % i, [128, 512], BF16) for i in range(NEX)]
            Tex = [T() for _ in range(NEX)]
            rec = self.sb(es, "rec", [128, 512], F32)
            Trec = T()
            ybT = self.sb(es, "ybT", [128, S], BF16)
            TybT = T()
            sc = [self.ps(es, "sc%d" % i, [128, 512]) for i in range(NSC)]
            Tsc = [T() for _ in range(NSC)]
            NUM = [self.ps(es, "NUM%d" % i, [128, 512]) for i in range(2)]
            DEN = [self.ps(es, "DEN%d" % i, [128, 512]) for i in range(2)]
            TN, TDn = [T(), T()], [T(), T()]
            P.op("pool", lambda e: e.memset(onesb[:], 1.0), writes=[Tones])
            for h in range(2):
                P.op("pool", lambda e: e.memset(kz[h][:], 0.0), writes=[Tq])
            if len(variants) > 0:
                P.dma("sp", [(rmk[:, v], self.rmask[v]) for v in range(len(variants))], writes=[Trm])
            vsrc = self.vna[s].rearrange("(n p) c -> p n c", p=128)
            si = 0
            ei = 0
            for hp in range(4):
                pairs = [(qT[:], self.qk[s, hp * 128:(hp + 1) * 128, :]),
                         (V[:], vsrc[:, :, hp * 128:(hp + 1) * 128])]
                for h in range(2):
                    pairs.append((kz[h][h * 64:(h + 1) * 64, :], self.qk[s, CW + hp * 128 + h * 64:CW + hp * 128 + (h + 1) * 64, :]))
                P.dma("sp", pairs, reads=[self.tdr("qk", s), self.tdr("vna", s)], writes=[Tq])
                tp = []
                for h in range(2):
                    for tt in range(2):
                        for r2 in range(2):
                            tp.append((tab[r2 * 64:(r2 + 1) * 64, h, tt, :].rearrange("p (m c) -> p m c", c=64),
                                       self.btab[l, 2 * hp + h, tt, :, (1 - r2):(1 - r2) + NE, :]))
                P.dma("sp", tp, writes=[Ttab])
                units = []
                for b in range(NB):
                    p0, npair, off, var = plan[b]
                    for h in range(2):
                        for j in range(npair):
                            units.append((b, h, j))

                def sc_part(u, idx):
                    b, h, j = u
                    p0, npair, off, var = plan[b]
                    qs = slice(b * 512, (b + 1) * 512)
                    tt = 0 if var is None else 1
                    keys = slice((p0 + j) * 128, (p0 + j + 1) * 128)
                    scb, Tscb = sc[idx % NSC], Tsc[idx % NSC]
                    P.op("pe", mm(scb[:, :], kz[h][:, keys], qT[:, qs]), reads=[Tq], writes=[Tscb])
                    m0 = 14 - off - 2 * j
                    e_, Te_ = e1[idx % NE1], Te1[idx % NE1]
                    x_, Tx_ = ex[idx % NEX], Tex[idx % NEX]
                    P.op("dve", lambda e: e.tensor_tensor(out=e_[:], in0=scb[:, :], in1=tab[:, h, tt, m0 * 64:(m0 + 8) * 64], op=ALU.add), reads=[Tscb, Ttab], writes=[Te_])
                    if var is not None:
                        P.op("pool", lambda e: e.tensor_tensor(out=e_[:], in0=e_[:], in1=rmk[:, var, j, :], op=ALU.add), reads=[Te_, Trm], writes=[Te_])
                    P.op("act", lambda e: e.activation(out=x_[:], in_=e_[:], func=AF.Exp), reads=[Te_], writes=[Tx_])

                def pv_part(u, idx):
                    b, h, j = u
                    p0, npair, off, var = plan[b]
                    qs = slice(b * 512, (b + 1) * 512)
                    ph = slice(h * 64, (h + 1) * 64)
                    x_, Tx_ = ex[idx % NEX], Tex[idx % NEX]
                    nb_, db_ = NUM[b % 2], DEN[b % 2]
                    P.op("pe", mm(nb_[ph, :], V[:, p0 + j, h * 64:(h + 1) * 64], x_[:], j == 0, j == npair - 1), reads=[Tq, Tx_], writes=[TN[b % 2]])
                    P.op("pe", mm(db_[ph, :], onesb[:], x_[:], j == 0, j == npair - 1), reads=[Tones, Tx_], writes=[TDn[b % 2]])
                    if h == 1 and j == npair - 1:
                        P.op("dve", lambda e: e.reciprocal(out=rec[:], in_=db_[:, :]), reads=[TDn[b % 2]], writes=[Trec])
                        P.op("dve", lambda e: e.tensor_tensor(out=ybT[:, qs], in0=nb_[:, :], in1=rec[:], op=ALU.mult), reads=[TN[b % 2], Trec], writes=[TybT])

                nu = len(units)
                for i in range(nu + LA):
                    if i < nu:
                        sc_part(units[i], i)
                    if i >= LA:
                        pv_part(units[i - LA], i - LA)
                P.dma("sp", [(self.yb[s, hp * 128:(hp + 1) * 128, :], ybT[:])], reads=[TybT], writes=[self.tdr("yb", s)])

    def stage_M(self, l):
        P, nc, S = self.P, self.nc, self.S
        P.barrier()
        with ExitStack() as es:
            WA = self.sb(es, "WA", [128, 4, D], BF16)
            WB = self.sb(es, "WB", [128, 4, D], BF16)
            WO = self.sb(es, "WO", [128, 8, D], BF16)
            Tw = T()
            P.dma("pool", [(WA[:], self.w_a[l].rearrange("(kc p) c -> p kc c", p=128)),
                           (WB[:], self.w_b[l].rearrange("(kc p) c -> p kc c", p=128))], writes=[Tw])
            wo_src = self.w_out[l].rearrange("(kc p) c -> p kc c", p=128)
            P.dma("pool", [(WO[:, 0:4], wo_src[:, 0:4]), (WO[:, 4:8], wo_src[:, 4:8])], writes=[Tw])
            lng = self.sb(es, "lngM", [128, D], F32)
            lnb = self.sb(es, "lnbM", [128, D], F32)
            Tgb = T()
            P.dma("sp", [(lng[:], self.lnp[2 + 4 * l]), (lnb[:], self.lnp[3 + 4 * l])], writes=[Tgb])
            yaT = [self.sb(es, "yaT%d" % i, [128, 4, 512], BF16) for i in range(2)]
            ybT = [self.sb(es, "ybTm%d" % i, [128, 4, 512], BF16) for i in range(2)]
            gab = [self.sb(es, "gab%d" % i, [128, 16, 512], BF16) for i in range(2)]
            Tin = [T(), T()]
            mT = self.sb(es, "mT", [128, 8, 512], BF16)
            TmT = T()
            m1 = [self.sb(es, "m1_%d" % i, [128, 512], F32) for i in range(2)]
            m2 = [self.sb(es, "m2_%d" % i, [128, 512], F32) for i in range(2)]
            Tm1 = [T(), T()]
            Tm2 = [T(), T()]
            xt = [self.sb(es, "xtM%d" % i, [128, D], F32) for i in range(2)]
            Txt = [T(), T()]
            st = self.sb(es, "stM", [128, 2, 6], F32)
            mv = self.sb(es, "mvM", [128, 4], F32)
            Tst = T()
            pa = [self.ps(es, "pa%d" % i, [128, 512]) for i in range(2)]
            pb = [self.ps(es, "pbm%d" % i, [128, 512]) for i in range(2)]
            po = [self.ps(es, "po%d" % i, [128, 512]) for i in range(4)]
            Tpa, Tpb, Tpo = [T(), T()], [T(), T()], [T() for _ in range(4)]
            gi = 0
            ti = 0
            for s in range(self.NS):
                ya_src = self.ya[s].rearrange("(c p) t -> p c t", p=128)
                yb_src = self.yb[s].rearrange("(c p) t -> p c t", p=128)
                gt_src = self.gt[s].rearrange("(c p) t -> p c t", p=128)
                for g in range(self.NG):
                    ts = slice(g * 512, (g + 1) * 512)
                    bsel = gi % 2
                    gi += 1
                    P.dma("sp", [(yaT[bsel][:], ya_src[:, :, ts]), (ybT[bsel][:], yb_src[:, :, ts]), (gab[bsel][:], gt_src[:, :, ts])],
                          reads=[self.tdr("ya", s), self.tdr("yb", s), self.tdr("gt", s)], writes=[Tin[bsel]])
                    for j in range(8):
                        js = slice(j * 128, (j + 1) * 128)
                        a_, Ta_ = pa[j % 2], Tpa[j % 2]
                        b_, Tb_ = pb[j % 2], Tpb[j % 2]
                        for kc in range(4):
                            P.op("pe", mm(a_[:, :], WA[:, kc, js], yaT[bsel][:, kc, :], kc == 0, kc == 3), reads=[Tw, Tin[bsel]], writes=[Ta_])
                        for kc in range(4):
                            P.op("pe", mm(b_[:, :], WB[:, kc, js], ybT[bsel][:, kc, :], kc == 0, kc == 3), reads=[Tw, Tin[bsel]], writes=[Tb_])
                        P.op("dve", lambda e: e.tensor_tensor(out=m1[j % 2][:], in0=a_[:, :], in1=gab[bsel][:, j, :], op=ALU.mult), reads=[Ta_, Tin[bsel]], writes=[Tm1[j % 2]])
                        P.op("dve", lambda e: e.tensor_tensor(out=m2[j % 2][:], in0=b_[:, :], in1=gab[bsel][:, 8 + j, :], op=ALU.mult), reads=[Tb_, Tin[bsel]], writes=[Tm2[j % 2]])
                        P.op("pool", lambda e: e.tensor_tensor(out=mT[:, j, :], in0=m1[j % 2][:], in1=m2[j % 2][:], op=ALU.add), reads=[Tm1[j % 2], Tm2[j % 2]], writes=[TmT])
                    for t in range(4):
                        tok = slice(t * 128, (t + 1) * 128)
                        r0 = s * S + g * 512 + t * 128
                        x_, Tx_ = xt[ti % 2], Txt[ti % 2]
                        P.dma("sp", [(x_[:], self.xres[r0:r0 + 128, :])], reads=[self.tdr("xres", s, g * 4 + t)], writes=[Tx_])
                        for n in range(2):
                            o_, To_ = po[(2 * ti + n) % 4], Tpo[(2 * ti + n) % 4]
                            for kc in range(8):
                                P.op("pe", mm(o_[:, :], mT[:, kc, tok], WO[:, kc, n * 512:(n + 1) * 512], kc == 0, kc == 7), reads=[TmT, Tw], writes=[To_])
                            P.op("dve", lambda e: e.scalar_tensor_tensor(out=x_[:, n * 512:(n + 1) * 512], in0=x_[:, n * 512:(n + 1) * 512], scalar=ALPHA, in1=o_[:, :], op0=ALU.mult, op1=ALU.add),
                                 reads=[Tx_, To_], writes=[Tx_])
                        ti += 1
                        self.layer_norm(P, x_, Tx_, lng[:], lnb[:], Tgb, st, mv, Tst, x_[:], Tx_)
                        P.dma("sp", [(self.xres[r0:r0 + 128, :], x_[:])], reads=[Tx_], writes=[self.tdr("xres", s, g * 4 + t)])

    def stage_F(self, l, last):
        P, nc, S = self.P, self.nc, self.S
        P.barrier()
        G = 256
        NFC = DFF // 128
        with ExitStack() as es:
            W1 = self.sb(es, "W1", [128, 8, 2 * DFF], BF16)
            W2 = self.sb(es, "W2", [128, NFC, D], BF16)
            Tw1 = [T() for _ in range(8)]
            Tw2 = T()
            w1s = self.w_f1[l].rearrange("(kc p) c -> p kc c", p=128)
            for kc in range(8):
                P.dma("pool", [(W1[:, kc, :], w1s[:, kc, :])], writes=[Tw1[kc]])
            w2s = self.w_f2[l].rearrange("(fc p) c -> p fc c", p=128)
            for a in range(0, NFC, 6):
                bnd = min(a + 6, NFC)
                P.dma("pool", [(W2[:, a:bnd, :], w2s[:, a:bnd, :])], writes=[Tw2])
            lng = self.sb(es, "lngF", [128, D], F32)
            lnb = self.sb(es, "lnbF", [128, D], F32)
            Tgb = T()
            P.dma("sp", [(lng[:], self.lnp[4 + 4 * l]), (lnb[:], self.lnp[5 + 4 * l])], writes=[Tgb])
            xt = [self.sb(es, "xtF%d" % i, [128, D], F32) for i in range(2)]
            Txt = [T(), T()]
            x1T = self.sb(es, "x1T", [128, 8, G], BF16)
            Tx1T = T()
            hT = self.sb(es, "hT", [128, NFC, G], BF16)
            ThT = T()
            sgl = [self.sb(es, "sgl%d" % i, [128, G], F32) for i in range(2)]
            Tsg = [T(), T()]
            st = self.sb(es, "stF", [128, 2, 6], F32)
            mv = self.sb(es, "mvF", [128, 4], F32)
            Tst = T()
            pt = [self.ps(es, "pt%d" % i, [128, 512]) for i in range(2)]
            pg = [self.ps(es, "pg%d" % i, [128, 512]) for i in range(2)]
            pu = [self.ps(es, "pu%d" % i, [128, 512]) for i in range(2)]
            po = [self.ps(es, "pof%d" % i, [128, 512]) for i in range(2)]
            Tpt, Tpg, Tpu, Tpo = [T(), T()], [T(), T()], [T(), T()], [T(), T()]
            dst = self.y if last else self.xres
            for s in range(self.NS):
                for g in range(S // G):
                    for t in range(G // 128):
                        r0 = s * S + g * G + t * 128
                        P.dma("sp", [(xt[t][:], self.xres[r0:r0 + 128, :])], reads=[self.tdr("xres", s, g * 2 + t)], writes=[Txt[t]])
                        for half in range(2):
                            for q in range(4):
                                kc = half * 4 + q
                                P.op("pe", lambda e: e.transpose(pt[half][:, q * 128:(q + 1) * 128], xt[t][:, kc * 128:(kc + 1) * 128], self.identf),
                                     reads=[Txt[t], self.Tc], writes=[Tpt[half]])
                            o_ap = x1T[:, half * 4:half * 4 + 4, t * 128:(t + 1) * 128]
                            i_ap = pt[half][:].rearrange("p (a b) -> p a b", a=4)
                            if half == 0:
                                P.op("act", lambda e: e.activation(out=o_ap, in_=i_ap, func=AF.Copy), reads=[Tpt[half]], writes=[Tx1T])
                            else:
                                P.op("dve", lambda e: e.tensor_copy(out=o_ap, in_=i_ap), reads=[Tpt[half]], writes=[Tx1T])
                    for f in range(NFC):
                        g_, Tg_ = pg[f % 2], Tpg[f % 2]
                        u_, Tu_ = pu[f % 2], Tpu[f % 2]
                        for kc in range(8):
                            P.op("pe", mm(g_[:, 0:G], W1[:, kc, f * 128:(f + 1) * 128], x1T[:, kc, :], kc == 0, kc == 7), reads=[Tw1[kc], Tx1T], writes=[Tg_])
                        for kc in range(8):
                            P.op("pe", mm(u_[:, 0:G], W1[:, kc, DFF + f * 128:DFF + (f + 1) * 128], x1T[:, kc, :], kc == 0, kc == 7), reads=[Tw1[kc], Tx1T], writes=[Tu_])
                        P.op("act", lambda e: e.activation(out=sgl[f % 2][:], in_=g_[:, 0:G], func=AF.Silu), reads=[Tg_], writes=[Tsg[f % 2]])
                        P.op("dve", lambda e: e.tensor_tensor(out=hT[:, f, :], in0=u_[:, 0:G], in1=sgl[f % 2][:], op=ALU.mult), reads=[Tu_, Tsg[f % 2]], writes=[ThT])
                    for t in range(G // 128):
                        r0 = s * S + g * G + t * 128
                        tok = slice(t * 128, (t + 1) * 128)
                        for n in range(2):
                            o_, To_ = po[n], Tpo[n]
                            for fc in range(NFC):
                                P.op("pe", mm(o_[:, :], hT[:, fc, tok], W2[:, fc, n * 512:(n + 1) * 512], fc == 0, fc == NFC - 1), reads=[ThT, Tw2], writes=[To_])
                            P.op("dve", lambda e: e.scalar_tensor_tensor(out=xt[t][:, n * 512:(n + 1) * 512], in0=xt[t][:, n * 512:(n + 1) * 512], scalar=ALPHA, in1=o_[:, :], op0=ALU.mult, op1=ALU.add),
                                 reads=[Txt[t], To_], writes=[Txt[t]])
                        self.layer_norm(P, xt[t], Txt[t], lng[:], lnb[:], Tgb, st, mv, Tst, xt[t][:], Txt[t])
                        P.dma("sp", [(dst[r0:r0 + 128, :], xt[t][:])], reads=[Txt[t]], writes=[self.tdr("xres" if not last else "y", s, g * 2 + t)])


PV_MU0, PV_MU1 = 0, 14
PV_W0 = 28
PV_A0 = 36
PV_KK = 44
PV_KA = 48
PV_RK = 52
PV_GG = 56
PV_GB = 60
NPV = 64
C_ID = 0
C_ONES = 128
C_EPS = 256
C_MASK = 260
NCST = C_MASK + 4 * 128
CB_ID, CB_ID2, CB_ONES, CB_SMASK = 0, 128, 384, 448
NCB = 448


def na_plan(R):
    npair = min(8, R // 2)
    plan, variants = [], []
    kh = min(8, R)
    for b in range(R // 8):
        i0 = 8 * b
        mid = (i0 - 4 >= 0) and (i0 + 11 <= R) and kh == 8
        p0 = min(max(4 * b - 2, 0), R // 2 - npair)
        off = 2 * p0 - i0
        var = None
        if not mid:
            m = np.full((npair, 2, 8), NEG, np.float32)
            for j in range(npair):
                for r2 in range(2):
                    kr = 2 * (p0 + j) + r2
                    for qr in range(8):
                        i = i0 + qr
                        rs = min(max(i - kh // 2, 0), R - kh)
                        if rs <= kr < rs + kh:
                            m[j, r2, qr] = 0.0
            key = m.tobytes()
            for vi, (k2, _) in enumerate(variants):
                if k2 == key:
                    var = vi
                    break
            else:
                variants.append((key, m))
                var = len(variants) - 1
        plan.append((p0, npair, off, var))
    return plan, [m for (_, m) in variants]


def host_consts(R):
    cst = np.zeros((128, NCST), np.float32)
    cst[:, C_ID:C_ID + 128] = np.eye(128, dtype=np.float32)
    od = np.zeros((128, 128), np.float32)
    od[0:64, 0:64] = 1.0
    od[64:128, 64:128] = 1.0
    cst[:, C_ONES:C_ONES + 128] = od
    cst[:, C_EPS] = LN_EPS
    cst[:, C_EPS + 1] = GN_EPS
    cst[:, C_EPS + 2] = 1e-30
    i = np.arange(128)
    cst[:, C_MASK + 0:C_MASK + 128] = (i[:, None] > i[None, :])
    cst[:, C_MASK + 128:C_MASK + 256] = (i[:, None] < i[None, :])
    cst[:, C_MASK + 256:C_MASK + 384] = (i[:, None] >= i[None, :])
    cst[:, C_MASK + 384:C_MASK + 512] = (i[:, None] <= i[None, :])
    plan, variants = na_plan(R)
    rm = np.zeros((max(len(variants), 1), 128, 8, 512), np.float32)
    for vi, m in enumerate(variants):
        for j in range(m.shape[0]):
            for r2 in range(2):
                rm[vi, r2 * 64:(r2 + 1) * 64, j, :] = np.repeat(m[j, r2], 64)[None, :]
    return cst, rm


def host_layer_params(inp, L):
    f = lambda a: np.asarray(a, np.float32)
    pvec = np.zeros((L, 128, NPV), np.float32)
    lora = np.zeros((L, 128, 4, CW), np.float32)
    btab = np.full((L, 8, 2, 64, NE + 1, 64), NEG, np.float32)
    lnp = np.zeros((2 + 4 * L, 128, D), np.float32)
    lnp[0] = f(inp["ln_in_g"])[None, :]
    lnp[1] = f(inp["ln_in_b"])[None, :]
    c = np.arange(64)
    qc = np.arange(64)
    cs = np.clip(qc - 8, 0, GRID_W - 16)
    colvalid = (c[:, None] >= cs[None, :]) & (c[:, None] < cs[None, :] + 16)
    dc = c[:, None] - qc[None, :] + 15
    dcc = np.clip(dc, 0, 30)
    for l in range(L):
        mu = f(inp["shift_mu"][l])
        pad = np.zeros((2, 14 * 128), np.float32)
        pad[:, :RWC] = mu
        pvec[l, :, PV_MU0:PV_MU0 + 14] = pad[0].reshape(14, 128).T
        pvec[l, :, PV_MU1:PV_MU1 + 14] = pad[1].reshape(14, 128).T
        for d in range(2):
            pvec[l, :, PV_W0 + 4 * d:PV_W0 + 4 * d + 4] = f(inp["decay_w0"][l, d]).reshape(4, 128).T
            pvec[l, :, PV_A0 + 4 * d:PV_A0 + 4 * d + 4] = f(inp["iclr_a0"][l, d]).reshape(4, 128).T
            lora[l, 32 * d:32 * d + 32, d, :] = f(inp["decay_up"][l, d])
            lora[l, 64 + 32 * d:64 + 32 * d + 32, 2 + d, :] = f(inp["iclr_up"][l, d])
        pvec[l, :, PV_KK:PV_KK + 4] = f(inp["k_k"][l]).reshape(4, 128).T
        pvec[l, :, PV_KA:PV_KA + 4] = f(inp["k_a"][l]).reshape(4, 128).T
        pvec[l, :, PV_RK:PV_RK + 4] = f(inp["r_k"][l]).reshape(4, 128).T
        pvec[l, :, PV_GG:PV_GG + 4] = f(inp["gn_g"][l]).reshape(4, 128).T
        pvec[l, :, PV_GB:PV_GB + 4] = f(inp["gn_b"][l]).reshape(4, 128).T
        lnp[2 + 4 * l + 0] = f(inp["ln1_g"][l])[None, :]
        lnp[2 + 4 * l + 1] = f(inp["ln1_b"][l])[None, :]
        lnp[2 + 4 * l + 2] = f(inp["ln2_g"][l])[None, :]
        lnp[2 + 4 * l + 3] = f(inp["ln2_b"][l])[None, :]
        rpb = f(inp["na_rpb"][l])
        for mp in range(NE + 1):
            delta = 15 - mp
            dr = delta + 7
            if 0 <= dr <= 14:
                vals = np.where(colvalid[None], rpb[:, dr][:, dcc], NEG)
                btab[l, :, 1, :, mp, :] = vals
                if 3 <= dr <= 10:
                    btab[l, :, 0, :, mp, :] = vals
    return pvec, lora, lnp, btab


def kernel(**inputs):
    L, NS, S, NCORE = 4, 2, 4096, 8
    b = Builder(L, NS, S)
    nc = b.build()
    pvec, lora, lnp, btab = host_layer_params(inputs, L)
    cst, rm = host_consts(S // GRID_W)
    x = np.asarray(inputs["x"], np.float32)
    f = lambda a: np.ascontiguousarray(np.asarray(a, np.float32))
    shared = {"w_in": f(inputs["w_in"]), "w_a": f(inputs["w_branch_rwkv"]), "w_b": f(inputs["w_branch_na"]),
              "w_out": f(inputs["w_out"]), "w_f1": f(inputs["w_ffn_in"]), "w_f2": f(inputs["w_ffn_out"]),
              "pvec": pvec, "lora": lora, "gup": f(inputs["gate_up"]), "lnp": lnp, "btab": btab, "cst": cst, "rmask": rm}
    in_maps = []
    for c in range(NCORE):
        m = dict(shared)
        m["x"] = np.ascontiguousarray(x[c * NS:(c + 1) * NS].reshape(NS * S, D))
        in_maps.append(m)
    res = run_bass_kernel_spmd(nc, in_maps, core_ids=list(range(NCORE)))
    out = np.stack([np.asarray(r["y"]).reshape(NS, S, D) for r in res.results], axis=0)
    return out.reshape(NCORE * NS, S, D).astype(np.float32)
```
